# Optimizing a Trainium2 kernel written in Bass

```python
import jax, jax.numpy as jnp
from jax import lax
import numpy as np

D_MODEL = 2048
BATCH = 4
SEQ = 2048
DEPTH = 4

HEAD_DIM = 128
SB_HEADS = D_MODEL // (2 * HEAD_DIM)
GDN_HEADS = D_MODEL // (2 * HEAD_DIM)
SB_WIDTH = SB_HEADS * HEAD_DIM
GDN_WIDTH = GDN_HEADS * HEAD_DIM
D_MIX = SB_WIDTH + GDN_WIDTH
N_IN = 3 * SB_WIDTH + 4 * GDN_WIDTH + 2 * GDN_HEADS
SB_BLOCK = 128
SHORT_CONV = 4
CHUNK = 64
N_MEM = 256
X_HEADS = 4
X_HEAD_DIM = 128
X_WIDTH = X_HEADS * X_HEAD_DIM
D_FF = (11 * D_MODEL) // 4
FFN_CONV = 3
EPS = 1e-6

kernel_name = "hybrid_stickbreak_gdn_memxattn_convffn"


def rmsnorm(x, g):
    xf = x.astype(jnp.float32)
    y = xf * lax.rsqrt(jnp.mean(xf * xf, axis=-1, keepdims=True) + EPS)
    return (y * g.astype(jnp.float32)).astype(x.dtype)


def l2norm(x):
    return x * lax.rsqrt(jnp.sum(x * x, axis=-1, keepdims=True) + EPS)


def causal_dwconv(x, w):
    k = w.shape[0]
    return lax.conv_general_dilated(
        x, w[:, None, :].astype(x.dtype), window_strides=(1,), padding=[(k - 1, 0)],
        dimension_numbers=('NWC', 'WIO', 'NWC'), feature_group_count=x.shape[-1])


def stick_breaking_attention(q, k, v):
    b, s, h, dh = q.shape
    scale = dh ** -0.5
    n_blk = s // SB_BLOCK
    qb = jnp.moveaxis(q.reshape(b, n_blk, SB_BLOCK, h, dh), 1, 0)
    key_pos = jnp.arange(s)

    def block(args):
        q_blk, blk_idx = args
        z = jnp.einsum('bqhd,bshd->bhqs', q_blk, k) * scale
        q_pos = blk_idx * SB_BLOCK + jnp.arange(SB_BLOCK)
        valid = key_pos[None, :] < q_pos[:, None]
        log_beta = jax.nn.log_sigmoid(z)
        log_stay = jnp.where(valid, jax.nn.log_sigmoid(-z), 0.0)
        later = lax.cumsum(log_stay, axis=3, reverse=True) - log_stay
        weights = jnp.where(valid, jnp.exp(log_beta + later), 0.0)
        return jnp.einsum('bhqs,bshd->bqhd', weights, v)

    out = lax.map(block, (qb, jnp.arange(n_blk)))
    return jnp.moveaxis(out, 0, 1).reshape(b, s, h, dh)


def gated_delta_rule(q, k, v, beta, g):
    b, s, h, dk = q.shape
    dv = v.shape[-1]
    n = s // CHUNK

    def chunks(t):
        t = jnp.swapaxes(t, 1, 2)
        return t.reshape(b, h, n, CHUNK, *t.shape[3:])

    q, k, v, beta, g = chunks(q), chunks(k), chunks(v), chunks(beta), chunks(g)
    g_cum = jnp.cumsum(g, axis=-1)
    idx = jnp.arange(CHUNK)
    lower_incl = idx[:, None] >= idx[None, :]
    strict = idx[:, None] > idx[None, :]
    decay = jnp.exp(jnp.where(lower_incl, g_cum[..., :, None] - g_cum[..., None, :], -jnp.inf))
    k_beta = k * beta[..., None]
    v_beta = v * beta[..., None]
    lmat = jnp.where(strict, jnp.einsum('bhnid,bhnjd->bhnij', k_beta, k) * decay, 0.0)
    eye = jnp.eye(CHUNK, dtype=jnp.float32)
    t_inv = lax.linalg.triangular_solve(eye + lmat, jnp.broadcast_to(eye, lmat.shape),
                                        left_side=True, lower=True, unit_diagonal=True)
    u = t_inv @ v_beta
    w = t_inv @ (k_beta * jnp.exp(g_cum)[..., None])
    attn_intra = jnp.where(lower_incl, jnp.einsum('bhnid,bhnjd->bhnij', q, k) * decay, 0.0)
    q_decay = q * jnp.exp(g_cum)[..., None]
    k_tail = k * jnp.exp(g_cum[..., -1:] - g_cum)[..., None]
    chunk_decay = jnp.exp(g_cum[..., -1])

    def step(state, xs):
        u_c, w_c, qd_c, a_c, kt_c, cd_c = xs
        v_new = u_c - w_c @ state
        o = qd_c @ state + a_c @ v_new
        state = state * cd_c[..., None, None] + jnp.swapaxes(kt_c, -1, -2) @ v_new
        return state, o

    xs = tuple(jnp.moveaxis(t, 2, 0) for t in (u, w, q_decay, attn_intra, k_tail, chunk_decay))
    state0 = jnp.zeros((b, h, dk, dv), jnp.float32)
    _, o = lax.scan(step, state0, xs)
    o = jnp.moveaxis(o, 0, 2).reshape(b, h, s, dv)
    return jnp.swapaxes(o, 1, 2)


def gated_deltanet(q, k, v, z, b_logit, a_logit, conv_w, a_log, dt_bias, norm_g):
    bsz, s, _ = q.shape
    qkv = jnp.concatenate([q, k, v], axis=-1).astype(jnp.float32)
    qkv = jax.nn.silu(causal_dwconv(qkv, conv_w.astype(jnp.float32)))
    q, k, v = jnp.split(qkv, 3, axis=-1)
    heads = lambda t: t.reshape(bsz, s, GDN_HEADS, HEAD_DIM)
    q = l2norm(heads(q)) * (HEAD_DIM ** -0.5)
    k = l2norm(heads(k))
    v = heads(v)
    beta = jax.nn.sigmoid(b_logit.astype(jnp.float32))
    g = -jnp.exp(a_log.astype(jnp.float32)) * jax.nn.softplus(
        a_logit.astype(jnp.float32) + dt_bias.astype(jnp.float32))
    o = gated_delta_rule(q, k, v, beta, g)
    o = rmsnorm(o, norm_g) * jax.nn.silu(heads(z.astype(jnp.float32)))
    return o.reshape(bsz, s, GDN_WIDTH)


def memory_cross_attention(h, mem_n, w_q, w_kv, w_o):
    bsz, s, _ = h.shape
    m = mem_n.shape[1]
    q = (h @ w_q).reshape(bsz, s, X_HEADS, X_HEAD_DIM)
    k, v = jnp.split(mem_n @ w_kv, 2, axis=-1)
    k = k.reshape(bsz, m, X_HEADS, X_HEAD_DIM)
    v = v.reshape(bsz, m, X_HEADS, X_HEAD_DIM)
    logits = jnp.einsum('bshd,bmhd->bhsm', q.astype(jnp.float32), k.astype(jnp.float32)) * (X_HEAD_DIM ** -0.5)
    p = jax.nn.softmax(logits, axis=-1)
    o = jnp.einsum('bhsm,bmhd->bshd', p, v.astype(jnp.float32))
    return o.reshape(bsz, s, X_WIDTH).astype(h.dtype) @ w_o


def conv_ffn(h, w_up, conv_w, conv_b, w_down):
    u = causal_dwconv(h @ w_up, conv_w) + conv_b
    gate, up = jnp.split(u, 2, axis=-1)
    return (jax.nn.silu(gate) * up) @ w_down


def setup_inputs(seed: int = 0) -> dict:
    key = jax.random.key(seed)
    ks = iter(jax.random.split(key, 32))
    f32 = jnp.float32
    nrm = lambda shape, scale: jax.random.normal(next(ks), shape, f32) * scale
    gain = lambda shape: 1.0 + 0.02 * jax.random.normal(next(ks), shape, f32)
    dt = jnp.exp(jax.random.uniform(next(ks), (DEPTH, GDN_HEADS), f32, np.log(1e-3), np.log(1e-1)))
    return {
        "x": jax.random.normal(next(ks), (BATCH, SEQ, D_MODEL), f32),
        "mem": jax.random.normal(next(ks), (BATCH, N_MEM, D_MODEL), f32),
        "mix_norm": gain((DEPTH, D_MODEL)),
        "w_in": nrm((DEPTH, D_MODEL, N_IN), D_MODEL ** -0.5),
        "gdn_conv": nrm((DEPTH, SHORT_CONV, 3 * GDN_WIDTH), SHORT_CONV ** -0.5),
        "gdn_a_log": jnp.log(jax.random.uniform(next(ks), (DEPTH, GDN_HEADS), f32, 1.0, 16.0)),
        "gdn_dt_bias": dt + jnp.log(-jnp.expm1(-dt)),
        "gdn_norm": gain((DEPTH, HEAD_DIM)),
        "w_out": nrm((DEPTH, D_MIX, D_MODEL), D_MIX ** -0.5),
        "xattn_norm": gain((DEPTH, D_MODEL)),
        "mem_norm": gain((DEPTH, D_MODEL)),
        "w_xq": nrm((DEPTH, D_MODEL, X_WIDTH), D_MODEL ** -0.5),
        "w_xkv": nrm((DEPTH, D_MODEL, 2 * X_WIDTH), D_MODEL ** -0.5),
        "w_xo": nrm((DEPTH, X_WIDTH, D_MODEL), X_WIDTH ** -0.5),
        "ffn_norm": gain((DEPTH, D_MODEL)),
        "w_up": nrm((DEPTH, D_MODEL, 2 * D_FF), D_MODEL ** -0.5),
        "ffn_conv": nrm((DEPTH, FFN_CONV, 2 * D_FF), FFN_CONV ** -0.5),
        "ffn_conv_bias": nrm((DEPTH, 2 * D_FF), 0.02),
        "w_down": nrm((DEPTH, D_FF, D_MODEL), D_FF ** -0.5),
        "final_norm": gain((D_MODEL,)),
    }


def reference(x, mem, mix_norm, w_in, gdn_conv, gdn_a_log, gdn_dt_bias, gdn_norm, w_out,
              xattn_norm, mem_norm, w_xq, w_xkv, w_xo, ffn_norm, w_up, ffn_conv,
              ffn_conv_bias, w_down, final_norm):
    bsz, s, _ = x.shape
    cuts = [SB_WIDTH, 2 * SB_WIDTH, 3 * SB_WIDTH,
            3 * SB_WIDTH + GDN_WIDTH, 3 * SB_WIDTH + 2 * GDN_WIDTH,
            3 * SB_WIDTH + 3 * GDN_WIDTH, 3 * SB_WIDTH + 4 * GDN_WIDTH,
            3 * SB_WIDTH + 4 * GDN_WIDTH + GDN_HEADS]
    sb_heads = lambda t: t.astype(jnp.float32).reshape(bsz, s, SB_HEADS, HEAD_DIM)
    for l in range(DEPTH):
        h = rmsnorm(x, mix_norm[l])
        proj = h @ w_in[l]
        sq, sk, sv, gq, gk, gv, gz, gb, ga = jnp.split(proj, cuts, axis=-1)
        sb_out = stick_breaking_attention(sb_heads(sq), sb_heads(sk), sb_heads(sv))
        sb_out = sb_out.reshape(bsz, s, SB_WIDTH)
        gdn_out = gated_deltanet(gq, gk, gv, gz, gb, ga, gdn_conv[l], gdn_a_log[l],
                                 gdn_dt_bias[l], gdn_norm[l])
        mixed = jnp.concatenate([sb_out, gdn_out], axis=-1).astype(x.dtype)
        x = x + mixed @ w_out[l]
        mem_n = rmsnorm(mem, mem_norm[l])
        x = x + memory_cross_attention(rmsnorm(x, xattn_norm[l]), mem_n, w_xq[l], w_xkv[l], w_xo[l])
        x = x + conv_ffn(rmsnorm(x, ffn_norm[l]), w_up[l], ffn_conv[l], ffn_conv_bias[l], w_down[l])
    return rmsnorm(x, final_norm)
```

```python
import numpy as np
from contextlib import ExitStack
import concourse.bass as bass
import concourse.mybir as mybir
from concourse.bass_utils import run_bass_kernel_spmd

F32 = mybir.dt.float32
BF16 = mybir.dt.bfloat16
AF = mybir.ActivationFunctionType
ALU = mybir.AluOpType

S = 2048
D = 2048
KC = 16
NMEM = 256
DFF = 5632
NFB = 88
EPS = 1e-6
ENGS = ("pe", "act", "dve", "pool", "sp")


class Sched:
    def __init__(self, nc):
        self.nc = nc
        self.ops = {e: [] for e in ENGS}
        self.state = {}
        self.subs = {}
        self.seen = {e: {} for e in ENGS}
        self.dma_count = {}

    @staticmethod
    def _norm(k):
        if isinstance(k, tuple):
            return (k[0], k[1:] if len(k) > 1 else None)
        return (k, None)

    def _conf(self, base, sub):
        if sub is None:
            return [(base, None)] + [(base, s) for s in self.subs.get(base, ())]
        return [(base, sub), (base, None)]

    def rec(self, eng, fn, reads=(), writes=(), dma_sem=None):
        deps = {}
        if eng != "pe":
            pr = [k for k in reads if (k[0] if isinstance(k, tuple) else k) in ("PAB", "PC", "PD", "PT")]
            if pr:
                reads = [k for k in reads if k not in pr]
                writes = list(writes) + pr

        def add(pid, raw=False):
            if pid is None:
                return
            stream, idx = pid
            if stream == eng and (not raw or eng in ("pe", "sp")):
                return
            if deps.get(stream, -1) < idx:
                deps[stream] = idx

        rk = [self._norm(k) for k in reads]
        wk = [self._norm(k) for k in writes]
        for base, sub in rk:
            for ck in self._conf(base, sub):
                st = self.state.get(ck)
                if st is not None:
                    add(st[0], True)
        for base, sub in wk:
            for ck in self._conf(base, sub):
                st = self.state.get(ck)
                if st is not None:
                    add(st[0])
                    for s_, i_ in st[1].items():
                        add((s_, i_))
        waits = []
        seen = self.seen[eng]
        for stream, idx in deps.items():
            if isinstance(stream, tuple):
                cnt = self.dma_count[stream[1]]
                if seen.get(stream, -1) >= cnt:
                    continue
                seen[stream] = cnt
                waits.append((stream, cnt))
            else:
                if seen.get(stream, -1) >= idx:
                    continue
                seen[stream] = idx
                self.ops[stream][idx]["inc"] = True
                waits.append((stream, idx))
        if dma_sem is not None:
            c = self.dma_count.get(dma_sem, 0) + 1
            self.dma_count[dma_sem] = c
            pid = (("d", dma_sem), c)
        else:
            pid = (eng, len(self.ops[eng]))
        self.ops[eng].append({"fn": fn, "waits": waits, "inc": False, "dma": dma_sem})
        for base, sub in rk:
            st = self.state.setdefault((base, sub), [None, {}])
            if sub is not None:
                self.subs.setdefault(base, set()).add(sub)
            if st[1].get(pid[0], -1) < pid[1]:
                st[1][pid[0]] = pid[1]
        for base, sub in wk:
            if sub is None:
                for s in self.subs.get(base, ()):
                    self.state.pop((base, s), None)
                self.subs[base] = set()
            else:
                self.subs.setdefault(base, set()).add(sub)
            self.state[(base, sub)] = [pid, {}]

    def emit(self, es):
        nc = self.nc
        sems = {}
        for e in ("pe", "act", "dve", "pool"):
            sems[e] = es.enter_context(nc.semaphore("s_" + e))
        for name in self.dma_count:
            sems[("d", name)] = es.enter_context(nc.semaphore("d_" + name))
        cum = {}
        for e in ENGS:
            c = 0
            arr = []
            for op in self.ops[e]:
                if op["inc"]:
                    c += 1
                arr.append(c)
            cum[e] = arr
        block = es.enter_context(nc.Block())
        ops = self.ops

        def run(e, engh):
            for op in ops[e]:
                for stream, v in op["waits"]:
                    if isinstance(stream, tuple):
                        engh.wait_ge(sems[stream], 16 * v)
                    else:
                        engh.wait_ge(sems[stream], cum[stream][v])
                ins = op["fn"](engh)
                if ins is None:
                    continue
                if op["dma"] is not None:
                    ins.then_inc(sems[("d", op["dma"])], 16)
                elif op["inc"]:
                    ins.then_inc(sems[e], 1)

        @block.tensor
        def _(t):
            run("pe", t)

        @block.scalar
        def _(t):
            run("act", t)

        @block.vector
        def _(t):
            run("dve", t)

        @block.gpsimd
        def _(t):
            run("pool", t)

        @block.sync
        def _(t):
            run("sp", t)


class Gen:
    def __init__(self, nc, es):
        self.nc = nc
        self.es = es
        self.sc = Sched(nc)
        slab = nc.alloc_sbuf_tensor("slab", [128, 207000], mybir.dt.uint8)
        self.base = nc.lookup_mloc(slab).addr
        self.nps = 0

    def at(self, name, shape, dt, off):
        return self.nc.alloc_sbuf_tensor_at(name, shape, dt, offset=self.base + off)

    def mm(self, out, lhsT, rhs, start=True, stop=True, r=(), w=()):
        self.sc.rec("pe", lambda e: e.matmul(out, lhsT, rhs, start=start, stop=stop), reads=r, writes=w)

    def tr(self, out, in_, ident, r=(), w=()):
        self.sc.rec("pe", lambda e: e.transpose(out, in_, ident), reads=r, writes=w)

    def op(self, eng, method, r=(), w=(), **kw):
        self.sc.rec(eng, lambda e: getattr(e, method)(**kw), reads=r, writes=w)

    def act(self, out, in_, func, r=(), w=(), **kw):
        self.sc.rec("act", lambda e: e.activation(out=out, in_=in_, func=func, **kw), reads=r, writes=w)

    def dma(self, eng, out, in_, sem, r=(), w=()):
        self.sc.rec(eng, lambda e: e.dma_start(out=out, in_=in_), reads=r, writes=w, dma_sem=sem)


class WStream:
    def __init__(self, g, tiles):
        self.g = g
        self.tiles = tiles
        self.plan = []
        self.loaded = 0
        self.used = 0

    def add(self, ap, nk=16):
        self.plan.append((ap, nk))

    def get(self, hold=1):
        n = len(self.tiles)
        while self.loaded < len(self.plan) and self.loaded < self.used + n - (hold - 1):
            ap, nk = self.plan[self.loaded]
            i = self.loaded % n
            self.g.dma("pool", self.tiles[i][:, 0:nk, :], ap, "w%d" % i, w=[("wb", i)])
            self.loaded += 1
        i = self.used % n
        self.used += 1
        return self.tiles[i], ("wb", i)


def build(nl, final, dbg=False, cfg=None):
    cfg = cfg or {}
    nc = bass.Bass("TRN2", target_bir_lowering=False)
    dr = lambda name, shape, dt, kind="ExternalInput": nc.dram_tensor(name, shape, dt, kind=kind).ap()
    xin = dr("xin", [KC, 128, S], F32)
    memT = dr("memT", [KC, 128, NMEM], F32)
    cst = dr("cst", [128, 7 * 128 + 4 * 512], F32)
    nrm = dr("nrm", [128, nl * 64 + 16], F32)
    w_in = dr("w_in", [nl, 56, 128, 16, 128], F32)
    w_ba = dr("w_ba", [nl, 128, 16, 16], F32)
    gconv = dr("gconv", [nl, 128, 24 * 4], F32)
    gsm = dr("gsm", [nl, 128, 17], F32)
    w_out = dr("w_out", [nl, 16, 128, 16, 128], F32)
    w_xq = dr("w_xq", [nl, 4, 128, 16, 128], F32)
    w_xkv = dr("w_xkv", [nl, 8, 128, 16, 128], F32)
    w_xo = dr("w_xo", [nl, 4, 128, 16, 128], F32)
    w_up = dr("w_up", [nl, NFB, 128, 16, 128], F32)
    fconv = dr("fconv", [nl, 128, NFB * 4], F32)
    w_dn = dr("w_dn", [nl, 16, 4, 128, 11, 128], F32)
    okind = "ExternalOutput"
    xw = dr("xw", [KC, 128, S], F32, kind=okind if not final else "Internal")
    yout = dr("yout", [KC, 128, S], F32, kind=okind) if final else None
    mixT = dr("mixT", [KC, 128, S], BF16, kind=okind if dbg else "Internal")
    dbg_keys = []

    def dump(name, ap, keys, shape, dt):
        if not dbg or not cfg.get("dump"):
            return
        if name not in cfg["dump"]:
            return
        t = dr("D_" + name, shape, dt, kind=okind)
        g.dma("sp", t, ap, "dbg", r=keys, w=["D_" + name])
        dbg_keys.append("D_" + name)

    NSB = cfg.get("sb", 8); NGD = cfg.get("gdn", 8); DO_OUT = cfg.get("out", True)
    DO_X = cfg.get("xattn", True); DO_F = cfg.get("ffn", True)
    es = ExitStack()
    with es:
        g = Gen(nc, es)
        sc = g.sc
        o = 0
        cst_t = g.at("cst_t", [128, 7 * 128 + 4 * 512], F32, o); o += (7 * 128 + 4 * 512) * 4
        ident = cst_t[:, 0:128]; Lincl = cst_t[:, 128:256]; Uincl = cst_t[:, 256:384]
        Ustr = cst_t[:, 384:512]; mSL = cst_t[:, 512:640]; mUI = cst_t[:, 640:768]; ones = cst_t[:, 768:896]
        maskd = [cst_t[:, 896 + 512 * d: 896 + 512 * (d + 1)] for d in range(4)]
        identb = g.at("identb", [128, 128], BF16, o); o += 256
        nrm_t = g.at("nrm_t", [128, nl * 64 + 16], F32, o); o += (nl * 64 + 16) * 4
        gconv_t = g.at("gconv_t", [128, 96], F32, o); o += 384
        gsm_t = g.at("gsm_t", [128, 17], F32, o); o += 68 + 28
        fconv_t = g.at("fconv_t", [128, NFB * 4], F32, o); o += NFB * 16
        halo = g.at("halo", [128, NFB, 2], F32, o); o += NFB * 8
        rstd = g.at("rstd", [128, 512], F32, o); o += 2048
        sq = g.at("sq", [128, 2, 512], F32, o); o += 4096
        NWB = 6
        wbt = g.at("wbt", [128, NWB, 16, 128], BF16, o); o += NWB * 4096
        wtiles = [wbt[:, i] for i in range(NWB)]
        xt_off = o
        xt = g.at("xt", [128, KC, 512], F32, o); o += 32768
        big_off = o
        hT = g.at("hT", [128, KC, S], BF16, o)
        ar = o + 65536
        o += 122880
        assert o <= 207000, o
        ws = WStream(g, wtiles)

        PA = es.enter_context(nc.psum_tensor("PA", [128, 1024], F32))
        PB = es.enter_context(nc.psum_tensor("PB", [128, 1024], F32))
        PC = es.enter_context(nc.psum_tensor("PC", [128, 1024], F32))
        PD = es.enter_context(nc.psum_tensor("PD", [128, 512], F32))
        PT = es.enter_context(nc.psum_tensor("PT", [128, 1024], BF16))

        g.dma("sp", cst_t[:], cst, "c", w=["cst"])
        g.dma("sp", nrm_t[:], nrm, "c", w=["nrm"])
        g.op("dve", "tensor_copy", r=["cst"], w=["identb"], out=identb[:], in_=ident)
        for kc in range(KC):
            g.dma("sp", xw[kc], xin[kc], "xcp", w=[("xw", kc)])

        def gcol(l, which, kc):
            c = l * 64 + which * 16 + kc
            return nrm_t[:, c:c + 1]

        XT_ALIAS = [("cg", 0), ("cg", 1), ("cu", 0), ("cu", 1), ("xr2", 0), ("xr2", 1), "gq", "gk", "gv", "gz"]

        def rmsnorm(src, srckey, gc, dst, T0, T1, dkey, extra_w=()):
            for t0 in range(T0, T1, 512):
                for kc in range(KC):
                    g.dma("sp", xt[:, kc, :], src[kc, :, t0:t0 + 512], "x", r=[(srckey, kc)], w=[("xt", kc)] + XT_ALIAS)
                    g.act(sq[:, kc % 2, :], xt[:, kc, :], AF.Square, r=[("xt", kc)], w=[("sq", kc % 2)])
                    g.mm(PD[:], ones, sq[:, kc % 2, :], start=(kc == 0), stop=(kc == KC - 1),
                         r=["cst", ("sq", kc % 2)], w=["PD"])
                g.act(rstd[:], PD[:], AF.Sqrt, r=["PD"], w=["rstd"], bias=EPS, scale=1.0 / D)
                g.op("dve", "reciprocal", r=["rstd"], w=["rstd"], out=rstd[:], in_=rstd[:])
                for kc in range(KC):
                    g.op("dve", "scalar_tensor_tensor", r=[("xt", kc), "nrm", "rstd"], w=[(dkey, kc, t0)],
                         out=dst[:, kc, t0 - T0:t0 - T0 + 512], in0=xt[:, kc, :], scalar=gc(kc), in1=rstd[:],
                         op0=ALU.mult, op1=ALU.mult)

        def proj(src, skey, T, evac):
            wt, wk = ws.get()
            for tt in range(T // 512):
                pa = (PA, PB)[tt % 2]
                half = (tt // 2) % 2
                pv = pa[:, half * 512:(half + 1) * 512]
                pk = ("PAB", tt % 2, half)
                for kc in range(KC):
                    g.mm(pv, wt[:, kc, :], src[:, kc, tt * 512:(tt + 1) * 512], start=(kc == 0), stop=(kc == KC - 1),
                         r=[wk, skey], w=[pk])
                evac(tt, pv, pk)

        def residual_linear(wplan_nk, nkc, src, skey, T0, T):
            raise NotImplementedError

        for l in range(nl):
            g.dma("sp", gconv_t[:], gconv[l], "c", w=["gconv"])
            g.dma("sp", gsm_t[:], gsm[l], "c", w=["gsm"])
            g.dma("sp", fconv_t[:], fconv[l], "c", w=["fconv"])
            for hd in range(NSB):
                for j in range(3):
                    ws.add(w_in[l, j * 8 + hd])
            for hd in range(NGD):
                for j in range(4):
                    ws.add(w_in[l, 24 + j * 8 + hd])
            for ob in range(16 if DO_OUT else 0):
                ws.add(w_out[l, ob])
            for j in range(8 if DO_X else 0):
                ws.add(w_xkv[l, j])
            for j in range(4 if DO_X else 0):
                ws.add(w_xq[l, j])
            for j in range(4 if DO_X else 0):
                ws.add(w_xo[l, j])
            for tt in range(2 if DO_F else 0):
                for fb in range(44):
                    ws.add(w_up[l, fb]); ws.add(w_up[l, 44 + fb])
                for ob in range(16):
                    for kg in range(4):
                        ws.add(w_dn[l, ob, kg], 11)

            rmsnorm(xw, "xw", lambda kc: gcol(l, 0, kc), hT, 0, S, "hT")

            a = ar
            wba_t = g.at("wba_t%d" % l, [128, 16, 16], BF16, a); a += 512
            bg = g.at("bg%d" % l, [128, 16, 16], F32, a); a += 1024
            beta_t = g.at("beta%d" % l, [128, 16, 8], F32, a); a += 512
            nbeta_t = g.at("nbeta%d" % l, [128, 16, 8], F32, a); a += 512
            g_t = g.at("g_t%d" % l, [128, 16, 8], F32, a); a += 512
            nA = g.at("nA%d" % l, [128, 8], F32, a); a += 32
            gates_end = a
            g.dma("pool", wba_t[:], w_ba[l], "wba", w=["wba", "actT"])
            for blk in range(16):
                for kc in range(KC):
                    g.mm(PD[:, blk * 16:(blk + 1) * 16], hT[:, kc, blk * 128:(blk + 1) * 128], wba_t[:, kc, :],
                         start=(kc == 0), stop=(kc == KC - 1), r=["hT", "wba"], w=["PD"])
            g.op("dve", "tensor_copy", r=["PD"], w=["bg"], out=bg[:].rearrange("p a b -> p (a b)"), in_=PD[:, 0:256])
            g.act(beta_t[:], bg[:, :, 0:8], AF.Exp, r=["bg"], w=["beta"], scale=-1.0)
            g.op("dve", "tensor_scalar_add", r=["beta"], w=["beta"], out=beta_t[:], in0=beta_t[:], scalar1=1.0)
            g.op("dve", "reciprocal", r=["beta"], w=["beta"], out=beta_t[:], in_=beta_t[:])
            g.op("dve", "tensor_scalar_mul", r=["beta"], w=["nbeta"], out=nbeta_t[:], in0=beta_t[:], scalar1=-1.0)
            g.act(nA[:], gsm_t[:, 0:8], AF.Exp, r=["gsm"], w=["nA"])
            g.op("dve", "tensor_scalar_mul", r=["nA"], w=["nA"], out=nA[:], in0=nA[:], scalar1=-1.0)
            for blk in range(16):
                g.op("dve", "tensor_tensor", r=["bg", "gsm"], w=["g_t"], out=g_t[:, blk, :], in0=bg[:, blk, 8:16],
                     in1=gsm_t[:, 8:16], op=ALU.add)
            g.act(g_t[:], g_t[:], AF.Exp, r=["g_t"], w=["g_t"])
            g.act(g_t[:], g_t[:], AF.Ln, r=["g_t"], w=["g_t"], bias=1.0)
            for blk in range(16):
                g.op("dve", "tensor_tensor", r=["g_t", "nA"], w=["g_t"], out=g_t[:, blk, :], in0=g_t[:, blk, :],
                     in1=nA[:], op=ALU.mult)

            a = gates_end
            a = (a + 63) // 64 * 64
            qT = g.at("qT%d" % l, [128, S], BF16, a); a += 4096
            kT = g.at("kT%d" % l, [128, S], BF16, a); a += 4096
            vT = g.at("vT%d" % l, [128, S], BF16, a); a += 4096
            vtok = g.at("vtok%d" % l, [128, 16, 128], BF16, a); a += 4096
            ebuf = g.at("ebuf%d" % l, [128, 2, 512], F32, a); a += 4096
            spbuf = g.at("spbuf%d" % l, [128, 2, 512], F32, a); a += 4096
            ecbuf = g.at("ecbuf%d" % l, [128, 2, 512], F32, a); a += 4096
            racc = g.at("racc%d" % l, [128, 512], F32, a); a += 2048
            wbuf = g.at("wbuf%d" % l, [128, 2, 512], BF16, a); a += 2048
            obuf = g.at("obuf%d" % l, [128, 2, 512], BF16, a); a += 2048
            assert a <= big_off + 122880
            scale = 128.0 ** -0.5
            for hd in range(NSB):
                for j, dst, key in ((0, qT, "qT"), (1, kT, "kT"), (2, vT, "vT")):
                    def ev(tt, pv, pk, dst=dst, key=key):
                        if tt % 2 == 0:
                            g.act(dst[:, tt * 512:(tt + 1) * 512], pv, AF.Copy, r=[pk], w=[(key, tt)])
                        else:
                            g.op("dve", "tensor_copy", r=[pk], w=[(key, tt)], out=dst[:, tt * 512:(tt + 1) * 512], in_=pv)
                    proj(hT, "hT", S, ev)
                for half in range(2):
                    for b in range(8):
                        blk = half * 8 + b
                        g.tr(PT[:, b * 128:(b + 1) * 128], vT[:, blk * 128:(blk + 1) * 128], identb[:],
                             r=[("vT", blk // 4), "identb"], w=["PT"])
                    g.op("dve", "tensor_copy", r=["PT"], w=[("vtok", half)],
                         out=vtok[:, half * 8:(half + 1) * 8, :].rearrange("p a b -> p (a b)"), in_=PT[:])
                for qt in range(4):
                    qsl = slice(qt * 512, (qt + 1) * 512)
                    nkb = 4 * qt + 4
                    for it, kb in enumerate(range(nkb - 1, -1, -1)):
                        i2 = it % 2
                        pz = PC[:, i2 * 512:(i2 + 1) * 512]; pzk = ("PC", i2)
                        pcv = (PA, PB)[i2][:, 0:512] if False else None
                        g.mm(pz, kT[:, kb * 128:(kb + 1) * 128], qT[:, qsl], r=[("kT", kb // 4), ("qT", qt)], w=[pzk])
                        e_ = ebuf[:, i2, :]; sp_ = spbuf[:, i2, :]; ec_ = ecbuf[:, i2, :]
                        g.act(e_, pz, AF.Exp, r=[pzk], w=[("e", i2)], scale=scale)
                        g.act(sp_, e_, AF.Ln, r=[("e", i2)], w=[("sp", i2)], bias=1.0)
                        diag = kb >= 4 * qt
                        if diag:
                            md = maskd[kb - 4 * qt]
                            g.op("dve", "tensor_tensor", r=[("sp", i2), "cst"], w=[("sp", i2)], out=sp_, in0=sp_, in1=md, op=ALU.mult)
                            g.op("dve", "tensor_tensor", r=[("e", i2), "cst"], w=[("e", i2)], out=e_, in0=e_, in1=md, op=ALU.mult)
                        pcs = (PA, PB)[i2][:, 512:1024]; pck = ("PAB", i2, 1)
                        g.mm(pcs, Lincl, sp_, start=True, stop=(it == 0), r=["cst", ("sp", i2)], w=[pck])
                        if it > 0:
                            g.mm(pcs, ones, racc[:], start=False, stop=True, r=["cst", "racc"], w=[pck])
                        g.act(ec_, pcs, AF.Exp, r=[pck], w=[("ec", i2)], scale=-1.0)
                        g.op("dve", "tensor_tensor", r=[("e", i2), ("ec", i2)], w=[("wbuf", i2)], out=wbuf[:, i2, :], in0=e_, in1=ec_, op=ALU.mult)
                        if it == 0:
                            g.op("dve", "tensor_copy", r=[("sp", i2)], w=["racc"], out=racc[:], in_=sp_)
                        elif kb > 0:
                            g.op("dve", "tensor_tensor", r=[("sp", i2), "racc"], w=["racc"], out=racc[:], in0=racc[:], in1=sp_, op=ALU.add)
                        g.mm(PD[:], vtok[:, kb, :], wbuf[:, i2, :], start=(it == 0), stop=(kb == 0),
                             r=[("vtok", kb // 8), ("wbuf", i2)], w=["PD"])
                    ob_ = obuf[:, qt % 2, :]
                    g.act(ob_, PD[:], AF.Copy, r=["PD"], w=[("obuf", qt % 2)])
                    g.dma("sp", mixT[hd, :, qsl], ob_, "mo", r=[("obuf", qt % 2)], w=[("mixT", hd)])

            a = gates_end
            a = (a + 63) // 64 * 64
            gq = xt[:].rearrange("p a b -> p (a b)")
            gqT = gq[:, 0:S]; gkT = gq[:, S:2 * S]; gvT = gq[:, 2 * S:3 * S]; gzT = gq[:, 3 * S:4 * S]
            cb = g.at("cb%d" % l, [128, S + 8], F32, a); a += (S + 8) * 4
            ogT = g.at("ogT%d" % l, [128, S], F32, a); a += S * 4
            wide = {}
            for nm in ("Gm", "DecL", "DecT", "egcb", "Qa", "Qb", "Pa", "Pb", "X", "vb", "kbg", "ktail", "u", "wT", "ATm", "qdT"):
                wide[nm] = g.at(nm + str(l), [128, 4, 128], F32, a); a += 2048
            Sst = g.at("Sst%d" % l, [128, 2, 128], F32, a); a += 1024
            vnew = g.at("vnew%d" % l, [128, 128], F32, a); a += 512
            ecols = g.at("ecols%d" % l, [128, 4, 4], F32, a); a += 64
            bcol2 = g.at("bcol2%d" % l, [128, 4], F32, a); a += 64
            assert a <= big_off + 122880, a
            Gm, DecL, DecT, egcb = wide["Gm"], wide["DecL"], wide["DecT"], wide["egcb"]
            X, vb, kbg, ktail, u_, wT_, ATm, qdT = (wide[n] for n in ("X", "vb", "kbg", "ktail", "u", "wT", "ATm", "qdT"))
            fl = lambda t: t[:].rearrange("p a b -> p (a b)")
            class _Stop(Exception):
                pass

            def chk(stage):
                if cfg.get("gstop") == stage:
                    raise _Stop()

            def gdn_head(hd):
                for j, dst, key in ((0, gqT, "gq"), (1, gkT, "gk"), (2, gvT, "gv"), (3, gzT, "gz")):
                    if j < 3:
                        g.op("dve", "memset", w=["cb"], ap=cb[:, 0:3], constant=0.0)
                        def ev(tt, pv, pk):
                            if tt % 2 == 0:
                                g.act(cb[:, 3 + tt * 512:3 + (tt + 1) * 512], pv, AF.Copy, r=[pk], w=["cb"])
                            else:
                                g.op("dve", "tensor_copy", r=[pk], w=["cb"], out=cb[:, 3 + tt * 512:3 + (tt + 1) * 512], in_=pv)
                        proj(hT, "hT", S, ev)
                        fb = j * 8 + hd
                        wc = lambda i: gconv_t[:, fb * 4 + i:fb * 4 + i + 1]
                        g.act(dst, cb[:, 3:3 + S], AF.Identity, r=["cb", "gconv"], w=[key], scale=wc(3))
                        for i in range(3):
                            g.op("dve", "scalar_tensor_tensor", r=["cb", "gconv", key], w=[key], out=dst, in0=cb[:, i:i + S],
                                 scalar=wc(i), in1=dst, op0=ALU.mult, op1=ALU.add)
                        g.act(dst, dst, AF.Silu, r=[key], w=[key])
                    else:
                        def ev(tt, pv, pk):
                            g.act(gzT[:, tt * 512:(tt + 1) * 512], pv, AF.Silu, r=[pk], w=["gz"])
                        proj(hT, "hT", S, ev)
                chk(1)
                for dst, key, sc_ in ((gqT, "gq", 128.0 ** -0.5), (gkT, "gk", 1.0)):
                    for tt in range(4):
                        tsl = slice(tt * 512, (tt + 1) * 512)
                        g.act(sq[:, tt % 2, :], dst[:, tsl], AF.Square, r=[key], w=[("sq", tt % 2)])
                        g.mm(PD[:], ones, sq[:, tt % 2, :], r=["cst", ("sq", tt % 2)], w=["PD"])
                        g.act(rstd[:], PD[:], AF.Sqrt, r=["PD"], w=["rstd"], bias=EPS, scale=1.0)
                        g.op("dve", "reciprocal", r=["rstd"], w=["rstd"], out=rstd[:], in_=rstd[:])
                        g.op("dve", "scalar_tensor_tensor", r=[key, "rstd"], w=[key], out=dst[:, tsl], in0=dst[:, tsl],
                             scalar=sc_, in1=rstd[:], op0=ALU.mult, op1=ALU.mult)
                chk(2)
                g.op("dve", "memset", w=[("Sst", 0)], ap=Sst[:, 0, :], constant=0.0)
                for grp in range(4):
                    blks = [grp * 4 + b for b in range(4)]
                    for b, n in enumerate(blks):
                        g.op("dve", "tensor_scalar", r=["cst", "g_t"], w=["Gm"], out=Gm[:, b, :], in0=Uincl,
                             scalar1=g_t[:, n, hd:hd + 1], scalar2=None, op0=ALU.mult)
                    for b, n in enumerate(blks):
                        g.mm(PA[:, b * 128:(b + 1) * 128], Gm[:, b, :], Ustr, r=["Gm", "cst"], w=[("PAB", 0, 0)])
                        g.mm(PA[:, 512 + b * 128:512 + (b + 1) * 128], Ustr, Gm[:, b, :], r=["Gm", "cst"], w=[("PAB", 0, 1)])
                        g.mm(PB[:, b * 128:(b + 1) * 128], ones, Gm[:, b, :], r=["Gm", "cst"], w=[("PAB", 1, 0)])
                        g.mm(PD[:, b * 4:b * 4 + 1], Gm[:, b, :], ones[:, 0:1], r=["Gm", "cst"], w=["PD"])
                        g.mm(PD[:, b * 4 + 1:b * 4 + 2], Ustr, g_t[:, n, hd:hd + 1], r=["g_t", "cst"], w=["PD"])
                        g.mm(PD[:, b * 4 + 2:b * 4 + 3], ones, g_t[:, n, hd:hd + 1], r=["g_t", "cst"], w=["PD"])
                    g.act(fl(DecL), PA[:, 0:512], AF.Exp, r=[("PAB", 0, 0)], w=["DecL"])
                    g.act(fl(DecT), PA[:, 512:1024], AF.Exp, r=[("PAB", 0, 1)], w=["DecT"])
                    g.act(fl(egcb), PB[:, 0:512], AF.Exp, r=[("PAB", 1, 0)], w=["egcb"])
                    g.act(ecols[:].rearrange("p a b -> p (a b)"), PD[:, 0:16], AF.Exp, r=["PD"], w=["ecols"])
                    for b, n in enumerate(blks):
                        g.op("dve", "tensor_tensor", r=["DecL", "cst"], w=["DecL"], out=DecL[:, b, :], in0=DecL[:, b, :], in1=mSL, op=ALU.mult)
                        g.op("dve", "tensor_tensor", r=["DecT", "cst"], w=["DecT"], out=DecT[:, b, :], in0=DecT[:, b, :], in1=mUI, op=ALU.mult)
                        g.op("dve", "tensor_tensor", r=["ecols", "beta"], w=["bcol2"], out=bcol2[:, b:b + 1], in0=ecols[:, b, 0:1],
                             in1=beta_t[:, n, hd:hd + 1], op=ALU.mult)
                    chk(3)
                    Q, P = wide["Qa"], wide["Pa"]
                    Q2, P2 = wide["Qb"], wide["Pb"]
                    for b, n in enumerate(blks):
                        bs = slice(n * 128, (n + 1) * 128)
                        g.mm(PC[:, b * 128:(b + 1) * 128], gkT[:, bs], gkT[:, bs], r=["gk"], w=[("PC", 0)])
                    chk(31)
                    for b, n in enumerate(blks):
                        g.op("dve", "scalar_tensor_tensor", r=[("PC", 0), "nbeta", "DecL"], w=["Qa"], out=Q[:, b, :],
                             in0=PC[:, b * 128:(b + 1) * 128], scalar=nbeta_t[:, n, hd:hd + 1], in1=DecL[:, b, :],
                             op0=ALU.mult, op1=ALU.mult)
                    chk(32)
                    for b in range(4):
                        g.mm(PC[:, 512 + b * 128:512 + (b + 1) * 128], Q[:, b, :], ident, r=["Qa", "cst"], w=[("PC", 1)])
                    chk(33)
                    g.act(fl(P), PC[:, 512:1024], AF.Copy, r=[("PC", 1)], w=["Pa"])
                    chk(34)
                    for b in range(4):
                        g.op("dve", "tensor_tensor", r=["Pa", "cst"], w=["X"], out=X[:, b, :],
                             in0=P[:, b, :], in1=ident, op=ALU.add)
                    chk(4)
                    qn, pn = "Qa", "Pa"
                    for step in range(6):
                        qn2 = "Qb" if qn == "Qa" else "Qa"
                        pn2 = "Pb" if pn == "Pa" else "Pa"
                        Q, P, Q2, P2 = wide[qn], wide[pn], wide[qn2], wide[pn2]
                        for b in range(4):
                            g.mm(PC[:, b * 128:(b + 1) * 128], P[:, b, :], Q[:, b, :], r=[qn, pn], w=[("PC", 0)])
                        g.act(fl(Q2), PC[:, 0:512], AF.Copy, r=[("PC", 0)], w=[qn2])
                        if step < 5:
                            for b in range(4):
                                g.mm(PC[:, 512 + b * 128:512 + (b + 1) * 128], Q[:, b, :], P[:, b, :], r=[qn, pn], w=[("PC", 1)])
                            g.op("dve", "tensor_copy", r=[("PC", 1)], w=[pn2], out=fl(P2), in_=PC[:, 512:1024])
                        for b in range(4):
                            g.mm(PB[:, 512 + b * 128:512 + (b + 1) * 128], Q2[:, b, :], X[:, b, :], r=[qn2, "X"], w=[("PAB", 1, 1)])
                        g.op("dve", "tensor_tensor", r=[("PAB", 1, 1), "X"], w=["X"], out=fl(X), in0=fl(X), in1=PB[:, 512:1024], op=ALU.add)
                        qn, pn = qn2, pn2
                    chk(5)
                    for b, n in enumerate(blks):
                        bs = slice(n * 128, (n + 1) * 128)
                        g.mm(PA[:, b * 128:(b + 1) * 128], gkT[:, bs], ident, r=["gk", "cst"], w=[("PAB", 0, 0)])
                        g.mm(PA[:, 512 + b * 128:512 + (b + 1) * 128], gvT[:, bs], ident, r=["gv", "cst"], w=[("PAB", 0, 1)])
                    for b, n in enumerate(blks):
                        g.op("dve", "tensor_scalar", r=[("PAB", 0, 0), "bcol2"], w=["kbg"], out=kbg[:, b, :], in0=PA[:, b * 128:(b + 1) * 128],
                             scalar1=bcol2[:, b:b + 1], scalar2=None, op0=ALU.mult)
                        g.act(ktail[:, b, :], PA[:, b * 128:(b + 1) * 128], AF.Identity, r=[("PAB", 0, 0), "ecols"], w=["ktail"],
                              scale=ecols[:, b, 1:2])
                        g.op("dve", "tensor_scalar", r=[("PAB", 0, 1), "beta"], w=["vb"], out=vb[:, b, :],
                             in0=PA[:, 512 + b * 128:512 + (b + 1) * 128], scalar1=beta_t[:, n, hd:hd + 1], scalar2=None, op0=ALU.mult)
                    for b, n in enumerate(blks):
                        bs = slice(n * 128, (n + 1) * 128)
                        g.mm(PA[:, b * 128:(b + 1) * 128], X[:, b, :], vb[:, b, :], r=["X", "vb"], w=[("PAB", 0, 0)])
                        g.mm(PA[:, 512 + b * 128:512 + (b + 1) * 128], kbg[:, b, :], X[:, b, :], r=["X", "kbg"], w=[("PAB", 0, 1)])
                        g.mm(PB[:, b * 128:(b + 1) * 128], gkT[:, bs], gqT[:, bs], r=["gk", "gq"], w=[("PAB", 1, 0)])
                    g.act(fl(u_), PA[:, 0:512], AF.Copy, r=[("PAB", 0, 0)], w=["u"])
                    g.act(fl(wT_), PA[:, 512:1024], AF.Copy, r=[("PAB", 0, 1)], w=["wT"])
                    g.op("dve", "tensor_tensor", r=[("PAB", 1, 0), "DecT"], w=["ATm"], out=fl(ATm), in0=fl(DecT), in1=PB[:, 0:512], op=ALU.mult)
                    g.op("dve", "tensor_tensor", r=["gq", "egcb"], w=["qdT"], out=fl(qdT), in0=gqT[:, grp * 512:(grp + 1) * 512], in1=fl(egcb), op=ALU.mult)
                    chk(6)
                    for b, n in enumerate(blks):
                        s0 = n % 2; s1 = 1 - s0
                        Sc = Sst[:, s0, :]; Sn = Sst[:, s1, :]
                        g.mm(PC[:, 0:128], wT_[:, b, :], Sc, r=["wT", ("Sst", s0)], w=[("PC", 0)])
                        g.op("dve", "tensor_tensor", r=["u", ("PC", 0)], w=["vnew"], out=vnew[:], in0=u_[:, b, :], in1=PC[:, 0:128], op=ALU.subtract)
                        g.mm(PC[:, 512:640], Sc, qdT[:, b, :], start=True, stop=False, r=[("Sst", s0), "qdT"], w=[("PC", 1)])
                        g.mm(PC[:, 512:640], vnew[:], ATm[:, b, :], start=False, stop=True, r=["vnew", "ATm"], w=[("PC", 1)])
                        g.mm(PB[:, 512:640], ktail[:, b, :], vnew[:], r=["ktail", "vnew"], w=[("PAB", 1, 1)])
                        g.op("dve", "scalar_tensor_tensor", r=[("Sst", s0), "ecols", ("PAB", 1, 1)], w=[("Sst", s1)], out=Sn, in0=Sc,
                             scalar=ecols[:, b, 2:3], in1=PB[:, 512:640], op0=ALU.mult, op1=ALU.add)
                        g.act(ogT[:, n * 128:(n + 1) * 128], PC[:, 512:640], AF.Copy, r=[("PC", 1)], w=["ogT"])
                chk(7)
                for tt in range(4):
                    tsl = slice(tt * 512, (tt + 1) * 512)
                    g.act(sq[:, tt % 2, :], ogT[:, tsl], AF.Square, r=["ogT"], w=[("sq", tt % 2)])
                    g.mm(PD[:], ones, sq[:, tt % 2, :], r=["cst", ("sq", tt % 2)], w=["PD"])
                    g.act(rstd[:], PD[:], AF.Sqrt, r=["PD"], w=["rstd"], bias=EPS, scale=1.0 / 128)
                    g.op("dve", "reciprocal", r=["rstd"], w=["rstd"], out=rstd[:], in_=rstd[:])
                    g.op("dve", "scalar_tensor_tensor", r=["ogT", "gsm", "rstd"], w=["ogT"], out=ogT[:, tsl], in0=ogT[:, tsl],
                         scalar=gsm_t[:, 16:17], in1=rstd[:], op0=ALU.mult, op1=ALU.mult)
                    ob_ = obuf[:, tt % 2, :]
                    g.op("dve", "tensor_tensor", r=["ogT", "gz"], w=[("obuf", tt % 2)], out=ob_, in0=ogT[:, tsl], in1=gzT[:, tsl], op=ALU.mult)
                    g.dma("sp", mixT[8 + hd, :, tsl], ob_, "mo", r=[("obuf", tt % 2)], w=[("mixT", 8 + hd)])

            for hd in range(NGD):
                try:
                    gdn_head(hd)
                except _Stop:
                    break

            def resid_block(nblk_k, src, skey, T0, T, xtmp, xkey, wkc_list):
                pass

            xr = g.at("xr%d" % l, [128, 2, 1024], F32, ar)
            for kc in range(KC if DO_OUT else 0):
                g.dma("sp", hT[:, kc, :], mixT[kc], "mi", r=[("mixT", kc)], w=["hT"])

            def lin_resid(src, skey, ncin, T0, T, getw):
                for ob in range(16):
                    wl = getw(ob)
                    for th in range(T // 1024):
                        t0 = T0 + th * 1024
                        xs = xr[:, (ob + th) % 2, :]; xk = ("xr", (ob + th) % 2)
                        g.dma("sp", xs, xw[ob, :, t0:t0 + 1024], "xr", r=[("xw", ob)], w=[xk])
                        pa = (PA, PB)[(ob + th) % 2]; pk0 = ("PAB", (ob + th) % 2, 0); pk1 = ("PAB", (ob + th) % 2, 1)
                        for hf in range(2):
                            for i, (wt, wk, ki, ci) in enumerate(wl):
                                g.mm(pa[:, hf * 512:(hf + 1) * 512], wt[:, ki, :], src[:, ci, th * 1024 + hf * 512: th * 1024 + (hf + 1) * 512],
                                     start=(i == 0), stop=(i == len(wl) - 1), r=[wk, skey], w=[(pk0, pk1)[hf]])
                        g.op("dve", "tensor_tensor", r=[pk0, pk1, xk], w=[xk], out=xs, in0=xs, in1=pa[:], op=ALU.add)
                        g.dma("sp", xw[ob, :, t0:t0 + 1024], xs, "xr", r=[xk], w=[("xw", ob)])

            def getw_out(ob):
                wt, wk = ws.get()
                return [(wt, wk, kc, kc) for kc in range(KC)]
            if DO_OUT:
                lin_resid(hT, "hT", 16, 0, S, getw_out)
            if not DO_X:
                continue

            a = ar + 8192
            memn = g.at("memn%d" % l, [128, KC, NMEM], BF16, a); a += 8192
            KT = g.at("KT%d" % l, [128, 4, NMEM], BF16, a); a += 2048
            Vt = g.at("Vt%d" % l, [128, 2, 512], BF16, a); a += 2048
            xq = g.at("xq%d" % l, [128, 4, S], BF16, a); a += 16384
            xo = g.at("xo%d" % l, [128, 4, S], BF16, a); a += 16384
            sc_t = g.at("sc_t%d" % l, [128, NMEM], F32, a); a += 1024
            pb_t = g.at("pb_t%d" % l, [128, NMEM], BF16, a); a += 512
            pT_t = g.at("pT_t%d" % l, [128, 2, 128], BF16, a); a += 512
            mx = g.at("mx%d" % l, [128, 4], F32, a); a += 64
            assert a <= big_off + 122880
            for kc in range(KC):
                g.dma("sp", xt[:, kc, 0:NMEM], memT[kc], "x", w=[("xt", kc)] + XT_ALIAS)
                g.act(sq[:, kc % 2, 0:NMEM], xt[:, kc, 0:NMEM], AF.Square, r=[("xt", kc)], w=[("sq", kc % 2)])
                g.mm(PD[:, 0:NMEM], ones, sq[:, kc % 2, 0:NMEM], start=(kc == 0), stop=(kc == KC - 1), r=["cst", ("sq", kc % 2)], w=["PD"])
            g.act(rstd[:, 0:NMEM], PD[:, 0:NMEM], AF.Sqrt, r=["PD"], w=["rstd"], bias=EPS, scale=1.0 / D)
            g.op("dve", "reciprocal", r=["rstd"], w=["rstd"], out=rstd[:, 0:NMEM], in_=rstd[:, 0:NMEM])
            for kc in range(KC):
                g.op("dve", "scalar_tensor_tensor", r=[("xt", kc), "nrm", "rstd"], w=["memn"], out=memn[:, kc, :], in0=xt[:, kc, 0:NMEM],
                     scalar=gcol(l, 2, kc), in1=rstd[:, 0:NMEM], op0=ALU.mult, op1=ALU.mult)
            for j in range(4):
                wt, wk = ws.get()
                for kc in range(KC):
                    g.mm(PC[:, 0:NMEM], wt[:, kc, :], memn[:, kc, :], start=(kc == 0), stop=(kc == KC - 1), r=[wk, "memn"], w=[("PC", 0)])
                g.act(KT[:, j, :], PC[:, 0:NMEM], AF.Copy, r=[("PC", 0)], w=["KT"])
            for j in range(4):
                wt, wk = ws.get()
                for mb in range(2):
                    for kc in range(KC):
                        g.mm(PC[:, 512 + mb * 128:512 + (mb + 1) * 128], memn[:, kc, mb * 128:(mb + 1) * 128], wt[:, kc, :],
                             start=(kc == 0), stop=(kc == KC - 1), r=[wk, "memn"], w=[("PC", 1)])
                for mb in range(2):
                    g.act(Vt[:, mb, j * 128:(j + 1) * 128], PC[:, 512 + mb * 128:512 + (mb + 1) * 128], AF.Copy, r=[("PC", 1)], w=["Vt"])
            dump("memn", memn[:].rearrange("p a b -> p (a b)"), ["memn"], [128, KC * NMEM], BF16)
            dump("KT", KT[:].rearrange("p a b -> p (a b)"), ["KT"], [128, 4 * NMEM], BF16)
            dump("Vt", Vt[:].rearrange("p a b -> p (a b)"), ["Vt"], [128, 1024], BF16)
            rmsnorm(xw, "xw", lambda kc: gcol(l, 1, kc), hT, 0, S, "hT")
            for j in range(4):
                def ev(tt, pv, pk, j=j):
                    if tt % 2 == 0:
                        g.act(xq[:, j, tt * 512:(tt + 1) * 512], pv, AF.Copy, r=[pk], w=["xq"])
                    else:
                        g.op("dve", "tensor_copy", r=[pk], w=["xq"], out=xq[:, j, tt * 512:(tt + 1) * 512], in_=pv)
                proj(hT, "hT", S, ev)
            for tb in range(16):
                tbs = slice(tb * 128, (tb + 1) * 128)
                for j in range(4):
                    i2 = (tb * 4 + j) % 2
                    pz = PC[:, i2 * 512:i2 * 512 + NMEM]; pzk = ("PC", i2)
                    g.mm(pz, xq[:, j, tbs], KT[:, j, :], r=["xq", "KT"], w=[pzk])
                    g.op("dve", "reduce_max", r=[pzk], w=["mx"], out=mx[:, 0:1], in_=pz, axis=mybir.AxisListType.X)
                    g.op("dve", "tensor_scalar_mul", r=["mx"], w=["mx"], out=mx[:, 1:2], in0=mx[:, 0:1], scalar1=-scale)
                    g.op("dve", "memset", r=["sc_t"], w=["mx"], ap=mx[:, 2:3], constant=0.0)
                    g.act(sc_t[:], pz, AF.Exp, r=[pzk, "mx"], w=["sc_t"], bias=mx[:, 1:2], scale=scale, accum_out=mx[:, 2:3])
                    g.op("dve", "reciprocal", r=["mx", "sc_t"], w=["mx"], out=mx[:, 3:4], in_=mx[:, 2:3])
                    g.op("dve", "tensor_scalar", r=["sc_t", "mx"], w=["pb_t"], out=pb_t[:], in0=sc_t[:], scalar1=mx[:, 3:4], scalar2=None, op0=ALU.mult)
                    for mb in range(2):
                        g.tr(PT[:, mb * 128:(mb + 1) * 128], pb_t[:, mb * 128:(mb + 1) * 128], identb[:], r=["pb_t", "identb"], w=["PT"])
                    g.act(pT_t[:].rearrange("p a b -> p (a b)"), PT[:, 0:256], AF.Copy, r=["PT"], w=["pT_t"])
                    for mb in range(2):
                        g.mm(PD[:, j * 128:(j + 1) * 128], Vt[:, mb, j * 128:(j + 1) * 128], pT_t[:, mb, :], start=(mb == 0), stop=(mb == 1),
                             r=["Vt", "pT_t"], w=["PD"])
                for j in range(4):
                    pass
                g.op("dve", "tensor_copy", r=["PD"], w=["xo"], out=xo[:, :, tbs], in_=PD[:].rearrange("p (a b) -> p a b", a=4))
                if tb == 15:
                    dump("mx", mx[:], ["mx"], [128, 4], F32)
                    dump("sc_t", sc_t[:], ["sc_t"], [128, NMEM], F32)
                    dump("pb_t", pb_t[:], ["pb_t"], [128, NMEM], BF16)

            dump("xq", xq[:].rearrange("p a b -> p (a b)"), ["xq"], [128, 4 * S], BF16)
            dump("xo", xo[:].rearrange("p a b -> p (a b)"), ["xo"], [128, 4 * S], BF16)

            def getw_xo(ob):
                if ob % 4 == 0:
                    getw_xo.cur = ws.get()
                wt, wk = getw_xo.cur
                return [(wt, wk, (ob % 4) * 4 + kc, kc) for kc in range(4)]
            lin_resid(xo, "xo", 4, 0, S, getw_xo)
            if not DO_F:
                continue

            hF = g.at("hF%d" % l, [128, KC, 1024], BF16, big_off)
            actT = g.at("actT%d" % l, [128, 44, 1024], BF16, big_off + 32768)
            cg = g.at("cg%d" % l, [128, 2, 1024], F32, xt_off)
            cu = g.at("cu%d" % l, [128, 2, 1024], F32, xt_off + 8192)
            hc = g.at("hc%d" % l, [128, 2, 4], F32, xt_off + 16384)
            xr2 = g.at("xr2%d" % l, [128, 2, 1024], F32, xt_off + 16384 + 64)
            for tt in range(2):
                T0 = tt * 1024
                rmsnorm(xw, "xw", lambda kc: gcol(l, 3, kc), hF, T0, T0 + 1024, "hF",
                        extra_w=[("cg", 0), ("cg", 1), ("cu", 0), ("cu", 1), ("xr2", 0), ("xr2", 1)])
                for fb in range(44):
                    for gi, (cbuf, ckey, blk) in enumerate(((cg, "cg", fb), (cu, "cu", 44 + fb))):
                        wt, wk = ws.get()
                        pa = (PA, PB)[gi]; pk = ("PAB", gi, 0); pkb = ("PAB", gi, 1)
                        for hf in range(2):
                            for kc in range(KC):
                                g.mm(pa[:, hf * 512:(hf + 1) * 512], wt[:, kc, :], hF[:, kc, hf * 512:(hf + 1) * 512],
                                     start=(kc == 0), stop=(kc == KC - 1), r=[wk, "hF"], w=[(pk, pkb)[hf]])
                        c_ = cbuf[:, fb % 2, :]; ck = (ckey, fb % 2)
                        w0 = fconv_t[:, blk * 4:blk * 4 + 1]; w1 = fconv_t[:, blk * 4 + 1:blk * 4 + 2]
                        w2 = fconv_t[:, blk * 4 + 2:blk * 4 + 3]; bb = fconv_t[:, blk * 4 + 3:blk * 4 + 4]
                        g.act(c_, pa[:], AF.Identity, r=[pk, pkb, "fconv"], w=[ck], scale=w2, bias=bb)
                        g.op("dve", "scalar_tensor_tensor", r=[pk, pkb, "fconv", ck], w=[ck], out=c_[:, 1:1024], in0=pa[:, 0:1023], scalar=w1,
                             in1=c_[:, 1:1024], op0=ALU.mult, op1=ALU.add)
                        g.op("dve", "scalar_tensor_tensor", r=[pk, pkb, "fconv", ck], w=[ck], out=c_[:, 2:1024], in0=pa[:, 0:1022], scalar=w0,
                             in1=c_[:, 2:1024], op0=ALU.mult, op1=ALU.add)
                        if tt == 1:
                            hh = halo[:, blk, :]
                            g.op("dve", "scalar_tensor_tensor", r=["halo", "fconv", ck], w=[ck], out=c_[:, 0:2], in0=hh, scalar=w0,
                                 in1=c_[:, 0:2], op0=ALU.mult, op1=ALU.add)
                            g.op("dve", "scalar_tensor_tensor", r=["halo", "fconv", ck], w=[ck], out=c_[:, 0:1], in0=hh[:, 1:2], scalar=w1,
                                 in1=c_[:, 0:1], op0=ALU.mult, op1=ALU.add)
                        else:
                            g.act(halo[:, blk, :], pa[:, 1022:1024], AF.Copy, r=[pkb], w=["halo"])
                    g.act(cg[:, fb % 2, :], cg[:, fb % 2, :], AF.Silu, r=[("cg", fb % 2)], w=[("cg", fb % 2)])
                    g.op("dve", "tensor_tensor", r=[("cg", fb % 2), ("cu", fb % 2)], w=[("actT", fb)], out=actT[:, fb, :],
                         in0=cg[:, fb % 2, :], in1=cu[:, fb % 2, :], op=ALU.mult)
                if tt == 0:
                    dump("hF", hF[:].rearrange("p a b -> p (a b)"), ["hF"], [128, KC * 1024], BF16)
                    dump("actT", actT[:].rearrange("p a b -> p (a b)"), ["actT"], [128, 44 * 1024], BF16)
                for ob in range(16):
                    wl = []
                    for kg in range(4):
                        wt, wk = ws.get(hold=4)
                        wl += [(wt, wk, i, kg * 11 + i) for i in range(11)]
                    xs = xr2[:, ob % 2, :]; xk = ("xr2", ob % 2)
                    g.dma("sp", xs, xw[ob, :, T0:T0 + 1024], "xr", r=[("xw", ob)], w=[xk] + [("xt", kc_) for kc_ in range(8, 13)])
                    pa = (PA, PB)[ob % 2]; pk = ("PAB", ob % 2, 0); pkb = ("PAB", ob % 2, 1)
                    for hf in range(2):
                        for i, (wt, wk, ki, ci) in enumerate(wl):
                            g.mm(pa[:, hf * 512:(hf + 1) * 512], wt[:, ki, :], actT[:, ci, hf * 512:(hf + 1) * 512],
                                 start=(i == 0), stop=(i == 43), r=[wk, ("actT", ci)], w=[(pk, pkb)[hf]])
                    g.op("dve", "tensor_tensor", r=[pk, pkb, xk], w=[xk], out=xs, in0=xs, in1=pa[:], op=ALU.add)
                    g.dma("sp", xw[ob, :, T0:T0 + 1024], xs, "xr", r=[xk], w=[("xw", ob)])

        if final:
            fT = g.at("fT", [128, KC, 512], F32, big_off)
            for t0 in range(0, S, 512):
                for kc in range(KC):
                    g.dma("sp", xt[:, kc, :], xw[kc, :, t0:t0 + 512], "x", r=[("xw", kc)], w=[("xt", kc)] + XT_ALIAS)
                    g.act(sq[:, kc % 2, :], xt[:, kc, :], AF.Square, r=[("xt", kc)], w=[("sq", kc % 2)])
                    g.mm(PD[:], ones, sq[:, kc % 2, :], start=(kc == 0), stop=(kc == KC - 1), r=["cst", ("sq", kc % 2)], w=["PD"])
                g.act(rstd[:], PD[:], AF.Sqrt, r=["PD"], w=["rstd"], bias=EPS, scale=1.0 / D)
                g.op("dve", "reciprocal", r=["rstd"], w=["rstd"], out=rstd[:], in_=rstd[:])
                for kc in range(KC):
                    g.op("dve", "scalar_tensor_tensor", r=[("xt", kc), "nrm", "rstd"], w=[("fT", kc)], out=fT[:, kc, :], in0=xt[:, kc, :],
                         scalar=nrm_t[:, nl * 64 + kc:nl * 64 + kc + 1], in1=rstd[:], op0=ALU.mult, op1=ALU.mult)
                    g.dma("sp", yout[kc, :, t0:t0 + 512], fT[:, kc, :], "yo", r=[("fT", kc)], w=[("yout", kc)])
            sc.rec("sp", lambda e: None, reads=["yout"])
        else:
            sc.rec("sp", lambda e: None, reads=["xw", "mixT"] + dbg_keys)
        sc.emit(es)
    return nc


def _consts():
    i = np.arange(128)
    ident = np.eye(128, dtype=np.float32)
    Lincl = (i[:, None] >= i[None, :]).astype(np.float32)
    Uincl = (i[:, None] <= i[None, :]).astype(np.float32)
    Ustr = (i[:, None] > i[None, :]).astype(np.float32)
    mSL = (i[:, None] > i[None, :]).astype(np.float32)
    mUI = (i[:, None] <= i[None, :]).astype(np.float32)
    ones = np.ones((128, 128), np.float32)
    t = np.arange(512)
    md = [((128 * d + i[:, None]) < t[None, :]).astype(np.float32) for d in range(4)]
    return np.ascontiguousarray(np.concatenate([ident, Lincl, Uincl, Ustr, mSL, mUI, ones] + md, axis=1))


def _blk(w, nk):
    K, C = w.shape
    return np.ascontiguousarray(w.reshape(K // 128, 128, C // 128, 128).transpose(2, 1, 0, 3))


def _col(v):
    return np.ascontiguousarray(v.reshape(-1, 128).T)


def prep_layers(inp, ls):
    f = lambda k: np.asarray(inp[k], dtype=np.float32)
    nl = len(ls)
    out = {}
    nrm = np.zeros((128, nl * 64 + 16), np.float32)
    for i, l in enumerate(ls):
        for j, k in enumerate(("mix_norm", "xattn_norm", "mem_norm", "ffn_norm")):
            nrm[:, i * 64 + j * 16:i * 64 + (j + 1) * 16] = _col(f(k)[l])
    nrm[:, nl * 64:] = _col(f("final_norm"))
    out["nrm"] = nrm
    w_in = f("w_in")
    out["w_in"] = np.stack([_blk(w_in[l][:, :7168], 16) for l in ls])
    out["w_ba"] = np.stack([np.ascontiguousarray(w_in[l][:, 7168:7184].reshape(16, 128, 16).transpose(1, 0, 2)) for l in ls])
    gc = f("gdn_conv")
    out["gconv"] = np.stack([np.ascontiguousarray(gc[l].reshape(4, 24, 128).transpose(2, 1, 0).reshape(128, 96)) for l in ls])
    gsm = np.zeros((nl, 128, 17), np.float32)
    for i, l in enumerate(ls):
        gsm[i, :, 0:8] = f("gdn_a_log")[l][None, :]
        gsm[i, :, 8:16] = f("gdn_dt_bias")[l][None, :]
        gsm[i, :, 16] = f("gdn_norm")[l]
    out["gsm"] = gsm
    out["w_out"] = np.stack([_blk(f("w_out")[l], 16) for l in ls])
    out["w_xq"] = np.stack([_blk(f("w_xq")[l], 16) for l in ls])
    out["w_xkv"] = np.stack([_blk(f("w_xkv")[l], 16) for l in ls])
    wxo = f("w_xo")
    out["w_xo"] = np.stack([np.ascontiguousarray(
        wxo[l].reshape(4, 128, 4, 4, 128).transpose(2, 1, 3, 0, 4).reshape(4, 128, 16, 128)) for l in ls])
    out["w_up"] = np.stack([_blk(f("w_up")[l], 16) for l in ls])
    fc = f("ffn_conv"); fb_ = f("ffn_conv_bias")
    fconv = np.zeros((nl, 128, NFB, 4), np.float32)
    for i, l in enumerate(ls):
        fconv[i, :, :, 0:3] = fc[l].reshape(3, NFB, 128).transpose(2, 1, 0)
        fconv[i, :, :, 3] = fb_[l].reshape(NFB, 128).T
    out["fconv"] = fconv.reshape(nl, 128, NFB * 4)
    wd = f("w_down")
    out["w_dn"] = np.stack([np.ascontiguousarray(
        wd[l].reshape(4, 11, 128, 16, 128).transpose(3, 0, 2, 1, 4)) for l in ls])
    out["cst"] = _consts()
    return out


_NC_CACHE = {}


def _get_nc(nl, final, dbg=False):
    key = (nl, final, dbg)
    if key not in _NC_CACHE:
        _NC_CACHE[key] = build(nl, final, dbg)
    return _NC_CACHE[key]


NCORES = 8
FUSED = True


def kernel(**inputs):
    x = np.asarray(inputs["x"], dtype=np.float32)
    mem = np.asarray(inputs["mem"], dtype=np.float32)
    B = x.shape[0]
    xT = [np.ascontiguousarray(x[b].T.reshape(KC, 128, S)) for b in range(B)]
    mT = [np.ascontiguousarray(mem[b].T.reshape(KC, 128, NMEM)) for b in range(B)]
    groups = [[0, 1, 2, 3]] if FUSED else [[0], [1], [2], [3]]
    cur = xT
    for gi, ls in enumerate(groups):
        final = gi == len(groups) - 1
        wts = prep_layers(inputs, ls)
        nc = _get_nc(len(ls), final)
        in_maps = []
        for c in range(NCORES):
            m = dict(wts)
            m["xin"] = cur[c % B]
            m["memT"] = mT[c % B]
            in_maps.append(m)
        res = run_bass_kernel_spmd(nc, in_maps, core_ids=list(range(NCORES)))
        key = "yout" if final else "xw"
        cur = [np.asarray(res.results[b][key]) for b in range(B)]
    out = np.stack([np.ascontiguousarray(cur[b].reshape(D, S).T) for b in range(B)])
    return out.astype(np.float32)
```

```python
import numpy as np
from contextlib import ExitStack
import concourse.bass as bass
import concourse.mybir as mybir
from concourse.bass_utils import run_bass_kernel_spmd

F32 = mybir.dt.float32
BF16 = mybir.dt.bfloat16
AF = mybir.ActivationFunctionType
ALU = mybir.AluOpType

S = 2048
D = 2048
KC = 16
NMEM = 256
DFF = 5632
NFB = 88
EPS = 1e-6
ENGS = ("pe", "act", "dve", "pool", "sp")


class Sched:
    def __init__(self, nc):
        self.nc = nc
        self.ops = {e: [] for e in ENGS}
        self.state = {}
        self.subs = {}
        self.seen = {e: {} for e in ENGS}
        self.dma_count = {}

    @staticmethod
    def _norm(k):
        if isinstance(k, tuple):
            return (k[0], k[1:] if len(k) > 1 else None)
        return (k, None)

    def _conf(self, base, sub):
        if sub is None:
            return [(base, None)] + [(base, s) for s in self.subs.get(base, ())]
        return [(base, sub), (base, None)]

    def rec(self, eng, fn, reads=(), writes=(), dma_sem=None):
        deps = {}
        if eng != "pe":
            pr = [k for k in reads if (k[0] if isinstance(k, tuple) else k) in ("PAB", "PC", "PD", "PT")]
            if pr:
                reads = [k for k in reads if k not in pr]
                writes = list(writes) + pr

        def add(pid, raw=False):
            if pid is None:
                return
            stream, idx = pid
            if stream == eng and (not raw or eng in ("pe", "sp")):
                return
            if deps.get(stream, -1) < idx:
                deps[stream] = idx

        rk = [self._norm(k) for k in reads]
        wk = [self._norm(k) for k in writes]
        for base, sub in rk:
            for ck in self._conf(base, sub):
                st = self.state.get(ck)
                if st is not None:
                    add(st[0], True)
        for base, sub in wk:
            for ck in self._conf(base, sub):
                st = self.state.get(ck)
                if st is not None:
                    add(st[0])
                    for s_, i_ in st[1].items():
                        add((s_, i_))
        waits = []
        seen = self.seen[eng]
        for stream, idx in deps.items():
            if isinstance(stream, tuple):
                cnt = self.dma_count[stream[1]]
                if seen.get(stream, -1) >= cnt:
                    continue
                seen[stream] = cnt
                waits.append((stream, cnt))
            else:
                if seen.get(stream, -1) >= idx:
                    continue
                seen[stream] = idx
                self.ops[stream][idx]["inc"] = True
                waits.append((stream, idx))
        if dma_sem is not None:
            c = self.dma_count.get(dma_sem, 0) + 1
            self.dma_count[dma_sem] = c
            pid = (("d", dma_sem), c)
        else:
            pid = (eng, len(self.ops[eng]))
        self.ops[eng].append({"fn": fn, "waits": waits, "inc": False, "dma": dma_sem})
        for base, sub in rk:
            st = self.state.setdefault((base, sub), [None, {}])
            if sub is not None:
                self.subs.setdefault(base, set()).add(sub)
            if st[1].get(pid[0], -1) < pid[1]:
                st[1][pid[0]] = pid[1]
        for base, sub in wk:
            if sub is None:
                for s in self.subs.get(base, ()):
                    self.state.pop((base, s), None)
                self.subs[base] = set()
            else:
                self.subs.setdefault(base, set()).add(sub)
            self.state[(base, sub)] = [pid, {}]

    def emit(self, es):
        nc = self.nc
        sems = {}
        for e in ("pe", "act", "dve", "pool"):
            sems[e] = es.enter_context(nc.semaphore("s_" + e))
        for name in self.dma_count:
            sems[("d", name)] = es.enter_context(nc.semaphore("d_" + name))
        cum = {}
        for e in ENGS:
            c = 0
            arr = []
            for op in self.ops[e]:
                if op["inc"]:
                    c += 1
                arr.append(c)
            cum[e] = arr
        block = es.enter_context(nc.Block())
        ops = self.ops

        def run(e, engh):
            for op in ops[e]:
                for stream, v in op["waits"]:
                    if isinstance(stream, tuple):
                        engh.wait_ge(sems[stream], 16 * v)
                    else:
                        engh.wait_ge(sems[stream], cum[stream][v])
                ins = op["fn"](engh)
                if ins is None:
                    continue
                if op["dma"] is not None:
                    ins.then_inc(sems[("d", op["dma"])], 16)
                elif op["inc"]:
                    ins.then_inc(sems[e], 1)

        @block.tensor
        def _(t):
            run("pe", t)

        @block.scalar
        def _(t):
            run("act", t)

        @block.vector
        def _(t):
            run("dve", t)

        @block.gpsimd
        def _(t):
            run("pool", t)

        @block.sync
        def _(t):
            run("sp", t)


class Gen:
    def __init__(self, nc, es):
        self.nc = nc
        self.es = es
        self.sc = Sched(nc)
        slab = nc.alloc_sbuf_tensor("slab", [128, 207000], mybir.dt.uint8)
        self.base = nc.lookup_mloc(slab).addr
        self.nps = 0

    def at(self, name, shape, dt, off):
        return self.nc.alloc_sbuf_tensor_at(name, shape, dt, offset=self.base + off)

    def mm(self, out, lhsT, rhs, start=True, stop=True, r=(), w=()):
        self.sc.rec("pe", lambda e: e.matmul(out, lhsT, rhs, start=start, stop=stop), reads=r, writes=w)

    def tr(self, out, in_, ident, r=(), w=()):
        self.sc.rec("pe", lambda e: e.transpose(out, in_, ident), reads=r, writes=w)

    def op(self, eng, method, r=(), w=(), **kw):
        self.sc.rec(eng, lambda e: getattr(e, method)(**kw), reads=r, writes=w)

    def act(self, out, in_, func, r=(), w=(), **kw):
        self.sc.rec("act", lambda e: e.activation(out=out, in_=in_, func=func, **kw), reads=r, writes=w)

    def dma(self, eng, out, in_, sem, r=(), w=()):
        self.sc.rec(eng, lambda e: e.dma_start(out=out, in_=in_), reads=r, writes=w, dma_sem=sem)


class WStream:
    def __init__(self, g, tiles):
        self.g = g
        self.tiles = tiles
        self.plan = []
        self.loaded = 0
        self.used = 0

    def add(self, ap, nk=16):
        self.plan.append((ap, nk))

    def get(self, hold=1):
        n = len(self.tiles)
        while self.loaded < len(self.plan) and self.loaded < self.used + n - (hold - 1):
            ap, nk = self.plan[self.loaded]
            i = self.loaded % n
            self.g.dma("pool", self.tiles[i][:, 0:nk, :], ap, "w%d" % i, w=[("wb", i)])
            self.loaded += 1
        i = self.used % n
        self.used += 1
        return self.tiles[i], ("wb", i)


def build(nl, final, dbg=False, cfg=None):
    cfg = cfg or {}
    nc = bass.Bass("TRN2", target_bir_lowering=False)
    dr = lambda name, shape, dt, kind="ExternalInput": nc.dram_tensor(name, shape, dt, kind=kind).ap()
    xin = dr("xin", [KC, 128, S], F32)
    memT = dr("memT", [KC, 128, NMEM], F32)
    cst = dr("cst", [128, 7 * 128 + 4 * 512], F32)
    nrm = dr("nrm", [128, nl * 64 + 16], F32)
    w_in = dr("w_in", [nl, 56, 128, 16, 128], F32)
    w_ba = dr("w_ba", [nl, 128, 16, 16], F32)
    gconv = dr("gconv", [nl, 128, 24 * 4], F32)
    gsm = dr("gsm", [nl, 128, 17], F32)
    w_out = dr("w_out", [nl, 16, 128, 16, 128], F32)
    w_xq = dr("w_xq", [nl, 4, 128, 16, 128], F32)
    w_xkv = dr("w_xkv", [nl, 8, 128, 16, 128], F32)
    w_xo = dr("w_xo", [nl, 4, 128, 16, 128], F32)
    w_up = dr("w_up", [nl, NFB, 128, 16, 128], F32)
    fconv = dr("fconv", [nl, 128, NFB * 4], F32)
    w_dn = dr("w_dn", [nl, 16, 4, 128, 11, 128], F32)
    okind = "ExternalOutput"
    xw = dr("xw", [KC, 128, S], F32, kind=okind if not final else "Internal")
    yout = dr("yout", [KC, 128, S], F32, kind=okind) if final else None
    mixT = dr("mixT", [KC, 128, S], BF16, kind=okind if dbg else "Internal")
    dbg_keys = []

    def dump(name, ap, keys, shape, dt):
        if not dbg or not cfg.get("dump"):
            return
        if name not in cfg["dump"]:
            return
        t = dr("D_" + name, shape, dt, kind=okind)
        g.dma("sp", t, ap, "dbg", r=keys, w=["D_" + name])
        dbg_keys.append("D_" + name)

    NSB = cfg.get("sb", 8); NGD = cfg.get("gdn", 8); DO_OUT = cfg.get("out", True)
    DO_X = cfg.get("xattn", True); DO_F = cfg.get("ffn", True)
    es = ExitStack()
    with es:
        g = Gen(nc, es)
        sc = g.sc
        o = 0
        cst_t = g.at("cst_t", [128, 7 * 128 + 4 * 512], F32, o); o += (7 * 128 + 4 * 512) * 4
        ident = cst_t[:, 0:128]; Lincl = cst_t[:, 128:256]; Uincl = cst_t[:, 256:384]
        Ustr = cst_t[:, 384:512]; mSL = cst_t[:, 512:640]; mUI = cst_t[:, 640:768]; ones = cst_t[:, 768:896]
        maskd = [cst_t[:, 896 + 512 * d: 896 + 512 * (d + 1)] for d in range(4)]
        identb = g.at("identb", [128, 128], BF16, o); o += 256
        nrm_t = g.at("nrm_t", [128, nl * 64 + 16], F32, o); o += (nl * 64 + 16) * 4
        gconv_t = g.at("gconv_t", [128, 96], F32, o); o += 384
        gsm_t = g.at("gsm_t", [128, 17], F32, o); o += 68 + 28
        fconv_t = g.at("fconv_t", [128, NFB * 4], F32, o); o += NFB * 16
        halo = g.at("halo", [128, NFB, 2], F32, o); o += NFB * 8
        rstd = g.at("rstd", [128, 512], F32, o); o += 2048
        sq = g.at("sq", [128, 2, 512], F32, o); o += 4096
        NWB = 6
        wbt = g.at("wbt", [128, NWB, 16, 128], BF16, o); o += NWB * 4096
        wtiles = [wbt[:, i] for i in range(NWB)]
        xt_off = o
        xt = g.at("xt", [128, KC, 512], F32, o); o += 32768
        big_off = o
        hT = g.at("hT", [128, KC, S], BF16, o)
        ar = o + 65536
        o += 122880
        assert o <= 207000, o
        ws = WStream(g, wtiles)

        PA = es.enter_context(nc.psum_tensor("PA", [128, 1024], F32))
        PB = es.enter_context(nc.psum_tensor("PB", [128, 1024], F32))
        PC = es.enter_context(nc.psum_tensor("PC", [128, 1024], F32))
        PD = es.enter_context(nc.psum_tensor("PD", [128, 512], F32))
        PTf = es.enter_context(nc.psum_tensor("PT", [128, 512], F32))
        PT = PTf.bitcast(BF16)

        g.dma("sp", cst_t[:], cst, "c", w=["cst"])
        g.dma("sp", nrm_t[:], nrm, "c", w=["nrm"])
        g.op("dve", "tensor_copy", r=["cst"], w=["identb"], out=identb[:], in_=ident)
        for kc in range(KC):
            g.dma("sp", xw[kc], xin[kc], "xcp", w=[("xw", kc)])

        def gcol(l, which, kc):
            c = l * 64 + which * 16 + kc
            return nrm_t[:, c:c + 1]

        XT_ALIAS = [("cg", 0), ("cg", 1), ("cu", 0), ("cu", 1), ("xr2", 0), ("xr2", 1), "gq", "gk", "gv", "gz"]

        def rmsnorm(src, srckey, gc, dst, T0, T1, dkey, extra_w=()):
            for t0 in range(T0, T1, 512):
                for kc in range(KC):
                    g.dma("sp", xt[:, kc, :], src[kc, :, t0:t0 + 512], "x", r=[(srckey, kc)], w=[("xt", kc)] + XT_ALIAS)
                    g.act(sq[:, kc % 2, :], xt[:, kc, :], AF.Square, r=[("xt", kc)], w=[("sq", kc % 2)])
                    g.mm(PD[:], ones, sq[:, kc % 2, :], start=(kc == 0), stop=(kc == KC - 1),
                         r=["cst", ("sq", kc % 2)], w=["PD"])
                g.act(rstd[:], PD[:], AF.Sqrt, r=["PD"], w=["rstd"], bias=EPS, scale=1.0 / D)
                g.op("dve", "reciprocal", r=["rstd"], w=["rstd"], out=rstd[:], in_=rstd[:])
                for kc in range(KC):
                    g.op("dve", "scalar_tensor_tensor", r=[("xt", kc), "nrm", "rstd"], w=[(dkey, kc, t0)],
                         out=dst[:, kc, t0 - T0:t0 - T0 + 512], in0=xt[:, kc, :], scalar=gc(kc), in1=rstd[:],
                         op0=ALU.mult, op1=ALU.mult)

        def proj(src, skey, T, evac):
            wt, wk = ws.get()
            for tt in range(T // 512):
                pa = (PA, PB)[tt % 2]
                half = (tt // 2) % 2
                pv = pa[:, half * 512:(half + 1) * 512]
                pk = ("PAB", tt % 2, half)
                for kc in range(KC):
                    g.mm(pv, wt[:, kc, :], src[:, kc, tt * 512:(tt + 1) * 512], start=(kc == 0), stop=(kc == KC - 1),
                         r=[wk, skey], w=[pk])
                evac(tt, pv, pk)

        def residual_linear(wplan_nk, nkc, src, skey, T0, T):
            raise NotImplementedError

        for l in range(nl):
            g.dma("sp", gconv_t[:], gconv[l], "c", w=["gconv"])
            g.dma("sp", gsm_t[:], gsm[l], "c", w=["gsm"])
            g.dma("sp", fconv_t[:], fconv[l], "c", w=["fconv"])
            for hd in range(NSB):
                for j in range(3):
                    ws.add(w_in[l, j * 8 + hd])
            for hd in range(NGD):
                for j in range(4):
                    ws.add(w_in[l, 24 + j * 8 + hd])
            for ob in range(16 if DO_OUT else 0):
                ws.add(w_out[l, ob])
            for j in range(8 if DO_X else 0):
                ws.add(w_xkv[l, j])
            for j in range(4 if DO_X else 0):
                ws.add(w_xq[l, j])
            for j in range(4 if DO_X else 0):
                ws.add(w_xo[l, j])
            for tt in range(2 if DO_F else 0):
                for fb in range(44):
                    ws.add(w_up[l, fb]); ws.add(w_up[l, 44 + fb])
                for ob in range(16):
                    for kg in range(4):
                        ws.add(w_dn[l, ob, kg], 11)

            rmsnorm(xw, "xw", lambda kc: gcol(l, 0, kc), hT, 0, S, "hT")

            a = ar
            wba_t = g.at("wba_t%d" % l, [128, 16, 16], BF16, a); a += 512
            bg = g.at("bg%d" % l, [128, 16, 16], F32, a); a += 1024
            beta_t = g.at("beta%d" % l, [128, 16, 8], F32, a); a += 512
            nbeta_t = g.at("nbeta%d" % l, [128, 16, 8], F32, a); a += 512
            g_t = g.at("g_t%d" % l, [128, 16, 8], F32, a); a += 512
            nA = g.at("nA%d" % l, [128, 8], F32, a); a += 32
            gates_end = a
            g.dma("pool", wba_t[:], w_ba[l], "wba", w=["wba", "actT"])
            for blk in range(16):
                for kc in range(KC):
                    g.mm(PD[:, blk * 16:(blk + 1) * 16], hT[:, kc, blk * 128:(blk + 1) * 128], wba_t[:, kc, :],
                         start=(kc == 0), stop=(kc == KC - 1), r=["hT", "wba"], w=["PD"])
            g.op("dve", "tensor_copy", r=["PD"], w=["bg"], out=bg[:].rearrange("p a b -> p (a b)"), in_=PD[:, 0:256])
            g.act(beta_t[:], bg[:, :, 0:8], AF.Exp, r=["bg"], w=["beta"], scale=-1.0)
            g.op("dve", "tensor_scalar_add", r=["beta"], w=["beta"], out=beta_t[:], in0=beta_t[:], scalar1=1.0)
            g.op("dve", "reciprocal", r=["beta"], w=["beta"], out=beta_t[:], in_=beta_t[:])
            g.op("dve", "tensor_scalar_mul", r=["beta"], w=["nbeta"], out=nbeta_t[:], in0=beta_t[:], scalar1=-1.0)
            g.act(nA[:], gsm_t[:, 0:8], AF.Exp, r=["gsm"], w=["nA"])
            g.op("dve", "tensor_scalar_mul", r=["nA"], w=["nA"], out=nA[:], in0=nA[:], scalar1=-1.0)
            for blk in range(16):
                g.op("dve", "tensor_tensor", r=["bg", "gsm"], w=["g_t"], out=g_t[:, blk, :], in0=bg[:, blk, 8:16],
                     in1=gsm_t[:, 8:16], op=ALU.add)
            g.act(g_t[:], g_t[:], AF.Exp, r=["g_t"], w=["g_t"])
            g.act(g_t[:], g_t[:], AF.Ln, r=["g_t"], w=["g_t"], bias=1.0)
            for blk in range(16):
                g.op("dve", "tensor_tensor", r=["g_t", "nA"], w=["g_t"], out=g_t[:, blk, :], in0=g_t[:, blk, :],
                     in1=nA[:], op=ALU.mult)

            a = gates_end
            a = (a + 63) // 64 * 64
            qT = g.at("qT%d" % l, [128, S], BF16, a); a += 4096
            kT = g.at("kT%d" % l, [128, S], BF16, a); a += 4096
            vT = g.at("vT%d" % l, [128, S], BF16, a); a += 4096
            vtok = g.at("vtok%d" % l, [128, 16, 128], BF16, a); a += 4096
            ebuf = g.at("ebuf%d" % l, [128, 3, 512], F32, a); a += 6144
            spbuf = g.at("spbuf%d" % l, [128, 3, 512], F32, a); a += 6144
            ecbuf = g.at("ecbuf%d" % l, [128, 2, 512], F32, a); a += 4096
            racc = g.at("racc%d" % l, [128, 512], F32, a); a += 2048
            wbuf = g.at("wbuf%d" % l, [128, 2, 512], BF16, a); a += 2048
            obuf = g.at("obuf%d" % l, [128, 2, 512], BF16, a); a += 2048
            assert a <= big_off + 122880
            scale = 128.0 ** -0.5
            for hd in range(NSB):
                for j, dst, key in ((0, qT, "qT"), (1, kT, "kT"), (2, vT, "vT")):
                    def ev(tt, pv, pk, dst=dst, key=key):
                        if tt % 2 == 0:
                            g.act(dst[:, tt * 512:(tt + 1) * 512], pv, AF.Copy, r=[pk], w=[(key, tt)])
                        else:
                            g.op("dve", "tensor_copy", r=[pk], w=[(key, tt)], out=dst[:, tt * 512:(tt + 1) * 512], in_=pv)
                    proj(hT, "hT", S, ev)
                for half in range(2):
                    for b in range(8):
                        blk = half * 8 + b
                        g.tr(PT[:, b * 128:(b + 1) * 128], vT[:, blk * 128:(blk + 1) * 128], identb[:],
                             r=[("vT", blk // 4), "identb"], w=["PT"])
                    g.op("dve", "tensor_copy", r=["PT"], w=[("vtok", half)],
                         out=vtok[:, half * 8:(half + 1) * 8, :].rearrange("p a b -> p (a b)"), in_=PT[:])
                its = []
                for qt in range(4):
                    nkb = 4 * qt + 4
                    for it, kb in enumerate(range(nkb - 1, -1, -1)):
                        its.append((qt, it, kb))

                def stA(n):
                    qt, it, kb = its[n]
                    i2 = n % 2; i3 = n % 3
                    pz = PC[:, i2 * 512:(i2 + 1) * 512]; pzk = ("PC", i2)
                    g.mm(pz, kT[:, kb * 128:(kb + 1) * 128], qT[:, qt * 512:(qt + 1) * 512], r=[("kT", kb // 4), ("qT", qt)], w=[pzk])
                    e_ = ebuf[:, i3, :]; sp_ = spbuf[:, i3, :]
                    g.act(e_, pz, AF.Exp, r=[pzk], w=[("e", i3)], scale=scale)
                    g.act(sp_, e_, AF.Ln, r=[("e", i3)], w=[("sp", i3)], bias=1.0)
                    if kb >= 4 * qt:
                        md = maskd[kb - 4 * qt]
                        g.op("dve", "tensor_tensor", r=[("sp", i3), "cst"], w=[("sp", i3)], out=sp_, in0=sp_, in1=md, op=ALU.mult)
                        g.op("dve", "tensor_tensor", r=[("e", i3), "cst"], w=[("e", i3)], out=e_, in0=e_, in1=md, op=ALU.mult)

                def stB(n):
                    qt, it, kb = its[n]
                    i2 = n % 2; i3 = n % 3
                    e_ = ebuf[:, i3, :]; sp_ = spbuf[:, i3, :]; ec_ = ecbuf[:, i2, :]
                    pcs = (PA, PB)[i2][:, 512:1024]; pck = ("PAB", i2, 1)
                    g.mm(pcs, Lincl, sp_, start=True, stop=(it == 0), r=["cst", ("sp", i3)], w=[pck])
                    if it > 0:
                        g.mm(pcs, ones, racc[:], start=False, stop=True, r=["cst", "racc"], w=[pck])
                    g.act(ec_, pcs, AF.Exp, r=[pck], w=[("ec", i2)], scale=-1.0)
                    g.op("dve", "tensor_tensor", r=[("e", i3), ("ec", i2)], w=[("wbuf", i2)], out=wbuf[:, i2, :], in0=e_, in1=ec_, op=ALU.mult)
                    if it == 0:
                        g.op("dve", "tensor_copy", r=[("sp", i3)], w=["racc"], out=racc[:], in_=sp_)
                    elif kb > 0:
                        g.op("dve", "tensor_tensor", r=[("sp", i3), "racc"], w=["racc"], out=racc[:], in0=racc[:], in1=sp_, op=ALU.add)

                def stC(n):
                    qt, it, kb = its[n]
                    i2 = n % 2
                    g.mm(PD[:], vtok[:, kb, :], wbuf[:, i2, :], start=(it == 0), stop=(kb == 0),
                         r=[("vtok", kb // 8), ("wbuf", i2)], w=["PD"])
                    if kb == 0:
                        ob_ = obuf[:, qt % 2, :]
                        g.act(ob_, PD[:], AF.Copy, r=["PD"], w=[("obuf", qt % 2)])
                        g.dma("sp", mixT[hd, :, qt * 512:(qt + 1) * 512], ob_, "mo", r=[("obuf", qt % 2)], w=[("mixT", hd)])

                NI = len(its)
                for n in range(NI + 2):
                    if n < NI:
                        stA(n)
                    if 0 <= n - 1 < NI:
                        stB(n - 1)
                    if 0 <= n - 2 < NI:
                        stC(n - 2)

            a = gates_end
            a = (a + 63) // 64 * 64
            gates_end_al = a
            gq = xt[:].rearrange("p a b -> p (a b)")
            gqT = gq[:, 0:S]; gkT = gq[:, S:2 * S]; gvT = gq[:, 2 * S:3 * S]; gzT = gq[:, 3 * S:4 * S]
            cb = g.at("cb%d" % l, [128, S + 8], F32, a); a += (S + 8) * 4
            ogT = g.at("ogT%d" % l, [128, S], F32, a); a += S * 4
            wide = {}
            for nm in ("Gm", "DecL", "DecT", "egcb", "Qa", "Qb", "Pa", "Pb", "X", "vb", "kbg", "ktail", "u", "wT", "ATm", "qdT"):
                wide[nm] = g.at(nm + str(l), [128, 4, 128], F32, a); a += 2048
            Sst = g.at("Sst%d" % l, [128, 2, 128], F32, a); a += 1024
            vnew = g.at("vnew%d" % l, [128, 128], F32, a); a += 512
            ecols = g.at("ecols%d" % l, [128, 4, 4], F32, a); a += 64
            bcol2 = g.at("bcol2%d" % l, [128, 4], F32, a); a += 64
            ecols1 = g.at("ecols1_%d" % l, [128, 4, 4], F32, a); a += 64
            ktail1 = g.at("ktail1_%d" % l, [128, 4, 128], F32, a); a += 2048
            assert a <= big_off + 122880, a
            cb_off = gates_end_al
            scan_in = {0: {"u": wide["u"], "wT": wide["wT"], "ATm": wide["ATm"], "qdT": wide["qdT"], "ktail": wide["ktail"], "ecols": ecols},
                       1: {"u": g.at("u1_%d" % l, [128, 4, 128], F32, cb_off), "wT": g.at("wT1_%d" % l, [128, 4, 128], F32, cb_off + 2048),
                           "ATm": g.at("ATm1_%d" % l, [128, 4, 128], F32, cb_off + 4096), "qdT": g.at("qdT1_%d" % l, [128, 4, 128], F32, cb_off + 6144),
                           "ktail": ktail1, "ecols": ecols1}}
            CB_ALIAS = ["u1", "wT1", "ATm1", "qdT1"]
            Gm, DecL, DecT, egcb = wide["Gm"], wide["DecL"], wide["DecT"], wide["egcb"]
            X, vb, kbg, ktail, u_, wT_, ATm, qdT = (wide[n] for n in ("X", "vb", "kbg", "ktail", "u", "wT", "ATm", "qdT"))
            fl = lambda t: t[:].rearrange("p a b -> p (a b)")
            class _Stop(Exception):
                pass

            def chk(stage):
                if cfg.get("gstop") == stage:
                    raise _Stop()

            def gdn_head(hd):
                for j, dst, key in ((0, gqT, "gq"), (1, gkT, "gk"), (2, gvT, "gv"), (3, gzT, "gz")):
                    if j < 3:
                        g.op("dve", "memset", w=["cb"] + CB_ALIAS, ap=cb[:, 0:3], constant=0.0)
                        def ev(tt, pv, pk):
                            if tt % 2 == 0:
                                g.act(cb[:, 3 + tt * 512:3 + (tt + 1) * 512], pv, AF.Copy, r=[pk], w=["cb"] + CB_ALIAS)
                            else:
                                g.op("dve", "tensor_copy", r=[pk], w=["cb"] + CB_ALIAS, out=cb[:, 3 + tt * 512:3 + (tt + 1) * 512], in_=pv)
                        proj(hT, "hT", S, ev)
                        fb = j * 8 + hd
                        wc = lambda i: gconv_t[:, fb * 4 + i:fb * 4 + i + 1]
                        g.act(dst, cb[:, 3:3 + S], AF.Identity, r=["cb", "gconv"], w=[key], scale=wc(3))
                        for i in range(3):
                            g.op("dve", "scalar_tensor_tensor", r=["cb", "gconv", key], w=[key], out=dst, in0=cb[:, i:i + S],
                                 scalar=wc(i), in1=dst, op0=ALU.mult, op1=ALU.add)
                        g.act(dst, dst, AF.Silu, r=[key], w=[key])
                    else:
                        def ev(tt, pv, pk):
                            g.act(gzT[:, tt * 512:(tt + 1) * 512], pv, AF.Silu, r=[pk], w=["gz"])
                        proj(hT, "hT", S, ev)
                chk(1)
                for dst, key, sc_ in ((gqT, "gq", 128.0 ** -0.5), (gkT, "gk", 1.0)):
                    for tt in range(4):
                        tsl = slice(tt * 512, (tt + 1) * 512)
                        g.act(sq[:, tt % 2, :], dst[:, tsl], AF.Square, r=[key], w=[("sq", tt % 2)])
                        g.mm(PD[:], ones, sq[:, tt % 2, :], r=["cst", ("sq", tt % 2)], w=["PD"])
                        g.act(rstd[:], PD[:], AF.Sqrt, r=["PD"], w=["rstd"], bias=EPS, scale=1.0)
                        g.op("dve", "reciprocal", r=["rstd"], w=["rstd"], out=rstd[:], in_=rstd[:])
                        g.op("dve", "scalar_tensor_tensor", r=[key, "rstd"], w=[key], out=dst[:, tsl], in0=dst[:, tsl],
                             scalar=sc_, in1=rstd[:], op0=ALU.mult, op1=ALU.mult)
                chk(2)
                g.op("dve", "memset", w=[("Sst", 0)], ap=Sst[:, 0, :], constant=0.0)

                def prep_stages(grp):
                    par = grp % 2
                    blks = [grp * 4 + b for b in range(4)]
                    si = scan_in[par]
                    u_, wT_, ATm, qdT, ktail, ecols = si["u"], si["wT"], si["ATm"], si["qdT"], si["ktail"], si["ecols"]
                    ku, kw, ka, kq, kk, ke = ("u%d" % par, "wT%d" % par, "ATm%d" % par, "qdT%d" % par, "ktail%d" % par, "ecols%d" % par)
                    st = []

                    def s1():
                        for b, n in enumerate(blks):
                            g.op("dve", "tensor_scalar", r=["cst", "g_t"], w=["Gm"], out=Gm[:, b, :], in0=Uincl,
                                 scalar1=g_t[:, n, hd:hd + 1], scalar2=None, op0=ALU.mult)
                    st.append(s1)

                    def s2():
                        for b, n in enumerate(blks):
                            g.mm(PA[:, b * 128:(b + 1) * 128], Gm[:, b, :], Ustr, r=["Gm", "cst"], w=[("PAB", 0, 0)])
                            g.mm(PA[:, 512 + b * 128:512 + (b + 1) * 128], Ustr, Gm[:, b, :], r=["Gm", "cst"], w=[("PAB", 0, 1)])
                            g.mm(PB[:, b * 128:(b + 1) * 128], ones, Gm[:, b, :], r=["Gm", "cst"], w=[("PAB", 1, 0)])
                            g.mm(PC[:, b * 4:b * 4 + 1], Gm[:, b, :], ones[:, 0:1], r=["Gm", "cst"], w=[("PC", 0)])
                            g.mm(PC[:, b * 4 + 1:b * 4 + 2], Ustr, g_t[:, n, hd:hd + 1], r=["g_t", "cst"], w=[("PC", 0)])
                            g.mm(PC[:, b * 4 + 2:b * 4 + 3], ones, g_t[:, n, hd:hd + 1], r=["g_t", "cst"], w=[("PC", 0)])
                    st.append(s2)

                    def s3():
                        g.act(fl(DecL), PA[:, 0:512], AF.Exp, r=[("PAB", 0, 0)], w=["DecL"])
                        g.act(fl(DecT), PA[:, 512:1024], AF.Exp, r=[("PAB", 0, 1)], w=["DecT"])
                        g.act(fl(egcb), PB[:, 0:512], AF.Exp, r=[("PAB", 1, 0)], w=["egcb"])
                        g.act(ecols[:].rearrange("p a b -> p (a b)"), PC[:, 0:16], AF.Exp, r=[("PC", 0)], w=[ke])
                    st.append(s3)

                    def s4():
                        for b, n in enumerate(blks):
                            g.op("dve", "tensor_tensor", r=["DecL", "cst"], w=["DecL"], out=DecL[:, b, :], in0=DecL[:, b, :], in1=mSL, op=ALU.mult)
                            g.op("dve", "tensor_tensor", r=["DecT", "cst"], w=["DecT"], out=DecT[:, b, :], in0=DecT[:, b, :], in1=mUI, op=ALU.mult)
                            g.op("dve", "tensor_tensor", r=[ke, "beta"], w=["bcol2"], out=bcol2[:, b:b + 1], in0=ecols[:, b, 0:1],
                                 in1=beta_t[:, n, hd:hd + 1], op=ALU.mult)
                        for b, n in enumerate(blks):
                            bs = slice(n * 128, (n + 1) * 128)
                            g.mm(PC[:, b * 128:(b + 1) * 128], gkT[:, bs], gkT[:, bs], r=["gk"], w=[("PC", 0)])
                    st.append(s4)

                    def s5():
                        Q = wide["Qa"]
                        for b, n in enumerate(blks):
                            g.op("dve", "scalar_tensor_tensor", r=[("PC", 0), "nbeta", "DecL"], w=["Qa"], out=Q[:, b, :],
                                 in0=PC[:, b * 128:(b + 1) * 128], scalar=nbeta_t[:, n, hd:hd + 1], in1=DecL[:, b, :],
                                 op0=ALU.mult, op1=ALU.mult)
                        for b in range(4):
                            g.mm(PC[:, 512 + b * 128:512 + (b + 1) * 128], Q[:, b, :], ident, r=["Qa", "cst"], w=[("PC", 1)])
                    st.append(s5)

                    def s6():
                        P = wide["Pa"]
                        g.act(fl(P), PC[:, 512:1024], AF.Copy, r=[("PC", 1)], w=["Pa"])
                        for b in range(4):
                            g.op("dve", "tensor_tensor", r=["Pa", "cst"], w=["X"], out=X[:, b, :], in0=P[:, b, :], in1=ident, op=ALU.add)
                    st.append(s6)
                    names = [("Qa", "Pa"), ("Qb", "Pb")]
                    for step in range(6):
                        qn, pn = names[step % 2]
                        qn2, pn2 = names[(step + 1) % 2]

                        def sa(qn=qn, pn=pn, qn2=qn2, pn2=pn2, step=step):
                            Q, P, Q2, P2 = wide[qn], wide[pn], wide[qn2], wide[pn2]
                            for b in range(4):
                                g.mm(PC[:, b * 128:(b + 1) * 128], P[:, b, :], Q[:, b, :], r=[qn, pn], w=[("PC", 0)])
                            if step < 5:
                                for b in range(4):
                                    g.mm(PC[:, 512 + b * 128:512 + (b + 1) * 128], Q[:, b, :], P[:, b, :], r=[qn, pn], w=[("PC", 1)])
                            g.act(fl(Q2), PC[:, 0:512], AF.Copy, r=[("PC", 0)], w=[qn2])
                            if step < 5:
                                g.op("dve", "tensor_copy", r=[("PC", 1)], w=[pn2], out=fl(P2), in_=PC[:, 512:1024])
                        st.append(sa)

                        def sb_(qn2=qn2):
                            Q2 = wide[qn2]
                            for b in range(4):
                                g.mm(PB[:, 512 + b * 128:512 + (b + 1) * 128], Q2[:, b, :], X[:, b, :], r=[qn2, "X"], w=[("PAB", 1, 1)])
                            g.op("dve", "tensor_tensor", r=[("PAB", 1, 1), "X"], w=["X"], out=fl(X), in0=fl(X), in1=PB[:, 512:1024], op=ALU.add)
                        st.append(sb_)

                    def s7():
                        for b, n in enumerate(blks):
                            bs = slice(n * 128, (n + 1) * 128)
                            g.mm(PA[:, b * 128:(b + 1) * 128], gkT[:, bs], ident, r=["gk", "cst"], w=[("PAB", 0, 0)])
                            g.mm(PA[:, 512 + b * 128:512 + (b + 1) * 128], gvT[:, bs], ident, r=["gv", "cst"], w=[("PAB", 0, 1)])
                            g.mm(PB[:, b * 128:(b + 1) * 128], gkT[:, bs], gqT[:, bs], r=["gk", "gq"], w=[("PAB", 1, 0)])
                    st.append(s7)

                    def s8():
                        for b, n in enumerate(blks):
                            g.op("dve", "tensor_scalar", r=[("PAB", 0, 0), "bcol2"], w=["kbg"], out=kbg[:, b, :], in0=PA[:, b * 128:(b + 1) * 128],
                                 scalar1=bcol2[:, b:b + 1], scalar2=None, op0=ALU.mult)
                            g.op("dve", "tensor_scalar", r=[("PAB", 0, 0), ke], w=[kk], out=ktail[:, b, :], in0=PA[:, b * 128:(b + 1) * 128],
                                 scalar1=ecols[:, b, 1:2], scalar2=None, op0=ALU.mult)
                        for b, n in enumerate(blks):
                            g.act(vb[:, b, :], PA[:, 512 + b * 128:512 + (b + 1) * 128], AF.Identity, r=[("PAB", 0, 1), "beta"], w=["vb"],
                                  scale=beta_t[:, n, hd:hd + 1])
                        g.op("dve", "tensor_tensor", r=[("PAB", 1, 0), "DecT"], w=[ka], out=fl(ATm), in0=fl(DecT), in1=PB[:, 0:512], op=ALU.mult)
                        g.op("dve", "tensor_tensor", r=["gq", "egcb"], w=[kq], out=fl(qdT), in0=gqT[:, grp * 512:(grp + 1) * 512], in1=fl(egcb), op=ALU.mult)
                    st.append(s8)

                    def s9():
                        for b, n in enumerate(blks):
                            g.mm(PA[:, b * 128:(b + 1) * 128], X[:, b, :], vb[:, b, :], r=["X", "vb"], w=[("PAB", 0, 0)])
                            g.mm(PA[:, 512 + b * 128:512 + (b + 1) * 128], kbg[:, b, :], X[:, b, :], r=["X", "kbg"], w=[("PAB", 0, 1)])
                        g.act(fl(u_), PA[:, 0:512], AF.Copy, r=[("PAB", 0, 0)], w=[ku])
                        g.op("dve", "tensor_copy", r=[("PAB", 0, 1)], w=[kw], out=fl(wT_), in_=PA[:, 512:1024])
                    st.append(s9)
                    return st

                def scan_stages(grp):
                    par = grp % 2
                    blks = [grp * 4 + b for b in range(4)]
                    si = scan_in[par]
                    u_, wT_, ATm, qdT, ktail, ecols = si["u"], si["wT"], si["ATm"], si["qdT"], si["ktail"], si["ecols"]
                    ku, kw, ka, kq, kk, ke = ("u%d" % par, "wT%d" % par, "ATm%d" % par, "qdT%d" % par, "ktail%d" % par, "ecols%d" % par)
                    st = []
                    for b, n in enumerate(blks):
                        s0 = n % 2; s1_ = 1 - s0
                        Sc = Sst[:, s0, :]; Sn = Sst[:, s1_, :]

                        def c1(b=b, s0=s0, Sc=Sc):
                            g.mm(PD[:, 0:128], wT_[:, b, :], Sc, r=[kw, ("Sst", s0)], w=["PD"])
                            g.op("dve", "tensor_tensor", r=[ku, "PD"], w=["vnew"], out=vnew[:], in0=u_[:, b, :], in1=PD[:, 0:128], op=ALU.subtract)
                        st.append(c1)

                        def c2(b=b, n=n, s0=s0, s1_=s1_, Sc=Sc, Sn=Sn):
                            g.mm(PTf[:, 0:128], Sc, qdT[:, b, :], start=True, stop=False, r=[("Sst", s0), kq], w=["PT"])
                            g.mm(PTf[:, 0:128], vnew[:], ATm[:, b, :], start=False, stop=True, r=["vnew", ka], w=["PT"])
                            g.mm(PD[:, 128:256], ktail[:, b, :], vnew[:], r=[kk, "vnew"], w=["PD"])
                            g.op("dve", "scalar_tensor_tensor", r=[("Sst", s0), ke, "PD"], w=[("Sst", s1_)], out=Sn, in0=Sc,
                                 scalar=ecols[:, b, 2:3], in1=PD[:, 128:256], op0=ALU.mult, op1=ALU.add)
                            g.act(ogT[:, n * 128:(n + 1) * 128], PTf[:, 0:128], AF.Copy, r=["PT"], w=["ogT"])
                        st.append(c2)
                    return st

                prev = []
                for grp in range(5):
                    cur = prep_stages(grp) if grp < 4 else []
                    na, nb_ = len(cur), len(prev)
                    ia = ib = 0
                    while ia < na or ib < nb_:
                        if ia < na and (ib >= nb_ or ia * nb_ <= ib * na):
                            cur[ia](); ia += 1
                        else:
                            prev[ib](); ib += 1
                    prev = scan_stages(grp) if grp < 4 else []
                chk(7)
                for tt in range(4):
                    tsl = slice(tt * 512, (tt + 1) * 512)
                    g.act(sq[:, tt % 2, :], ogT[:, tsl], AF.Square, r=["ogT"], w=[("sq", tt % 2)])
                    g.mm(PD[:], ones, sq[:, tt % 2, :], r=["cst", ("sq", tt % 2)], w=["PD"])
                    g.act(rstd[:], PD[:], AF.Sqrt, r=["PD"], w=["rstd"], bias=EPS, scale=1.0 / 128)
                    g.op("dve", "reciprocal", r=["rstd"], w=["rstd"], out=rstd[:], in_=rstd[:])
                    g.op("dve", "scalar_tensor_tensor", r=["ogT", "gsm", "rstd"], w=["ogT"], out=ogT[:, tsl], in0=ogT[:, tsl],
                         scalar=gsm_t[:, 16:17], in1=rstd[:], op0=ALU.mult, op1=ALU.mult)
                    ob_ = obuf[:, tt % 2, :]
                    g.op("dve", "tensor_tensor", r=["ogT", "gz"], w=[("obuf", tt % 2)], out=ob_, in0=ogT[:, tsl], in1=gzT[:, tsl], op=ALU.mult)
                    g.dma("sp", mixT[8 + hd, :, tsl], ob_, "mo", r=[("obuf", tt % 2)], w=[("mixT", 8 + hd)])

            for hd in range(NGD):
                try:
                    gdn_head(hd)
                except _Stop:
                    break

            def resid_block(nblk_k, src, skey, T0, T, xtmp, xkey, wkc_list):
                pass

            xr = g.at("xr%d" % l, [128, 2, 1024], F32, ar)
            for kc in range(KC if DO_OUT else 0):
                g.dma("sp", hT[:, kc, :], mixT[kc], "mi", r=[("mixT", kc)], w=["hT"])

            def lin_resid(src, skey, ncin, T0, T, getw):
                for ob in range(16):
                    wl = getw(ob)
                    for th in range(T // 1024):
                        t0 = T0 + th * 1024
                        xs = xr[:, (ob + th) % 2, :]; xk = ("xr", (ob + th) % 2)
                        g.dma("sp", xs, xw[ob, :, t0:t0 + 1024], "xr", r=[("xw", ob)], w=[xk])
                        pa = (PA, PB)[(ob + th) % 2]; pk0 = ("PAB", (ob + th) % 2, 0); pk1 = ("PAB", (ob + th) % 2, 1)
                        for hf in range(2):
                            for i, (wt, wk, ki, ci) in enumerate(wl):
                                g.mm(pa[:, hf * 512:(hf + 1) * 512], wt[:, ki, :], src[:, ci, th * 1024 + hf * 512: th * 1024 + (hf + 1) * 512],
                                     start=(i == 0), stop=(i == len(wl) - 1), r=[wk, skey], w=[(pk0, pk1)[hf]])
                        g.op("dve", "tensor_tensor", r=[pk0, pk1, xk], w=[xk], out=xs, in0=xs, in1=pa[:], op=ALU.add)
                        g.dma("sp", xw[ob, :, t0:t0 + 1024], xs, "xr", r=[xk], w=[("xw", ob)])

            def getw_out(ob):
                wt, wk = ws.get()
                return [(wt, wk, kc, kc) for kc in range(KC)]
            if DO_OUT:
                lin_resid(hT, "hT", 16, 0, S, getw_out)
            if not DO_X:
                continue

            a = ar + 8192
            memn = g.at("memn%d" % l, [128, KC, NMEM], BF16, a); a += 8192
            KT = g.at("KT%d" % l, [128, 4, NMEM], BF16, a); a += 2048
            Vt = g.at("Vt%d" % l, [128, 2, 512], BF16, a); a += 2048
            xq = g.at("xq%d" % l, [128, 4, S], BF16, a); a += 16384
            xo = g.at("xo%d" % l, [128, 4, S], BF16, a); a += 16384
            sc_t = g.at("sc_t%d" % l, [128, NMEM], F32, a); a += 1024
            pb_t = g.at("pb_t%d" % l, [128, NMEM], BF16, a); a += 512
            pT_t = g.at("pT_t%d" % l, [128, 2, 128], BF16, a); a += 512
            mx = g.at("mx%d" % l, [128, 4], F32, a); a += 64
            assert a <= big_off + 122880
            for kc in range(KC):
                g.dma("sp", xt[:, kc, 0:NMEM], memT[kc], "x", w=[("xt", kc)] + XT_ALIAS)
                g.act(sq[:, kc % 2, 0:NMEM], xt[:, kc, 0:NMEM], AF.Square, r=[("xt", kc)], w=[("sq", kc % 2)])
                g.mm(PD[:, 0:NMEM], ones, sq[:, kc % 2, 0:NMEM], start=(kc == 0), stop=(kc == KC - 1), r=["cst", ("sq", kc % 2)], w=["PD"])
            g.act(rstd[:, 0:NMEM], PD[:, 0:NMEM], AF.Sqrt, r=["PD"], w=["rstd"], bias=EPS, scale=1.0 / D)
            g.op("dve", "reciprocal", r=["rstd"], w=["rstd"], out=rstd[:, 0:NMEM], in_=rstd[:, 0:NMEM])
            for kc in range(KC):
                g.op("dve", "scalar_tensor_tensor", r=[("xt", kc), "nrm", "rstd"], w=["memn"], out=memn[:, kc, :], in0=xt[:, kc, 0:NMEM],
                     scalar=gcol(l, 2, kc), in1=rstd[:, 0:NMEM], op0=ALU.mult, op1=ALU.mult)
            for j in range(4):
                wt, wk = ws.get()
                for kc in range(KC):
                    g.mm(PC[:, 0:NMEM], wt[:, kc, :], memn[:, kc, :], start=(kc == 0), stop=(kc == KC - 1), r=[wk, "memn"], w=[("PC", 0)])
                g.act(KT[:, j, :], PC[:, 0:NMEM], AF.Copy, r=[("PC", 0)], w=["KT"])
            for j in range(4):
                wt, wk = ws.get()
                for mb in range(2):
                    for kc in range(KC):
                        g.mm(PC[:, 512 + mb * 128:512 + (mb + 1) * 128], memn[:, kc, mb * 128:(mb + 1) * 128], wt[:, kc, :],
                             start=(kc == 0), stop=(kc == KC - 1), r=[wk, "memn"], w=[("PC", 1)])
                for mb in range(2):
                    g.act(Vt[:, mb, j * 128:(j + 1) * 128], PC[:, 512 + mb * 128:512 + (mb + 1) * 128], AF.Copy, r=[("PC", 1)], w=["Vt"])
            dump("memn", memn[:].rearrange("p a b -> p (a b)"), ["memn"], [128, KC * NMEM], BF16)
            dump("KT", KT[:].rearrange("p a b -> p (a b)"), ["KT"], [128, 4 * NMEM], BF16)
            dump("Vt", Vt[:].rearrange("p a b -> p (a b)"), ["Vt"], [128, 1024], BF16)
            rmsnorm(xw, "xw", lambda kc: gcol(l, 1, kc), hT, 0, S, "hT")
            for j in range(4):
                def ev(tt, pv, pk, j=j):
                    if tt % 2 == 0:
                        g.act(xq[:, j, tt * 512:(tt + 1) * 512], pv, AF.Copy, r=[pk], w=["xq"])
                    else:
                        g.op("dve", "tensor_copy", r=[pk], w=["xq"], out=xq[:, j, tt * 512:(tt + 1) * 512], in_=pv)
                proj(hT, "hT", S, ev)
            for tb in range(16):
                tbs = slice(tb * 128, (tb + 1) * 128)
                for j in range(4):
                    i2 = (tb * 4 + j) % 2
                    pz = PC[:, i2 * 512:i2 * 512 + NMEM]; pzk = ("PC", i2)
                    g.mm(pz, xq[:, j, tbs], KT[:, j, :], r=["xq", "KT"], w=[pzk])
                    g.op("dve", "reduce_max", r=[pzk], w=["mx"], out=mx[:, 0:1], in_=pz, axis=mybir.AxisListType.X)
                    g.op("dve", "tensor_scalar_mul", r=["mx"], w=["mx"], out=mx[:, 1:2], in0=mx[:, 0:1], scalar1=-scale)
                    g.op("dve", "memset", r=["sc_t"], w=["mx"], ap=mx[:, 2:3], constant=0.0)
                    g.act(sc_t[:], pz, AF.Exp, r=[pzk, "mx"], w=["sc_t"], bias=mx[:, 1:2], scale=scale, accum_out=mx[:, 2:3])
                    g.op("dve", "reciprocal", r=["mx", "sc_t"], w=["mx"], out=mx[:, 3:4], in_=mx[:, 2:3])
                    g.op("dve", "tensor_scalar", r=["sc_t", "mx"], w=["pb_t"], out=pb_t[:], in0=sc_t[:], scalar1=mx[:, 3:4], scalar2=None, op0=ALU.mult)
                    for mb in range(2):
                        g.tr(PT[:, mb * 128:(mb + 1) * 128], pb_t[:, mb * 128:(mb + 1) * 128], identb[:], r=["pb_t", "identb"], w=["PT"])
                    g.act(pT_t[:].rearrange("p a b -> p (a b)"), PT[:, 0:256], AF.Copy, r=["PT"], w=["pT_t"])
                    for mb in range(2):
                        g.mm(PD[:, j * 128:(j + 1) * 128], Vt[:, mb, j * 128:(j + 1) * 128], pT_t[:, mb, :], start=(mb == 0), stop=(mb == 1),
                             r=["Vt", "pT_t"], w=["PD"])
                for j in range(4):
                    pass
                g.op("dve", "tensor_copy", r=["PD"], w=["xo"], out=xo[:, :, tbs], in_=PD[:].rearrange("p (a b) -> p a b", a=4))
                if tb == 15:
                    dump("mx", mx[:], ["mx"], [128, 4], F32)
                    dump("sc_t", sc_t[:], ["sc_t"], [128, NMEM], F32)
                    dump("pb_t", pb_t[:], ["pb_t"], [128, NMEM], BF16)

            dump("xq", xq[:].rearrange("p a b -> p (a b)"), ["xq"], [128, 4 * S], BF16)
            dump("xo", xo[:].rearrange("p a b -> p (a b)"), ["xo"], [128, 4 * S], BF16)

            def getw_xo(ob):
                if ob % 4 == 0:
                    getw_xo.cur = ws.get()
                wt, wk = getw_xo.cur
                return [(wt, wk, (ob % 4) * 4 + kc, kc) for kc in range(4)]
            lin_resid(xo, "xo", 4, 0, S, getw_xo)
            if not DO_F:
                continue

            hF = g.at("hF%d" % l, [128, KC, 1024], BF16, big_off)
            actT = g.at("actT%d" % l, [128, 44, 1024], BF16, big_off + 32768)
            cg = g.at("cg%d" % l, [128, 2, 1024], F32, xt_off)
            cu = g.at("cu%d" % l, [128, 2, 1024], F32, xt_off + 8192)
            hc = g.at("hc%d" % l, [128, 2, 4], F32, xt_off + 16384)
            xr2 = g.at("xr2%d" % l, [128, 2, 1024], F32, xt_off + 16384 + 64)
            for tt in range(2):
                T0 = tt * 1024
                rmsnorm(xw, "xw", lambda kc: gcol(l, 3, kc), hF, T0, T0 + 1024, "hF",
                        extra_w=[("cg", 0), ("cg", 1), ("cu", 0), ("cu", 1), ("xr2", 0), ("xr2", 1)])
                for fb in range(44):
                    for gi, (cbuf, ckey, blk) in enumerate(((cg, "cg", fb), (cu, "cu", 44 + fb))):
                        wt, wk = ws.get()
                        pa = (PA, PB)[gi]; pk = ("PAB", gi, 0); pkb = ("PAB", gi, 1)
                        for hf in range(2):
                            for kc in range(KC):
                                g.mm(pa[:, hf * 512:(hf + 1) * 512], wt[:, kc, :], hF[:, kc, hf * 512:(hf + 1) * 512],
                                     start=(kc == 0), stop=(kc == KC - 1), r=[wk, "hF"], w=[(pk, pkb)[hf]])
                        c_ = cbuf[:, fb % 2, :]; ck = (ckey, fb % 2)
                        w0 = fconv_t[:, blk * 4:blk * 4 + 1]; w1 = fconv_t[:, blk * 4 + 1:blk * 4 + 2]
                        w2 = fconv_t[:, blk * 4 + 2:blk * 4 + 3]; bb = fconv_t[:, blk * 4 + 3:blk * 4 + 4]
                        g.act(c_, pa[:], AF.Identity, r=[pk, pkb, "fconv"], w=[ck], scale=w2, bias=bb)
                        g.op("dve", "scalar_tensor_tensor", r=[pk, pkb, "fconv", ck], w=[ck], out=c_[:, 1:1024], in0=pa[:, 0:1023], scalar=w1,
                             in1=c_[:, 1:1024], op0=ALU.mult, op1=ALU.add)
                        g.op("dve", "scalar_tensor_tensor", r=[pk, pkb, "fconv", ck], w=[ck], out=c_[:, 2:1024], in0=pa[:, 0:1022], scalar=w0,
                             in1=c_[:, 2:1024], op0=ALU.mult, op1=ALU.add)
                        if tt == 1:
                            hh = halo[:, blk, :]
                            g.op("dve", "scalar_tensor_tensor", r=["halo", "fconv", ck], w=[ck], out=c_[:, 0:2], in0=hh, scalar=w0,
                                 in1=c_[:, 0:2], op0=ALU.mult, op1=ALU.add)
                            g.op("dve", "scalar_tensor_tensor", r=["halo", "fconv", ck], w=[ck], out=c_[:, 0:1], in0=hh[:, 1:2], scalar=w1,
                                 in1=c_[:, 0:1], op0=ALU.mult, op1=ALU.add)
                        else:
                            g.act(halo[:, blk, :], pa[:, 1022:1024], AF.Copy, r=[pkb], w=["halo"])
                    g.act(cg[:, fb % 2, :], cg[:, fb % 2, :], AF.Silu, r=[("cg", fb % 2)], w=[("cg", fb % 2)])
                    g.op("dve", "tensor_tensor", r=[("cg", fb % 2), ("cu", fb % 2)], w=[("actT", fb)], out=actT[:, fb, :],
                         in0=cg[:, fb % 2, :], in1=cu[:, fb % 2, :], op=ALU.mult)
                if tt == 0:
                    dump("hF", hF[:].rearrange("p a b -> p (a b)"), ["hF"], [128, KC * 1024], BF16)
                    dump("actT", actT[:].rearrange("p a b -> p (a b)"), ["actT"], [128, 44 * 1024], BF16)
                for ob in range(16):
                    wl = []
                    for kg in range(4):
                        wt, wk = ws.get(hold=4)
                        wl += [(wt, wk, i, kg * 11 + i) for i in range(11)]
                    xs = xr2[:, ob % 2, :]; xk = ("xr2", ob % 2)
                    g.dma("sp", xs, xw[ob, :, T0:T0 + 1024], "xr", r=[("xw", ob)], w=[xk] + [("xt", kc_) for kc_ in range(8, 13)])
                    pa = (PA, PB)[ob % 2]; pk = ("PAB", ob % 2, 0); pkb = ("PAB", ob % 2, 1)
                    for hf in range(2):
                        for i, (wt, wk, ki, ci) in enumerate(wl):
                            g.mm(pa[:, hf * 512:(hf + 1) * 512], wt[:, ki, :], actT[:, ci, hf * 512:(hf + 1) * 512],
                                 start=(i == 0), stop=(i == 43), r=[wk, ("actT", ci)], w=[(pk, pkb)[hf]])
                    g.op("dve", "tensor_tensor", r=[pk, pkb, xk], w=[xk], out=xs, in0=xs, in1=pa[:], op=ALU.add)
                    g.dma("sp", xw[ob, :, T0:T0 + 1024], xs, "xr", r=[xk], w=[("xw", ob)])

        if final:
            fT = g.at("fT", [128, KC, 512], F32, big_off)
            for t0 in range(0, S, 512):
                for kc in range(KC):
                    g.dma("sp", xt[:, kc, :], xw[kc, :, t0:t0 + 512], "x", r=[("xw", kc)], w=[("xt", kc)] + XT_ALIAS)
                    g.act(sq[:, kc % 2, :], xt[:, kc, :], AF.Square, r=[("xt", kc)], w=[("sq", kc % 2)])
                    g.mm(PD[:], ones, sq[:, kc % 2, :], start=(kc == 0), stop=(kc == KC - 1), r=["cst", ("sq", kc % 2)], w=["PD"])
                g.act(rstd[:], PD[:], AF.Sqrt, r=["PD"], w=["rstd"], bias=EPS, scale=1.0 / D)
                g.op("dve", "reciprocal", r=["rstd"], w=["rstd"], out=rstd[:], in_=rstd[:])
                for kc in range(KC):
                    g.op("dve", "scalar_tensor_tensor", r=[("xt", kc), "nrm", "rstd"], w=[("fT", kc)], out=fT[:, kc, :], in0=xt[:, kc, :],
                         scalar=nrm_t[:, nl * 64 + kc:nl * 64 + kc + 1], in1=rstd[:], op0=ALU.mult, op1=ALU.mult)
                    g.dma("sp", yout[kc, :, t0:t0 + 512], fT[:, kc, :], "yo", r=[("fT", kc)], w=[("yout", kc)])
            sc.rec("sp", lambda e: None, reads=["yout"])
        else:
            sc.rec("sp", lambda e: None, reads=["xw", "mixT"] + dbg_keys)
        sc.emit(es)
    return nc


def _consts():
    i = np.arange(128)
    ident = np.eye(128, dtype=np.float32)
    Lincl = (i[:, None] >= i[None, :]).astype(np.float32)
    Uincl = (i[:, None] <= i[None, :]).astype(np.float32)
    Ustr = (i[:, None] > i[None, :]).astype(np.float32)
    mSL = (i[:, None] > i[None, :]).astype(np.float32)
    mUI = (i[:, None] <= i[None, :]).astype(np.float32)
    ones = np.ones((128, 128), np.float32)
    t = np.arange(512)
    md = [((128 * d + i[:, None]) < t[None, :]).astype(np.float32) for d in range(4)]
    return np.ascontiguousarray(np.concatenate([ident, Lincl, Uincl, Ustr, mSL, mUI, ones] + md, axis=1))


def _blk(w, nk):
    K, C = w.shape
    return np.ascontiguousarray(w.reshape(K // 128, 128, C // 128, 128).transpose(2, 1, 0, 3))


def _col(v):
    return np.ascontiguousarray(v.reshape(-1, 128).T)


def prep_layers(inp, ls):
    f = lambda k: np.asarray(inp[k], dtype=np.float32)
    nl = len(ls)
    out = {}
    nrm = np.zeros((128, nl * 64 + 16), np.float32)
    for i, l in enumerate(ls):
        for j, k in enumerate(("mix_norm", "xattn_norm", "mem_norm", "ffn_norm")):
            nrm[:, i * 64 + j * 16:i * 64 + (j + 1) * 16] = _col(f(k)[l])
    nrm[:, nl * 64:] = _col(f("final_norm"))
    out["nrm"] = nrm
    w_in = f("w_in")
    out["w_in"] = np.stack([_blk(w_in[l][:, :7168], 16) for l in ls])
    out["w_ba"] = np.stack([np.ascontiguousarray(w_in[l][:, 7168:7184].reshape(16, 128, 16).transpose(1, 0, 2)) for l in ls])
    gc = f("gdn_conv")
    out["gconv"] = np.stack([np.ascontiguousarray(gc[l].reshape(4, 24, 128).transpose(2, 1, 0).reshape(128, 96)) for l in ls])
    gsm = np.zeros((nl, 128, 17), np.float32)
    for i, l in enumerate(ls):
        gsm[i, :, 0:8] = f("gdn_a_log")[l][None, :]
        gsm[i, :, 8:16] = f("gdn_dt_bias")[l][None, :]
        gsm[i, :, 16] = f("gdn_norm")[l]
    out["gsm"] = gsm
    out["w_out"] = np.stack([_blk(f("w_out")[l], 16) for l in ls])
    out["w_xq"] = np.stack([_blk(f("w_xq")[l], 16) for l in ls])
    out["w_xkv"] = np.stack([_blk(f("w_xkv")[l], 16) for l in ls])
    wxo = f("w_xo")
    out["w_xo"] = np.stack([np.ascontiguousarray(
        wxo[l].reshape(4, 128, 4, 4, 128).transpose(2, 1, 3, 0, 4).reshape(4, 128, 16, 128)) for l in ls])
    out["w_up"] = np.stack([_blk(f("w_up")[l], 16) for l in ls])
    fc = f("ffn_conv"); fb_ = f("ffn_conv_bias")
    fconv = np.zeros((nl, 128, NFB, 4), np.float32)
    for i, l in enumerate(ls):
        fconv[i, :, :, 0:3] = fc[l].reshape(3, NFB, 128).transpose(2, 1, 0)
        fconv[i, :, :, 3] = fb_[l].reshape(NFB, 128).T
    out["fconv"] = fconv.reshape(nl, 128, NFB * 4)
    wd = f("w_down")
    out["w_dn"] = np.stack([np.ascontiguousarray(
        wd[l].reshape(4, 11, 128, 16, 128).transpose(3, 0, 2, 1, 4)) for l in ls])
    out["cst"] = _consts()
    return out


_NC_CACHE = {}


def _get_nc(nl, final, dbg=False):
    key = (nl, final, dbg)
    if key not in _NC_CACHE:
        _NC_CACHE[key] = build(nl, final, dbg)
    return _NC_CACHE[key]


NCORES = 8
FUSED = True


def kernel(**inputs):
    x = np.asarray(inputs["x"], dtype=np.float32)
    mem = np.asarray(inputs["mem"], dtype=np.float32)
    B = x.shape[0]
    xT = [np.ascontiguousarray(x[b].T.reshape(KC, 128, S)) for b in range(B)]
    mT = [np.ascontiguousarray(mem[b].T.reshape(KC, 128, NMEM)) for b in range(B)]
    groups = [[0, 1, 2, 3]] if FUSED else [[0], [1], [2], [3]]
    cur = xT
    for gi, ls in enumerate(groups):
        final = gi == len(groups) - 1
        wts = prep_layers(inputs, ls)
        nc = _get_nc(len(ls), final)
        in_maps = []
        for c in range(NCORES):
            m = dict(wts)
            m["xin"] = cur[c % B]
            m["memT"] = mT[c % B]
            in_maps.append(m)
        res = run_bass_kernel_spmd(nc, in_maps, core_ids=list(range(NCORES)))
        key = "yout" if final else "xw"
        cur = [np.asarray(res.results[b][key]) for b in range(B)]
    out = np.stack([np.ascontiguousarray(cur[b].reshape(D, S).T) for b in range(B)])
    return out.astype(np.float32)
```

```python
import numpy as np
from contextlib import ExitStack
import concourse.bass as bass
import concourse.mybir as mybir
from concourse.bass_utils import run_bass_kernel_spmd

F32 = mybir.dt.float32
BF16 = mybir.dt.bfloat16
AF = mybir.ActivationFunctionType
ALU = mybir.AluOpType

S = 2048
D = 2048
KC = 16
NMEM = 256
DFF = 5632
NFB = 88
EPS = 1e-6
ENGS = ("pe", "act", "dve", "pool", "sp")


class Sched:
    def __init__(self, nc):
        self.nc = nc
        self.ops = {e: [] for e in ENGS}
        self.state = {}
        self.subs = {}
        self.seen = {e: {} for e in ENGS}
        self.dma_count = {}

    @staticmethod
    def _norm(k):
        if isinstance(k, tuple):
            return (k[0], k[1:] if len(k) > 1 else None)
        return (k, None)

    def _conf(self, base, sub):
        if sub is None:
            return [(base, None)] + [(base, s) for s in self.subs.get(base, ())]
        return [(base, sub), (base, None)]

    def rec(self, eng, fn, reads=(), writes=(), dma_sem=None):
        deps = {}
        if eng != "pe":
            pr = [k for k in reads if (k[0] if isinstance(k, tuple) else k) in ("PAB", "PC", "PD", "PT")]
            if pr:
                reads = [k for k in reads if k not in pr]
                writes = list(writes) + pr

        def add(pid, raw=False):
            if pid is None:
                return
            stream, idx = pid
            if stream == eng and (not raw or eng in ("pe", "sp")):
                return
            if deps.get(stream, -1) < idx:
                deps[stream] = idx

        rk = [self._norm(k) for k in reads]
        wk = [self._norm(k) for k in writes]
        for base, sub in rk:
            for ck in self._conf(base, sub):
                st = self.state.get(ck)
                if st is not None:
                    add(st[0], True)
        for base, sub in wk:
            for ck in self._conf(base, sub):
                st = self.state.get(ck)
                if st is not None:
                    add(st[0])
                    for s_, i_ in st[1].items():
                        add((s_, i_))
        waits = []
        seen = self.seen[eng]
        for stream, idx in deps.items():
            if isinstance(stream, tuple):
                cnt = self.dma_count[stream[1]]
                if seen.get(stream, -1) >= cnt:
                    continue
                seen[stream] = cnt
                waits.append((stream, cnt))
            else:
                if seen.get(stream, -1) >= idx:
                    continue
                seen[stream] = idx
                self.ops[stream][idx]["inc"] = True
                waits.append((stream, idx))
        if dma_sem is not None:
            c = self.dma_count.get(dma_sem, 0) + 1
            self.dma_count[dma_sem] = c
            pid = (("d", dma_sem), c)
        else:
            pid = (eng, len(self.ops[eng]))
        self.ops[eng].append({"fn": fn, "waits": waits, "inc": False, "dma": dma_sem})
        for base, sub in rk:
            st = self.state.setdefault((base, sub), [None, {}])
            if sub is not None:
                self.subs.setdefault(base, set()).add(sub)
            if st[1].get(pid[0], -1) < pid[1]:
                st[1][pid[0]] = pid[1]
        for base, sub in wk:
            if sub is None:
                for s in self.subs.get(base, ()):
                    self.state.pop((base, s), None)
                self.subs[base] = set()
            else:
                self.subs.setdefault(base, set()).add(sub)
            self.state[(base, sub)] = [pid, {}]

    def emit(self, es):
        nc = self.nc
        sems = {}
        for e in ("pe", "act", "dve", "pool"):
            sems[e] = es.enter_context(nc.semaphore("s_" + e))
        for name in self.dma_count:
            sems[("d", name)] = es.enter_context(nc.semaphore("d_" + name))
        cum = {}
        for e in ENGS:
            c = 0
            arr = []
            for op in self.ops[e]:
                if op["inc"]:
                    c += 1
                arr.append(c)
            cum[e] = arr
        block = es.enter_context(nc.Block())
        ops = self.ops

        def run(e, engh):
            for op in ops[e]:
                for stream, v in op["waits"]:
                    if isinstance(stream, tuple):
                        engh.wait_ge(sems[stream], 16 * v)
                    else:
                        engh.wait_ge(sems[stream], cum[stream][v])
                ins = op["fn"](engh)
                if ins is None:
                    continue
                if op["dma"] is not None:
                    ins.then_inc(sems[("d", op["dma"])], 16)
                elif op["inc"]:
                    ins.then_inc(sems[e], 1)

        @block.tensor
        def _(t):
            run("pe", t)

        @block.scalar
        def _(t):
            run("act", t)

        @block.vector
        def _(t):
            run("dve", t)

        @block.gpsimd
        def _(t):
            run("pool", t)

        @block.sync
        def _(t):
            run("sp", t)


class Gen:
    def __init__(self, nc, es):
        self.nc = nc
        self.es = es
        self.sc = Sched(nc)
        slab = nc.alloc_sbuf_tensor("slab", [128, 207000], mybir.dt.uint8)
        self.base = nc.lookup_mloc(slab).addr
        self.nps = 0

    def at(self, name, shape, dt, off):
        return self.nc.alloc_sbuf_tensor_at(name, shape, dt, offset=self.base + off)

    def mm(self, out, lhsT, rhs, start=True, stop=True, r=(), w=()):
        self.sc.rec("pe", lambda e: e.matmul(out, lhsT, rhs, start=start, stop=stop), reads=r, writes=w)

    def tr(self, out, in_, ident, r=(), w=()):
        self.sc.rec("pe", lambda e: e.transpose(out, in_, ident), reads=r, writes=w)

    def op(self, eng, method, r=(), w=(), **kw):
        self.sc.rec(eng, lambda e: getattr(e, method)(**kw), reads=r, writes=w)

    def act(self, out, in_, func, r=(), w=(), **kw):
        self.sc.rec("act", lambda e: e.activation(out=out, in_=in_, func=func, **kw), reads=r, writes=w)

    def dma(self, eng, out, in_, sem, r=(), w=()):
        self.sc.rec(eng, lambda e: e.dma_start(out=out, in_=in_), reads=r, writes=w, dma_sem=sem)


class WStream:
    def __init__(self, g, tiles):
        self.g = g
        self.tiles = tiles
        self.plan = []
        self.loaded = 0
        self.used = 0

    def add(self, ap, nk=16):
        self.plan.append((ap, nk))

    def get(self, hold=1):
        n = len(self.tiles)
        while self.loaded < len(self.plan) and self.loaded < self.used + n - (hold - 1):
            ap, nk = self.plan[self.loaded]
            i = self.loaded % n
            self.g.dma("pool", self.tiles[i][:, 0:nk, :], ap, "w%d" % i, w=[("wb", i)])
            self.loaded += 1
        i = self.used % n
        self.used += 1
        return self.tiles[i], ("wb", i)


def build(nl, final, dbg=False, cfg=None):
    cfg = cfg or {}
    nc = bass.Bass("TRN2", target_bir_lowering=False)
    dr = lambda name, shape, dt, kind="ExternalInput": nc.dram_tensor(name, shape, dt, kind=kind).ap()
    xin = dr("xin", [KC, 128, S], F32)
    memT = dr("memT", [KC, 128, NMEM], F32)
    cst = dr("cst", [128, 7 * 128 + 4 * 512], F32)
    nrm = dr("nrm", [128, nl * 64 + 16], F32)
    w_in = dr("w_in", [nl, 56, 128, 16, 128], F32)
    w_ba = dr("w_ba", [nl, 128, 16, 16], F32)
    gconv = dr("gconv", [nl, 128, 24 * 4], F32)
    gsm = dr("gsm", [nl, 128, 17], F32)
    w_out = dr("w_out", [nl, 16, 128, 16, 128], F32)
    w_xq = dr("w_xq", [nl, 4, 128, 16, 128], F32)
    w_xkv = dr("w_xkv", [nl, 8, 128, 16, 128], F32)
    w_xo = dr("w_xo", [nl, 4, 128, 16, 128], F32)
    w_up = dr("w_up", [nl, NFB, 128, 16, 128], F32)
    fconv = dr("fconv", [nl, 128, NFB * 4], F32)
    w_dn = dr("w_dn", [nl, 16, 4, 128, 11, 128], F32)
    okind = "ExternalOutput"
    xw = dr("xw", [KC, 128, S], F32, kind=okind if not final else "Internal")
    yout = dr("yout", [KC, 128, S], F32, kind=okind) if final else None
    mixT = dr("mixT", [KC, 128, S], BF16, kind=okind if dbg else "Internal")
    dbg_keys = []

    def dump(name, ap, keys, shape, dt):
        if not dbg or not cfg.get("dump"):
            return
        if name not in cfg["dump"]:
            return
        t = dr("D_" + name, shape, dt, kind=okind)
        g.dma("sp", t, ap, "dbg", r=keys, w=["D_" + name])
        dbg_keys.append("D_" + name)

    NSB = cfg.get("sb", 8); NGD = cfg.get("gdn", 8); DO_OUT = cfg.get("out", True)
    DO_X = cfg.get("xattn", True); DO_F = cfg.get("ffn", True)
    es = ExitStack()
    with es:
        g = Gen(nc, es)
        sc = g.sc
        o = 0
        cst_t = g.at("cst_t", [128, 7 * 128 + 4 * 512], F32, o); o += (7 * 128 + 4 * 512) * 4
        ident = cst_t[:, 0:128]; Lincl = cst_t[:, 128:256]; Uincl = cst_t[:, 256:384]
        Ustr = cst_t[:, 384:512]; mSL = cst_t[:, 512:640]; mUI = cst_t[:, 640:768]; ones = cst_t[:, 768:896]
        maskd = [cst_t[:, 896 + 512 * d: 896 + 512 * (d + 1)] for d in range(4)]
        identb = g.at("identb", [128, 128], BF16, o); o += 256
        nrm_t = g.at("nrm_t", [128, nl * 64 + 16], F32, o); o += (nl * 64 + 16) * 4
        gconv_t = g.at("gconv_t", [128, 96], F32, o); o += 384
        gsm_t = g.at("gsm_t", [128, 17], F32, o); o += 68 + 28
        fconv_t = g.at("fconv_t", [128, NFB * 4], F32, o); o += NFB * 16
        halo = g.at("halo", [128, NFB, 2], F32, o); o += NFB * 8
        rstd = g.at("rstd", [128, 512], F32, o); o += 2048
        sq = g.at("sq", [128, 2, 512], F32, o); o += 4096
        sqb = g.at("sqb", [128, 2, 512], BF16, o); o += 2048
        onesb = g.at("onesb", [128, 128], BF16, o); o += 256
        NWB = 6
        wbt = g.at("wbt", [128, NWB, 16, 128], BF16, o); o += NWB * 4096
        wtiles = [wbt[:, i] for i in range(NWB)]
        xt_off = o
        xt = g.at("xt", [128, KC, 512], F32, o); o += 32768
        big_off = o
        hT = g.at("hT", [128, KC, S], BF16, o)
        ar = o + 65536
        o += 122880
        assert o <= 207000, o
        ws = WStream(g, wtiles)

        PA = es.enter_context(nc.psum_tensor("PA", [128, 1024], F32))
        PB = es.enter_context(nc.psum_tensor("PB", [128, 1024], F32))
        PC = es.enter_context(nc.psum_tensor("PC", [128, 1024], F32))
        PD = es.enter_context(nc.psum_tensor("PD", [128, 512], F32))
        PTf = es.enter_context(nc.psum_tensor("PT", [128, 512], F32))
        PT = PTf.bitcast(BF16)

        g.dma("sp", cst_t[:], cst, "c", w=["cst"])
        g.dma("sp", nrm_t[:], nrm, "c", w=["nrm"])
        g.op("dve", "tensor_copy", r=["cst"], w=["identb"], out=identb[:], in_=ident)
        g.op("dve", "tensor_copy", r=["cst"], w=["onesb"], out=onesb[:], in_=ones)
        for kc in range(KC):
            g.dma("sp", xw[kc], xin[kc], "xcp", w=[("xw", kc)])

        def gcol(l, which, kc):
            c = l * 64 + which * 16 + kc
            return nrm_t[:, c:c + 1]

        XT_ALIAS = [("cg", 0), ("cg", 1), ("cu", 0), ("cu", 1), ("xr2", 0), ("xr2", 1), "gq", "gk", "gv", "gz"]

        def rmsnorm(src, srckey, gc, dst, T0, T1, dkey, extra_w=()):
            for t0 in range(T0, T1, 512):
                for kc in range(KC):
                    g.dma("sp", xt[:, kc, :], src[kc, :, t0:t0 + 512], "x", r=[(srckey, kc)], w=[("xt", kc)] + XT_ALIAS)
                    g.act(sqb[:, kc % 2, :], xt[:, kc, :], AF.Square, r=[("xt", kc)], w=[("sqb", kc % 2)])
                    g.mm(PD[:], onesb[:], sqb[:, kc % 2, :], start=(kc == 0), stop=(kc == KC - 1),
                         r=["onesb", ("sqb", kc % 2)], w=["PD"])
                g.act(rstd[:], PD[:], AF.Ln, r=["PD"], w=["rstd"], bias=EPS, scale=1.0 / D)
                g.act(rstd[:], rstd[:], AF.Exp, r=["rstd"], w=["rstd"], scale=-0.5)
                for kc in range(KC):
                    g.op("dve", "scalar_tensor_tensor", r=[("xt", kc), "nrm", "rstd"], w=[(dkey, kc, t0)],
                         out=dst[:, kc, t0 - T0:t0 - T0 + 512], in0=xt[:, kc, :], scalar=gc(kc), in1=rstd[:],
                         op0=ALU.mult, op1=ALU.mult)

        def proj(src, skey, T, evac):
            wt, wk = ws.get()
            for tt in range(T // 512):
                pa = (PA, PB)[tt % 2]
                half = (tt // 2) % 2
                pv = pa[:, half * 512:(half + 1) * 512]
                pk = ("PAB", tt % 2, half)
                for kc in range(KC):
                    g.mm(pv, wt[:, kc, :], src[:, kc, tt * 512:(tt + 1) * 512], start=(kc == 0), stop=(kc == KC - 1),
                         r=[wk, skey], w=[pk])
                evac(tt, pv, pk)

        def residual_linear(wplan_nk, nkc, src, skey, T0, T):
            raise NotImplementedError

        for l in range(nl):
            g.dma("sp", gconv_t[:], gconv[l], "c", w=["gconv"])
            g.dma("sp", gsm_t[:], gsm[l], "c", w=["gsm"])
            g.dma("sp", fconv_t[:], fconv[l], "c", w=["fconv"])
            for hd in range(NSB):
                for j in range(3):
                    ws.add(w_in[l, j * 8 + hd])
            for hd in range(NGD):
                for j in range(4):
                    ws.add(w_in[l, 24 + j * 8 + hd])
            for ob in range(16 if DO_OUT else 0):
                ws.add(w_out[l, ob])
            for j in range(8 if DO_X else 0):
                ws.add(w_xkv[l, j])
            for j in range(4 if DO_X else 0):
                ws.add(w_xq[l, j])
            for j in range(4 if DO_X else 0):
                ws.add(w_xo[l, j])
            for tt in range(2 if DO_F else 0):
                for fb in range(44):
                    ws.add(w_up[l, fb]); ws.add(w_up[l, 44 + fb])
                for ob in range(16):
                    for kg in range(4):
                        ws.add(w_dn[l, ob, kg], 11)

            rmsnorm(xw, "xw", lambda kc: gcol(l, 0, kc), hT, 0, S, "hT")

            a = ar
            wba_t = g.at("wba_t%d" % l, [128, 16, 16], BF16, a); a += 512
            bg = g.at("bg%d" % l, [128, 16, 16], F32, a); a += 1024
            beta_t = g.at("beta%d" % l, [128, 16, 8], F32, a); a += 512
            nbeta_t = g.at("nbeta%d" % l, [128, 16, 8], F32, a); a += 512
            g_t = g.at("g_t%d" % l, [128, 16, 8], F32, a); a += 512
            nA = g.at("nA%d" % l, [128, 8], F32, a); a += 32
            gates_end = a
            g.dma("pool", wba_t[:], w_ba[l], "wba", w=["wba", "actT"])
            for blk in range(16):
                for kc in range(KC):
                    g.mm(PD[:, blk * 16:(blk + 1) * 16], hT[:, kc, blk * 128:(blk + 1) * 128], wba_t[:, kc, :],
                         start=(kc == 0), stop=(kc == KC - 1), r=["hT", "wba"], w=["PD"])
            g.op("dve", "tensor_copy", r=["PD"], w=["bg"], out=bg[:].rearrange("p a b -> p (a b)"), in_=PD[:, 0:256])
            g.act(beta_t[:], bg[:, :, 0:8], AF.Exp, r=["bg"], w=["beta"], scale=-1.0)
            g.op("dve", "tensor_scalar_add", r=["beta"], w=["beta"], out=beta_t[:], in0=beta_t[:], scalar1=1.0)
            g.op("dve", "reciprocal", r=["beta"], w=["beta"], out=beta_t[:], in_=beta_t[:])
            g.op("dve", "tensor_scalar_mul", r=["beta"], w=["nbeta"], out=nbeta_t[:], in0=beta_t[:], scalar1=-1.0)
            g.act(nA[:], gsm_t[:, 0:8], AF.Exp, r=["gsm"], w=["nA"])
            g.op("dve", "tensor_scalar_mul", r=["nA"], w=["nA"], out=nA[:], in0=nA[:], scalar1=-1.0)
            for blk in range(16):
                g.op("dve", "tensor_tensor", r=["bg", "gsm"], w=["g_t"], out=g_t[:, blk, :], in0=bg[:, blk, 8:16],
                     in1=gsm_t[:, 8:16], op=ALU.add)
            g.act(g_t[:], g_t[:], AF.Exp, r=["g_t"], w=["g_t"])
            g.act(g_t[:], g_t[:], AF.Ln, r=["g_t"], w=["g_t"], bias=1.0)
            for blk in range(16):
                g.op("dve", "tensor_tensor", r=["g_t", "nA"], w=["g_t"], out=g_t[:, blk, :], in0=g_t[:, blk, :],
                     in1=nA[:], op=ALU.mult)

            a = gates_end
            a = (a + 63) // 64 * 64
            qT = g.at("qT%d" % l, [128, S], BF16, a); a += 4096
            kT = g.at("kT%d" % l, [128, S], BF16, a); a += 4096
            vT = g.at("vT%d" % l, [128, S], BF16, a); a += 4096
            vtok = g.at("vtok%d" % l, [128, 16, 128], BF16, a); a += 4096
            ebuf = g.at("ebuf%d" % l, [128, 3, 512], F32, a); a += 6144
            spbuf = g.at("spbuf%d" % l, [128, 3, 512], F32, a); a += 6144
            ecbuf = g.at("ecbuf%d" % l, [128, 2, 512], F32, a); a += 4096
            racc = g.at("racc%d" % l, [128, 512], F32, a); a += 2048
            wbuf = g.at("wbuf%d" % l, [128, 2, 512], BF16, a); a += 2048
            obuf = g.at("obuf%d" % l, [128, 2, 512], BF16, a); a += 2048
            assert a <= big_off + 122880
            scale = 128.0 ** -0.5
            for hd in range(NSB):
                for j, dst, key in ((0, qT, "qT"), (1, kT, "kT"), (2, vT, "vT")):
                    def ev(tt, pv, pk, dst=dst, key=key):
                        if tt % 2 == 0:
                            g.act(dst[:, tt * 512:(tt + 1) * 512], pv, AF.Copy, r=[pk], w=[(key, tt)])
                        else:
                            g.op("dve", "tensor_copy", r=[pk], w=[(key, tt)], out=dst[:, tt * 512:(tt + 1) * 512], in_=pv)
                    proj(hT, "hT", S, ev)
                for half in range(2):
                    for b in range(8):
                        blk = half * 8 + b
                        g.tr(PT[:, b * 128:(b + 1) * 128], vT[:, blk * 128:(blk + 1) * 128], identb[:],
                             r=[("vT", blk // 4), "identb"], w=["PT"])
                    g.op("dve", "tensor_copy", r=["PT"], w=[("vtok", half)],
                         out=vtok[:, half * 8:(half + 1) * 8, :].rearrange("p a b -> p (a b)"), in_=PT[:])
                its = []
                for qt in range(4):
                    nkb = 4 * qt + 4
                    for it, kb in enumerate(range(nkb - 1, -1, -1)):
                        its.append((qt, it, kb))

                def stA(n):
                    qt, it, kb = its[n]
                    i2 = n % 2; i3 = n % 3
                    pz = PC[:, i2 * 512:(i2 + 1) * 512]; pzk = ("PC", i2)
                    g.mm(pz, kT[:, kb * 128:(kb + 1) * 128], qT[:, qt * 512:(qt + 1) * 512], r=[("kT", kb // 4), ("qT", qt)], w=[pzk])
                    e_ = ebuf[:, i3, :]; sp_ = spbuf[:, i3, :]
                    g.act(e_, pz, AF.Exp, r=[pzk], w=[("e", i3)], scale=scale)
                    g.act(sp_, e_, AF.Ln, r=[("e", i3)], w=[("sp", i3)], bias=1.0)
                    if kb >= 4 * qt:
                        md = maskd[kb - 4 * qt]
                        g.op("dve", "tensor_tensor", r=[("sp", i3), "cst"], w=[("sp", i3)], out=sp_, in0=sp_, in1=md, op=ALU.mult)
                        g.op("dve", "tensor_tensor", r=[("e", i3), "cst"], w=[("e", i3)], out=e_, in0=e_, in1=md, op=ALU.mult)

                def stB(n):
                    qt, it, kb = its[n]
                    i2 = n % 2; i3 = n % 3
                    e_ = ebuf[:, i3, :]; sp_ = spbuf[:, i3, :]; ec_ = ecbuf[:, i2, :]
                    pcs = (PA, PB)[i2][:, 512:1024]; pck = ("PAB", i2, 1)
                    g.mm(pcs, Lincl, sp_, start=True, stop=(it == 0), r=["cst", ("sp", i3)], w=[pck])
                    if it > 0:
                        g.mm(pcs, ones, racc[:], start=False, stop=True, r=["cst", "racc"], w=[pck])
                    g.act(ec_, pcs, AF.Exp, r=[pck], w=[("ec", i2)], scale=-1.0)
                    g.op("dve", "tensor_tensor", r=[("e", i3), ("ec", i2)], w=[("wbuf", i2)], out=wbuf[:, i2, :], in0=e_, in1=ec_, op=ALU.mult)
                    if it == 0:
                        g.op("dve", "tensor_copy", r=[("sp", i3)], w=["racc"], out=racc[:], in_=sp_)
                    elif kb > 0:
                        g.op("dve", "tensor_tensor", r=[("sp", i3), "racc"], w=["racc"], out=racc[:], in0=racc[:], in1=sp_, op=ALU.add)

                def stC(n):
                    qt, it, kb = its[n]
                    i2 = n % 2
                    g.mm(PD[:], vtok[:, kb, :], wbuf[:, i2, :], start=(it == 0), stop=(kb == 0),
                         r=[("vtok", kb // 8), ("wbuf", i2)], w=["PD"])
                    if kb == 0:
                        ob_ = obuf[:, qt % 2, :]
                        g.act(ob_, PD[:], AF.Copy, r=["PD"], w=[("obuf", qt % 2)])
                        g.dma("sp", mixT[hd, :, qt * 512:(qt + 1) * 512], ob_, "mo", r=[("obuf", qt % 2)], w=[("mixT", hd)])

                NI = len(its)
                for n in range(NI + 2):
                    if n < NI:
                        stA(n)
                    if 0 <= n - 1 < NI:
                        stB(n - 1)
                    if 0 <= n - 2 < NI:
                        stC(n - 2)

            a = gates_end
            a = (a + 63) // 64 * 64
            gates_end_al = a
            gq = xt[:].rearrange("p a b -> p (a b)")
            gqT = gq[:, 0:S]; gkT = gq[:, S:2 * S]; gvT = gq[:, 2 * S:3 * S]; gzT = gq[:, 3 * S:4 * S]
            cb = g.at("cb%d" % l, [128, S + 8], F32, a); a += (S + 8) * 4
            ogT = g.at("ogT%d" % l, [128, S], F32, a); a += S * 4
            wide = {}
            for nm in ("Gm", "DecL", "DecT", "egcb", "Qa", "Qb", "Pa", "Pb", "X", "vb", "kbg", "ktail", "u", "wT", "ATm", "qdT"):
                wide[nm] = g.at(nm + str(l), [128, 4, 128], F32, a); a += 2048
            Sst = g.at("Sst%d" % l, [128, 2, 128], F32, a); a += 1024
            vnew = g.at("vnew%d" % l, [128, 128], F32, a); a += 512
            ecols = g.at("ecols%d" % l, [128, 4, 4], F32, a); a += 64
            bcol2 = g.at("bcol2%d" % l, [128, 4], F32, a); a += 64
            ecols1 = g.at("ecols1_%d" % l, [128, 4, 4], F32, a); a += 64
            ktail1 = g.at("ktail1_%d" % l, [128, 4, 128], F32, a); a += 2048
            assert a <= big_off + 122880, a
            cb_off = gates_end_al
            scan_in = {0: {"u": wide["u"], "wT": wide["wT"], "ATm": wide["ATm"], "qdT": wide["qdT"], "ktail": wide["ktail"], "ecols": ecols},
                       1: {"u": g.at("u1_%d" % l, [128, 4, 128], F32, cb_off), "wT": g.at("wT1_%d" % l, [128, 4, 128], F32, cb_off + 2048),
                           "ATm": g.at("ATm1_%d" % l, [128, 4, 128], F32, cb_off + 4096), "qdT": g.at("qdT1_%d" % l, [128, 4, 128], F32, cb_off + 6144),
                           "ktail": ktail1, "ecols": ecols1}}
            CB_ALIAS = ["u1", "wT1", "ATm1", "qdT1"]
            Gm, DecL, DecT, egcb = wide["Gm"], wide["DecL"], wide["DecT"], wide["egcb"]
            X, vb, kbg, ktail, u_, wT_, ATm, qdT = (wide[n] for n in ("X", "vb", "kbg", "ktail", "u", "wT", "ATm", "qdT"))
            fl = lambda t: t[:].rearrange("p a b -> p (a b)")
            class _Stop(Exception):
                pass

            def chk(stage):
                if cfg.get("gstop") == stage:
                    raise _Stop()

            def gdn_head(hd):
                for j, dst, key in ((0, gqT, "gq"), (1, gkT, "gk"), (2, gvT, "gv"), (3, gzT, "gz")):
                    if j < 3:
                        g.op("dve", "memset", w=["cb"] + CB_ALIAS, ap=cb[:, 0:3], constant=0.0)
                        def ev(tt, pv, pk):
                            if tt % 2 == 0:
                                g.act(cb[:, 3 + tt * 512:3 + (tt + 1) * 512], pv, AF.Copy, r=[pk], w=["cb"] + CB_ALIAS)
                            else:
                                g.op("dve", "tensor_copy", r=[pk], w=["cb"] + CB_ALIAS, out=cb[:, 3 + tt * 512:3 + (tt + 1) * 512], in_=pv)
                        proj(hT, "hT", S, ev)
                        fb = j * 8 + hd
                        wc = lambda i: gconv_t[:, fb * 4 + i:fb * 4 + i + 1]
                        g.act(dst, cb[:, 3:3 + S], AF.Identity, r=["cb", "gconv"], w=[key], scale=wc(3))
                        for i in range(3):
                            g.op("dve", "scalar_tensor_tensor", r=["cb", "gconv", key], w=[key], out=dst, in0=cb[:, i:i + S],
                                 scalar=wc(i), in1=dst, op0=ALU.mult, op1=ALU.add)
                        g.act(dst, dst, AF.Silu, r=[key], w=[key])
                    else:
                        def ev(tt, pv, pk):
                            g.act(gzT[:, tt * 512:(tt + 1) * 512], pv, AF.Silu, r=[pk], w=["gz"])
                        proj(hT, "hT", S, ev)
                chk(1)
                for dst, key, sc_ in ((gqT, "gq", 128.0 ** -0.5), (gkT, "gk", 1.0)):
                    for tt in range(4):
                        tsl = slice(tt * 512, (tt + 1) * 512)
                        g.act(sq[:, tt % 2, :], dst[:, tsl], AF.Square, r=[key], w=[("sq", tt % 2)])
                        g.mm(PD[:], ones, sq[:, tt % 2, :], r=["cst", ("sq", tt % 2)], w=["PD"])
                        g.act(rstd[:], PD[:], AF.Ln, r=["PD"], w=["rstd"], bias=EPS, scale=1.0)
                        g.act(rstd[:], rstd[:], AF.Exp, r=["rstd"], w=["rstd"], scale=-0.5)
                        g.op("dve", "scalar_tensor_tensor", r=[key, "rstd"], w=[key], out=dst[:, tsl], in0=dst[:, tsl],
                             scalar=sc_, in1=rstd[:], op0=ALU.mult, op1=ALU.mult)
                chk(2)
                g.op("dve", "memset", w=[("Sst", 0)], ap=Sst[:, 0, :], constant=0.0)

                def prep_stages(grp):
                    par = grp % 2
                    blks = [grp * 4 + b for b in range(4)]
                    si = scan_in[par]
                    u_, wT_, ATm, qdT, ktail, ecols = si["u"], si["wT"], si["ATm"], si["qdT"], si["ktail"], si["ecols"]
                    ku, kw, ka, kq, kk, ke = ("u%d" % par, "wT%d" % par, "ATm%d" % par, "qdT%d" % par, "ktail%d" % par, "ecols%d" % par)
                    st = []

                    def s1():
                        for b, n in enumerate(blks):
                            g.op("dve", "tensor_scalar", r=["cst", "g_t"], w=["Gm"], out=Gm[:, b, :], in0=Uincl,
                                 scalar1=g_t[:, n, hd:hd + 1], scalar2=None, op0=ALU.mult)
                    st.append(s1)

                    def s2():
                        for b, n in enumerate(blks):
                            g.mm(PA[:, b * 128:(b + 1) * 128], Gm[:, b, :], Ustr, r=["Gm", "cst"], w=[("PAB", 0, 0)])
                            g.mm(PA[:, 512 + b * 128:512 + (b + 1) * 128], Ustr, Gm[:, b, :], r=["Gm", "cst"], w=[("PAB", 0, 1)])
                            g.mm(PB[:, b * 128:(b + 1) * 128], ones, Gm[:, b, :], r=["Gm", "cst"], w=[("PAB", 1, 0)])
                            g.mm(PC[:, b * 4:b * 4 + 1], Gm[:, b, :], ones[:, 0:1], r=["Gm", "cst"], w=[("PC", 0)])
                            g.mm(PC[:, b * 4 + 1:b * 4 + 2], Ustr, g_t[:, n, hd:hd + 1], r=["g_t", "cst"], w=[("PC", 0)])
                            g.mm(PC[:, b * 4 + 2:b * 4 + 3], ones, g_t[:, n, hd:hd + 1], r=["g_t", "cst"], w=[("PC", 0)])
                    st.append(s2)

                    def s3():
                        g.act(fl(DecL), PA[:, 0:512], AF.Exp, r=[("PAB", 0, 0)], w=["DecL"])
                        g.act(fl(DecT), PA[:, 512:1024], AF.Exp, r=[("PAB", 0, 1)], w=["DecT"])
                        g.act(fl(egcb), PB[:, 0:512], AF.Exp, r=[("PAB", 1, 0)], w=["egcb"])
                        g.act(ecols[:].rearrange("p a b -> p (a b)"), PC[:, 0:16], AF.Exp, r=[("PC", 0)], w=[ke])
                    st.append(s3)

                    def s4():
                        for b, n in enumerate(blks):
                            g.op("dve", "tensor_tensor", r=["DecL", "cst"], w=["DecL"], out=DecL[:, b, :], in0=DecL[:, b, :], in1=mSL, op=ALU.mult)
                            g.op("dve", "tensor_tensor", r=["DecT", "cst"], w=["DecT"], out=DecT[:, b, :], in0=DecT[:, b, :], in1=mUI, op=ALU.mult)
                            g.op("dve", "tensor_tensor", r=[ke, "beta"], w=["bcol2"], out=bcol2[:, b:b + 1], in0=ecols[:, b, 0:1],
                                 in1=beta_t[:, n, hd:hd + 1], op=ALU.mult)
                        for b, n in enumerate(blks):
                            bs = slice(n * 128, (n + 1) * 128)
                            g.mm(PC[:, b * 128:(b + 1) * 128], gkT[:, bs], gkT[:, bs], r=["gk"], w=[("PC", 0)])
                    st.append(s4)

                    def s5():
                        Q = wide["Qa"]
                        for b, n in enumerate(blks):
                            g.op("dve", "scalar_tensor_tensor", r=[("PC", 0), "nbeta", "DecL"], w=["Qa"], out=Q[:, b, :],
                                 in0=PC[:, b * 128:(b + 1) * 128], scalar=nbeta_t[:, n, hd:hd + 1], in1=DecL[:, b, :],
                                 op0=ALU.mult, op1=ALU.mult)
                        for b in range(4):
                            g.mm(PC[:, 512 + b * 128:512 + (b + 1) * 128], Q[:, b, :], ident, r=["Qa", "cst"], w=[("PC", 1)])
                    st.append(s5)

                    def s6():
                        P = wide["Pa"]
                        g.act(fl(P), PC[:, 512:1024], AF.Copy, r=[("PC", 1)], w=["Pa"])
                        for b in range(4):
                            g.op("dve", "tensor_tensor", r=["Pa", "cst"], w=["X"], out=X[:, b, :], in0=P[:, b, :], in1=ident, op=ALU.add)
                    st.append(s6)
                    names = [("Qa", "Pa"), ("Qb", "Pb")]
                    for step in range(6):
                        qn, pn = names[step % 2]
                        qn2, pn2 = names[(step + 1) % 2]

                        def sa(qn=qn, pn=pn, qn2=qn2, pn2=pn2, step=step):
                            Q, P, Q2, P2 = wide[qn], wide[pn], wide[qn2], wide[pn2]
                            for b in range(4):
                                g.mm(PC[:, b * 128:(b + 1) * 128], P[:, b, :], Q[:, b, :], r=[qn, pn], w=[("PC", 0)])
                            if step < 5:
                                for b in range(4):
                                    g.mm(PC[:, 512 + b * 128:512 + (b + 1) * 128], Q[:, b, :], P[:, b, :], r=[qn, pn], w=[("PC", 1)])
                            g.act(fl(Q2), PC[:, 0:512], AF.Copy, r=[("PC", 0)], w=[qn2])
                            if step < 5:
                                g.op("dve", "tensor_copy", r=[("PC", 1)], w=[pn2], out=fl(P2), in_=PC[:, 512:1024])
                        st.append(sa)

                        def sb_(qn2=qn2):
                            Q2 = wide[qn2]
                            for b in range(4):
                                g.mm(PB[:, 512 + b * 128:512 + (b + 1) * 128], Q2[:, b, :], X[:, b, :], r=[qn2, "X"], w=[("PAB", 1, 1)])
                            g.op("dve", "tensor_tensor", r=[("PAB", 1, 1), "X"], w=["X"], out=fl(X), in0=fl(X), in1=PB[:, 512:1024], op=ALU.add)
                        st.append(sb_)

                    def s7():
                        for b, n in enumerate(blks):
                            bs = slice(n * 128, (n + 1) * 128)
                            g.mm(PA[:, b * 128:(b + 1) * 128], gkT[:, bs], ident, r=["gk", "cst"], w=[("PAB", 0, 0)])
                            g.mm(PA[:, 512 + b * 128:512 + (b + 1) * 128], gvT[:, bs], ident, r=["gv", "cst"], w=[("PAB", 0, 1)])
                            g.mm(PB[:, b * 128:(b + 1) * 128], gkT[:, bs], gqT[:, bs], r=["gk", "gq"], w=[("PAB", 1, 0)])
                    st.append(s7)

                    def s8():
                        for b, n in enumerate(blks):
                            g.op("dve", "tensor_scalar", r=[("PAB", 0, 0), "bcol2"], w=["kbg"], out=kbg[:, b, :], in0=PA[:, b * 128:(b + 1) * 128],
                                 scalar1=bcol2[:, b:b + 1], scalar2=None, op0=ALU.mult)
                            g.op("dve", "tensor_scalar", r=[("PAB", 0, 0), ke], w=[kk], out=ktail[:, b, :], in0=PA[:, b * 128:(b + 1) * 128],
                                 scalar1=ecols[:, b, 1:2], scalar2=None, op0=ALU.mult)
                        for b, n in enumerate(blks):
                            g.act(vb[:, b, :], PA[:, 512 + b * 128:512 + (b + 1) * 128], AF.Identity, r=[("PAB", 0, 1), "beta"], w=["vb"],
                                  scale=beta_t[:, n, hd:hd + 1])
                        g.op("dve", "tensor_tensor", r=[("PAB", 1, 0), "DecT"], w=[ka], out=fl(ATm), in0=fl(DecT), in1=PB[:, 0:512], op=ALU.mult)
                        g.op("dve", "tensor_tensor", r=["gq", "egcb"], w=[kq], out=fl(qdT), in0=gqT[:, grp * 512:(grp + 1) * 512], in1=fl(egcb), op=ALU.mult)
                    st.append(s8)

                    def s9():
                        for b, n in enumerate(blks):
                            g.mm(PA[:, b * 128:(b + 1) * 128], X[:, b, :], vb[:, b, :], r=["X", "vb"], w=[("PAB", 0, 0)])
                            g.mm(PA[:, 512 + b * 128:512 + (b + 1) * 128], kbg[:, b, :], X[:, b, :], r=["X", "kbg"], w=[("PAB", 0, 1)])
                        g.act(fl(u_), PA[:, 0:512], AF.Copy, r=[("PAB", 0, 0)], w=[ku])
                        g.op("dve", "tensor_copy", r=[("PAB", 0, 1)], w=[kw], out=fl(wT_), in_=PA[:, 512:1024])
                    st.append(s9)
                    return st

                def scan_stages(grp):
                    par = grp % 2
                    blks = [grp * 4 + b for b in range(4)]
                    si = scan_in[par]
                    u_, wT_, ATm, qdT, ktail, ecols = si["u"], si["wT"], si["ATm"], si["qdT"], si["ktail"], si["ecols"]
                    ku, kw, ka, kq, kk, ke = ("u%d" % par, "wT%d" % par, "ATm%d" % par, "qdT%d" % par, "ktail%d" % par, "ecols%d" % par)
                    st = []
                    for b, n in enumerate(blks):
                        s0 = n % 2; s1_ = 1 - s0
                        Sc = Sst[:, s0, :]; Sn = Sst[:, s1_, :]

                        def c1(b=b, s0=s0, Sc=Sc):
                            g.mm(PD[:, 0:128], wT_[:, b, :], Sc, r=[kw, ("Sst", s0)], w=["PD"])
                            g.op("dve", "tensor_tensor", r=[ku, "PD"], w=["vnew"], out=vnew[:], in0=u_[:, b, :], in1=PD[:, 0:128], op=ALU.subtract)
                        st.append(c1)

                        def c2(b=b, n=n, s0=s0, s1_=s1_, Sc=Sc, Sn=Sn):
                            g.mm(PTf[:, 0:128], Sc, qdT[:, b, :], start=True, stop=False, r=[("Sst", s0), kq], w=["PT"])
                            g.mm(PTf[:, 0:128], vnew[:], ATm[:, b, :], start=False, stop=True, r=["vnew", ka], w=["PT"])
                            g.mm(PD[:, 128:256], ktail[:, b, :], vnew[:], r=[kk, "vnew"], w=["PD"])
                            g.op("dve", "scalar_tensor_tensor", r=[("Sst", s0), ke, "PD"], w=[("Sst", s1_)], out=Sn, in0=Sc,
                                 scalar=ecols[:, b, 2:3], in1=PD[:, 128:256], op0=ALU.mult, op1=ALU.add)
                            g.act(ogT[:, n * 128:(n + 1) * 128], PTf[:, 0:128], AF.Copy, r=["PT"], w=["ogT"])
                        st.append(c2)
                    return st

                prev = []
                for grp in range(5):
                    cur = prep_stages(grp) if grp < 4 else []
                    na, nb_ = len(cur), len(prev)
                    ia = ib = 0
                    while ia < na or ib < nb_:
                        if ia < na and (ib >= nb_ or ia * nb_ <= ib * na):
                            cur[ia](); ia += 1
                        else:
                            prev[ib](); ib += 1
                    prev = scan_stages(grp) if grp < 4 else []
                chk(7)
                for tt in range(4):
                    tsl = slice(tt * 512, (tt + 1) * 512)
                    g.act(sq[:, tt % 2, :], ogT[:, tsl], AF.Square, r=["ogT"], w=[("sq", tt % 2)])
                    g.mm(PD[:], ones, sq[:, tt % 2, :], r=["cst", ("sq", tt % 2)], w=["PD"])
                    g.act(rstd[:], PD[:], AF.Ln, r=["PD"], w=["rstd"], bias=EPS, scale=1.0 / 128)
                    g.act(rstd[:], rstd[:], AF.Exp, r=["rstd"], w=["rstd"], scale=-0.5)
                    g.op("dve", "scalar_tensor_tensor", r=["ogT", "gsm", "rstd"], w=["ogT"], out=ogT[:, tsl], in0=ogT[:, tsl],
                         scalar=gsm_t[:, 16:17], in1=rstd[:], op0=ALU.mult, op1=ALU.mult)
                    ob_ = obuf[:, tt % 2, :]
                    g.op("dve", "tensor_tensor", r=["ogT", "gz"], w=[("obuf", tt % 2)], out=ob_, in0=ogT[:, tsl], in1=gzT[:, tsl], op=ALU.mult)
                    g.dma("sp", mixT[8 + hd, :, tsl], ob_, "mo", r=[("obuf", tt % 2)], w=[("mixT", 8 + hd)])

            for hd in range(NGD):
                try:
                    gdn_head(hd)
                except _Stop:
                    break

            def resid_block(nblk_k, src, skey, T0, T, xtmp, xkey, wkc_list):
                pass

            xr = g.at("xr%d" % l, [128, 2, 1024], F32, ar)
            for kc in range(KC if DO_OUT else 0):
                g.dma("sp", hT[:, kc, :], mixT[kc], "mi", r=[("mixT", kc)], w=["hT"])

            def lin_resid(src, skey, ncin, T0, T, getw):
                for ob in range(16):
                    wl = getw(ob)
                    for th in range(T // 1024):
                        t0 = T0 + th * 1024
                        xs = xr[:, (ob + th) % 2, :]; xk = ("xr", (ob + th) % 2)
                        g.dma("sp", xs, xw[ob, :, t0:t0 + 1024], "xr", r=[("xw", ob)], w=[xk])
                        pa = (PA, PB)[(ob + th) % 2]; pk0 = ("PAB", (ob + th) % 2, 0); pk1 = ("PAB", (ob + th) % 2, 1)
                        for hf in range(2):
                            for i, (wt, wk, ki, ci) in enumerate(wl):
                                g.mm(pa[:, hf * 512:(hf + 1) * 512], wt[:, ki, :], src[:, ci, th * 1024 + hf * 512: th * 1024 + (hf + 1) * 512],
                                     start=(i == 0), stop=(i == len(wl) - 1), r=[wk, skey], w=[(pk0, pk1)[hf]])
                        g.op("dve", "tensor_tensor", r=[pk0, pk1, xk], w=[xk], out=xs, in0=xs, in1=pa[:], op=ALU.add)
                        g.dma("sp", xw[ob, :, t0:t0 + 1024], xs, "xr", r=[xk], w=[("xw", ob)])

            def getw_out(ob):
                wt, wk = ws.get()
                return [(wt, wk, kc, kc) for kc in range(KC)]
            if DO_OUT:
                lin_resid(hT, "hT", 16, 0, S, getw_out)
            if not DO_X:
                continue

            a = ar + 8192
            memn = g.at("memn%d" % l, [128, KC, NMEM], BF16, a); a += 8192
            KT = g.at("KT%d" % l, [128, 4, NMEM], BF16, a); a += 2048
            Vt = g.at("Vt%d" % l, [128, 2, 512], BF16, a); a += 2048
            xq = g.at("xq%d" % l, [128, 4, S], BF16, a); a += 16384
            xo = g.at("xo%d" % l, [128, 4, S], BF16, a); a += 16384
            sc_t = g.at("sc_t%d" % l, [128, NMEM], F32, a); a += 1024
            pb_t = g.at("pb_t%d" % l, [128, NMEM], BF16, a); a += 512
            pT_t = g.at("pT_t%d" % l, [128, 2, 128], BF16, a); a += 512
            mx = g.at("mx%d" % l, [128, 4], F32, a); a += 64
            assert a <= big_off + 122880
            for kc in range(KC):
                g.dma("sp", xt[:, kc, 0:NMEM], memT[kc], "x", w=[("xt", kc)] + XT_ALIAS)
                g.act(sqb[:, kc % 2, 0:NMEM], xt[:, kc, 0:NMEM], AF.Square, r=[("xt", kc)], w=[("sqb", kc % 2)])
                g.mm(PD[:, 0:NMEM], onesb[:], sqb[:, kc % 2, 0:NMEM], start=(kc == 0), stop=(kc == KC - 1), r=["onesb", ("sqb", kc % 2)], w=["PD"])
            g.act(rstd[:, 0:NMEM], PD[:, 0:NMEM], AF.Ln, r=["PD"], w=["rstd"], bias=EPS, scale=1.0 / D)
            g.act(rstd[:, 0:NMEM], rstd[:, 0:NMEM], AF.Exp, r=["rstd"], w=["rstd"], scale=-0.5)
            for kc in range(KC):
                g.op("dve", "scalar_tensor_tensor", r=[("xt", kc), "nrm", "rstd"], w=["memn"], out=memn[:, kc, :], in0=xt[:, kc, 0:NMEM],
                     scalar=gcol(l, 2, kc), in1=rstd[:, 0:NMEM], op0=ALU.mult, op1=ALU.mult)
            for j in range(4):
                wt, wk = ws.get()
                for kc in range(KC):
                    g.mm(PC[:, 0:NMEM], wt[:, kc, :], memn[:, kc, :], start=(kc == 0), stop=(kc == KC - 1), r=[wk, "memn"], w=[("PC", 0)])
                g.act(KT[:, j, :], PC[:, 0:NMEM], AF.Copy, r=[("PC", 0)], w=["KT"])
            for j in range(4):
                wt, wk = ws.get()
                for mb in range(2):
                    for kc in range(KC):
                        g.mm(PC[:, 512 + mb * 128:512 + (mb + 1) * 128], memn[:, kc, mb * 128:(mb + 1) * 128], wt[:, kc, :],
                             start=(kc == 0), stop=(kc == KC - 1), r=[wk, "memn"], w=[("PC", 1)])
                for mb in range(2):
                    g.act(Vt[:, mb, j * 128:(j + 1) * 128], PC[:, 512 + mb * 128:512 + (mb + 1) * 128], AF.Copy, r=[("PC", 1)], w=["Vt"])
            dump("memn", memn[:].rearrange("p a b -> p (a b)"), ["memn"], [128, KC * NMEM], BF16)
            dump("KT", KT[:].rearrange("p a b -> p (a b)"), ["KT"], [128, 4 * NMEM], BF16)
            dump("Vt", Vt[:].rearrange("p a b -> p (a b)"), ["Vt"], [128, 1024], BF16)
            rmsnorm(xw, "xw", lambda kc: gcol(l, 1, kc), hT, 0, S, "hT")
            for j in range(4):
                def ev(tt, pv, pk, j=j):
                    if tt % 2 == 0:
                        g.act(xq[:, j, tt * 512:(tt + 1) * 512], pv, AF.Copy, r=[pk], w=["xq"])
                    else:
                        g.op("dve", "tensor_copy", r=[pk], w=["xq"], out=xq[:, j, tt * 512:(tt + 1) * 512], in_=pv)
                proj(hT, "hT", S, ev)
            for tb in range(16):
                tbs = slice(tb * 128, (tb + 1) * 128)
                for j in range(4):
                    i2 = (tb * 4 + j) % 2
                    pz = PC[:, i2 * 512:i2 * 512 + NMEM]; pzk = ("PC", i2)
                    g.mm(pz, xq[:, j, tbs], KT[:, j, :], r=["xq", "KT"], w=[pzk])
                    g.op("dve", "reduce_max", r=[pzk], w=["mx"], out=mx[:, 0:1], in_=pz, axis=mybir.AxisListType.X)
                    g.op("dve", "tensor_scalar_mul", r=["mx"], w=["mx"], out=mx[:, 1:2], in0=mx[:, 0:1], scalar1=-scale)
                    g.op("dve", "memset", r=["sc_t"], w=["mx"], ap=mx[:, 2:3], constant=0.0)
                    g.act(sc_t[:], pz, AF.Exp, r=[pzk, "mx"], w=["sc_t"], bias=mx[:, 1:2], scale=scale, accum_out=mx[:, 2:3])
                    g.op("dve", "reciprocal", r=["mx", "sc_t"], w=["mx"], out=mx[:, 3:4], in_=mx[:, 2:3])
                    g.op("dve", "tensor_scalar", r=["sc_t", "mx"], w=["pb_t"], out=pb_t[:], in0=sc_t[:], scalar1=mx[:, 3:4], scalar2=None, op0=ALU.mult)
                    for mb in range(2):
                        g.tr(PT[:, mb * 128:(mb + 1) * 128], pb_t[:, mb * 128:(mb + 1) * 128], identb[:], r=["pb_t", "identb"], w=["PT"])
                    g.act(pT_t[:].rearrange("p a b -> p (a b)"), PT[:, 0:256], AF.Copy, r=["PT"], w=["pT_t"])
                    for mb in range(2):
                        g.mm(PD[:, j * 128:(j + 1) * 128], Vt[:, mb, j * 128:(j + 1) * 128], pT_t[:, mb, :], start=(mb == 0), stop=(mb == 1),
                             r=["Vt", "pT_t"], w=["PD"])
                for j in range(4):
                    pass
                g.op("dve", "tensor_copy", r=["PD"], w=["xo"], out=xo[:, :, tbs], in_=PD[:].rearrange("p (a b) -> p a b", a=4))
                if tb == 15:
                    dump("mx", mx[:], ["mx"], [128, 4], F32)
                    dump("sc_t", sc_t[:], ["sc_t"], [128, NMEM], F32)
                    dump("pb_t", pb_t[:], ["pb_t"], [128, NMEM], BF16)

            dump("xq", xq[:].rearrange("p a b -> p (a b)"), ["xq"], [128, 4 * S], BF16)
            dump("xo", xo[:].rearrange("p a b -> p (a b)"), ["xo"], [128, 4 * S], BF16)

            def getw_xo(ob):
                if ob % 4 == 0:
                    getw_xo.cur = ws.get()
                wt, wk = getw_xo.cur
                return [(wt, wk, (ob % 4) * 4 + kc, kc) for kc in range(4)]
            lin_resid(xo, "xo", 4, 0, S, getw_xo)
            if not DO_F:
                continue

            hF = g.at("hF%d" % l, [128, KC, 1024], BF16, big_off)
            actT = g.at("actT%d" % l, [128, 44, 1024], BF16, big_off + 32768)
            cg = g.at("cg%d" % l, [128, 2, 1024], F32, xt_off)
            cu = g.at("cu%d" % l, [128, 2, 1024], F32, xt_off + 8192)
            hc = g.at("hc%d" % l, [128, 2, 4], F32, xt_off + 16384)
            xr2 = g.at("xr2%d" % l, [128, 2, 1024], F32, xt_off + 16384 + 64)
            for tt in range(2):
                T0 = tt * 1024
                rmsnorm(xw, "xw", lambda kc: gcol(l, 3, kc), hF, T0, T0 + 1024, "hF",
                        extra_w=[("cg", 0), ("cg", 1), ("cu", 0), ("cu", 1), ("xr2", 0), ("xr2", 1)])
                for fb in range(44):
                    for gi, (cbuf, ckey, blk) in enumerate(((cg, "cg", fb), (cu, "cu", 44 + fb))):
                        wt, wk = ws.get()
                        pa = (PA, PB)[gi]; pk = ("PAB", gi, 0); pkb = ("PAB", gi, 1)
                        for hf in range(2):
                            for kc in range(KC):
                                g.mm(pa[:, hf * 512:(hf + 1) * 512], wt[:, kc, :], hF[:, kc, hf * 512:(hf + 1) * 512],
                                     start=(kc == 0), stop=(kc == KC - 1), r=[wk, "hF"], w=[(pk, pkb)[hf]])
                        c_ = cbuf[:, fb % 2, :]; ck = (ckey, fb % 2)
                        w0 = fconv_t[:, blk * 4:blk * 4 + 1]; w1 = fconv_t[:, blk * 4 + 1:blk * 4 + 2]
                        w2 = fconv_t[:, blk * 4 + 2:blk * 4 + 3]; bb = fconv_t[:, blk * 4 + 3:blk * 4 + 4]
                        g.act(c_, pa[:], AF.Identity, r=[pk, pkb, "fconv"], w=[ck], scale=w2, bias=bb)
                        g.op("dve", "scalar_tensor_tensor", r=[pk, pkb, "fconv", ck], w=[ck], out=c_[:, 1:1024], in0=pa[:, 0:1023], scalar=w1,
                             in1=c_[:, 1:1024], op0=ALU.mult, op1=ALU.add)
                        g.op("dve", "scalar_tensor_tensor", r=[pk, pkb, "fconv", ck], w=[ck], out=c_[:, 2:1024], in0=pa[:, 0:1022], scalar=w0,
                             in1=c_[:, 2:1024], op0=ALU.mult, op1=ALU.add)
                        if tt == 1:
                            hh = halo[:, blk, :]
                            g.op("dve", "scalar_tensor_tensor", r=["halo", "fconv", ck], w=[ck], out=c_[:, 0:2], in0=hh, scalar=w0,
                                 in1=c_[:, 0:2], op0=ALU.mult, op1=ALU.add)
                            g.op("dve", "scalar_tensor_tensor", r=["halo", "fconv", ck], w=[ck], out=c_[:, 0:1], in0=hh[:, 1:2], scalar=w1,
                                 in1=c_[:, 0:1], op0=ALU.mult, op1=ALU.add)
                        else:
                            g.act(halo[:, blk, :], pa[:, 1022:1024], AF.Copy, r=[pkb], w=["halo"])
                    g.act(cg[:, fb % 2, :], cg[:, fb % 2, :], AF.Silu, r=[("cg", fb % 2)], w=[("cg", fb % 2)])
                    g.op("dve", "tensor_tensor", r=[("cg", fb % 2), ("cu", fb % 2)], w=[("actT", fb)], out=actT[:, fb, :],
                         in0=cg[:, fb % 2, :], in1=cu[:, fb % 2, :], op=ALU.mult)
                if tt == 0:
                    dump("hF", hF[:].rearrange("p a b -> p (a b)"), ["hF"], [128, KC * 1024], BF16)
                    dump("actT", actT[:].rearrange("p a b -> p (a b)"), ["actT"], [128, 44 * 1024], BF16)
                for ob in range(16):
                    wl = []
                    for kg in range(4):
                        wt, wk = ws.get(hold=4)
                        wl += [(wt, wk, i, kg * 11 + i) for i in range(11)]
                    xs = xr2[:, ob % 2, :]; xk = ("xr2", ob % 2)
                    g.dma("sp", xs, xw[ob, :, T0:T0 + 1024], "xr", r=[("xw", ob)], w=[xk] + [("xt", kc_) for kc_ in range(8, 13)])
                    pa = (PA, PB)[ob % 2]; pk = ("PAB", ob % 2, 0); pkb = ("PAB", ob % 2, 1)
                    for hf in range(2):
                        for i, (wt, wk, ki, ci) in enumerate(wl):
                            g.mm(pa[:, hf * 512:(hf + 1) * 512], wt[:, ki, :], actT[:, ci, hf * 512:(hf + 1) * 512],
                                 start=(i == 0), stop=(i == 43), r=[wk, ("actT", ci)], w=[(pk, pkb)[hf]])
                    g.op("dve", "tensor_tensor", r=[pk, pkb, xk], w=[xk], out=xs, in0=xs, in1=pa[:], op=ALU.add)
                    g.dma("sp", xw[ob, :, T0:T0 + 1024], xs, "xr", r=[xk], w=[("xw", ob)])

        if final:
            fT = g.at("fT", [128, KC, 512], F32, big_off)
            for t0 in range(0, S, 512):
                for kc in range(KC):
                    g.dma("sp", xt[:, kc, :], xw[kc, :, t0:t0 + 512], "x", r=[("xw", kc)], w=[("xt", kc)] + XT_ALIAS)
                    g.act(sqb[:, kc % 2, :], xt[:, kc, :], AF.Square, r=[("xt", kc)], w=[("sqb", kc % 2)])
                    g.mm(PD[:], onesb[:], sqb[:, kc % 2, :], start=(kc == 0), stop=(kc == KC - 1), r=["onesb", ("sqb", kc % 2)], w=["PD"])
                g.act(rstd[:], PD[:], AF.Ln, r=["PD"], w=["rstd"], bias=EPS, scale=1.0 / D)
                g.act(rstd[:], rstd[:], AF.Exp, r=["rstd"], w=["rstd"], scale=-0.5)
                for kc in range(KC):
                    g.op("dve", "scalar_tensor_tensor", r=[("xt", kc), "nrm", "rstd"], w=[("fT", kc)], out=fT[:, kc, :], in0=xt[:, kc, :],
                         scalar=nrm_t[:, nl * 64 + kc:nl * 64 + kc + 1], in1=rstd[:], op0=ALU.mult, op1=ALU.mult)
                    g.dma("sp", yout[kc, :, t0:t0 + 512], fT[:, kc, :], "yo", r=[("fT", kc)], w=[("yout", kc)])
            sc.rec("sp", lambda e: None, reads=["yout"])
        else:
            sc.rec("sp", lambda e: None, reads=["xw", "mixT"] + dbg_keys)
        sc.emit(es)
    return nc


def _consts():
    i = np.arange(128)
    ident = np.eye(128, dtype=np.float32)
    Lincl = (i[:, None] >= i[None, :]).astype(np.float32)
    Uincl = (i[:, None] <= i[None, :]).astype(np.float32)
    Ustr = (i[:, None] > i[None, :]).astype(np.float32)
    mSL = (i[:, None] > i[None, :]).astype(np.float32)
    mUI = (i[:, None] <= i[None, :]).astype(np.float32)
    ones = np.ones((128, 128), np.float32)
    t = np.arange(512)
    md = [((128 * d + i[:, None]) < t[None, :]).astype(np.float32) for d in range(4)]
    return np.ascontiguousarray(np.concatenate([ident, Lincl, Uincl, Ustr, mSL, mUI, ones] + md, axis=1))


def _blk(w, nk):
    K, C = w.shape
    return np.ascontiguousarray(w.reshape(K // 128, 128, C // 128, 128).transpose(2, 1, 0, 3))


def _col(v):
    return np.ascontiguousarray(v.reshape(-1, 128).T)


def prep_layers(inp, ls):
    f = lambda k: np.asarray(inp[k], dtype=np.float32)
    nl = len(ls)
    out = {}
    nrm = np.zeros((128, nl * 64 + 16), np.float32)
    for i, l in enumerate(ls):
        for j, k in enumerate(("mix_norm", "xattn_norm", "mem_norm", "ffn_norm")):
            nrm[:, i * 64 + j * 16:i * 64 + (j + 1) * 16] = _col(f(k)[l])
    nrm[:, nl * 64:] = _col(f("final_norm"))
    out["nrm"] = nrm
    w_in = f("w_in")
    out["w_in"] = np.stack([_blk(w_in[l][:, :7168], 16) for l in ls])
    out["w_ba"] = np.stack([np.ascontiguousarray(w_in[l][:, 7168:7184].reshape(16, 128, 16).transpose(1, 0, 2)) for l in ls])
    gc = f("gdn_conv")
    out["gconv"] = np.stack([np.ascontiguousarray(gc[l].reshape(4, 24, 128).transpose(2, 1, 0).reshape(128, 96)) for l in ls])
    gsm = np.zeros((nl, 128, 17), np.float32)
    for i, l in enumerate(ls):
        gsm[i, :, 0:8] = f("gdn_a_log")[l][None, :]
        gsm[i, :, 8:16] = f("gdn_dt_bias")[l][None, :]
        gsm[i, :, 16] = f("gdn_norm")[l]
    out["gsm"] = gsm
    out["w_out"] = np.stack([_blk(f("w_out")[l], 16) for l in ls])
    out["w_xq"] = np.stack([_blk(f("w_xq")[l], 16) for l in ls])
    out["w_xkv"] = np.stack([_blk(f("w_xkv")[l], 16) for l in ls])
    wxo = f("w_xo")
    out["w_xo"] = np.stack([np.ascontiguousarray(
        wxo[l].reshape(4, 128, 4, 4, 128).transpose(2, 1, 3, 0, 4).reshape(4, 128, 16, 128)) for l in ls])
    out["w_up"] = np.stack([_blk(f("w_up")[l], 16) for l in ls])
    fc = f("ffn_conv"); fb_ = f("ffn_conv_bias")
    fconv = np.zeros((nl, 128, NFB, 4), np.float32)
    for i, l in enumerate(ls):
        fconv[i, :, :, 0:3] = fc[l].reshape(3, NFB, 128).transpose(2, 1, 0)
        fconv[i, :, :, 3] = fb_[l].reshape(NFB, 128).T
    out["fconv"] = fconv.reshape(nl, 128, NFB * 4)
    wd = f("w_down")
    out["w_dn"] = np.stack([np.ascontiguousarray(
        wd[l].reshape(4, 11, 128, 16, 128).transpose(3, 0, 2, 1, 4)) for l in ls])
    out["cst"] = _consts()
    return out


_NC_CACHE = {}


def _get_nc(nl, final, dbg=False):
    key = (nl, final, dbg)
    if key not in _NC_CACHE:
        _NC_CACHE[key] = build(nl, final, dbg)
    return _NC_CACHE[key]


NCORES = 8
FUSED = True


def kernel(**inputs):
    x = np.asarray(inputs["x"], dtype=np.float32)
    mem = np.asarray(inputs["mem"], dtype=np.float32)
    B = x.shape[0]
    xT = [np.ascontiguousarray(x[b].T.reshape(KC, 128, S)) for b in range(B)]
    mT = [np.ascontiguousarray(mem[b].T.reshape(KC, 128, NMEM)) for b in range(B)]
    groups = [[0, 1, 2, 3]] if FUSED else [[0], [1], [2], [3]]
    cur = xT
    for gi, ls in enumerate(groups):
        final = gi == len(groups) - 1
        wts = prep_layers(inputs, ls)
        nc = _get_nc(len(ls), final)
        in_maps = []
        for c in range(NCORES):
            m = dict(wts)
            m["xin"] = cur[c % B]
            m["memT"] = mT[c % B]
            in_maps.append(m)
        res = run_bass_kernel_spmd(nc, in_maps, core_ids=list(range(NCORES)))
        key = "yout" if final else "xw"
        cur = [np.asarray(res.results[b][key]) for b in range(B)]
    out = np.stack([np.ascontiguousarray(cur[b].reshape(D, S).T) for b in range(B)])
    return out.astype(np.float32)
```

```python
import numpy as np
from contextlib import ExitStack
import concourse.bass as bass
import concourse.mybir as mybir
from concourse.bass_utils import run_bass_kernel_spmd

F32 = mybir.dt.float32
BF16 = mybir.dt.bfloat16
AF = mybir.ActivationFunctionType
ALU = mybir.AluOpType

S = 2048
D = 2048
KC = 16
NMEM = 256
DFF = 5632
NFB = 88
EPS = 1e-6
ENGS = ("pe", "act", "dve", "pool", "sp")


class Sched:
    def __init__(self, nc):
        self.nc = nc
        self.ops = {e: [] for e in ENGS}
        self.state = {}
        self.subs = {}
        self.seen = {e: {} for e in ENGS}
        self.dma_count = {}

    @staticmethod
    def _norm(k):
        if isinstance(k, tuple):
            return (k[0], k[1:] if len(k) > 1 else None)
        return (k, None)

    def _conf(self, base, sub):
        if sub is None:
            return [(base, None)] + [(base, s) for s in self.subs.get(base, ())]
        return [(base, sub), (base, None)]

    def rec(self, eng, fn, reads=(), writes=(), dma_sem=None):
        deps = {}
        if eng != "pe":
            pr = [k for k in reads if (k[0] if isinstance(k, tuple) else k) in ("PAB", "PC", "PD", "PT")]
            if pr:
                reads = [k for k in reads if k not in pr]
                writes = list(writes) + pr

        def add(pid, raw=False):
            if pid is None:
                return
            stream, idx = pid
            if stream == eng and (not raw or eng in ("pe", "sp")):
                return
            if deps.get(stream, -1) < idx:
                deps[stream] = idx

        rk = [self._norm(k) for k in reads]
        wk = [self._norm(k) for k in writes]
        for base, sub in rk:
            for ck in self._conf(base, sub):
                st = self.state.get(ck)
                if st is not None:
                    add(st[0], True)
        for base, sub in wk:
            for ck in self._conf(base, sub):
                st = self.state.get(ck)
                if st is not None:
                    add(st[0])
                    for s_, i_ in st[1].items():
                        add((s_, i_))
        waits = []
        seen = self.seen[eng]
        for stream, idx in deps.items():
            if isinstance(stream, tuple):
                cnt = self.dma_count[stream[1]]
                if seen.get(stream, -1) >= cnt:
                    continue
                seen[stream] = cnt
                waits.append((stream, cnt))
            else:
                if seen.get(stream, -1) >= idx:
                    continue
                seen[stream] = idx
                self.ops[stream][idx]["inc"] = True
                waits.append((stream, idx))
        if dma_sem is not None:
            c = self.dma_count.get(dma_sem, 0) + 1
            self.dma_count[dma_sem] = c
            pid = (("d", dma_sem), c)
        else:
            pid = (eng, len(self.ops[eng]))
        self.ops[eng].append({"fn": fn, "waits": waits, "inc": False, "dma": dma_sem})
        for base, sub in rk:
            st = self.state.setdefault((base, sub), [None, {}])
            if sub is not None:
                self.subs.setdefault(base, set()).add(sub)
            if st[1].get(pid[0], -1) < pid[1]:
                st[1][pid[0]] = pid[1]
        for base, sub in wk:
            if sub is None:
                for s in self.subs.get(base, ()):
                    self.state.pop((base, s), None)
                self.subs[base] = set()
            else:
                self.subs.setdefault(base, set()).add(sub)
            self.state[(base, sub)] = [pid, {}]

    def emit(self, es):
        nc = self.nc
        sems = {}
        for e in ("pe", "act", "dve", "pool"):
            sems[e] = es.enter_context(nc.semaphore("s_" + e))
        for name in self.dma_count:
            sems[("d", name)] = es.enter_context(nc.semaphore("d_" + name))
        cum = {}
        for e in ENGS:
            c = 0
            arr = []
            for op in self.ops[e]:
                if op["inc"]:
                    c += 1
                arr.append(c)
            cum[e] = arr
        block = es.enter_context(nc.Block())
        ops = self.ops

        def run(e, engh):
            for op in ops[e]:
                for stream, v in op["waits"]:
                    if isinstance(stream, tuple):
                        engh.wait_ge(sems[stream], 16 * v)
                    else:
                        engh.wait_ge(sems[stream], cum[stream][v])
                ins = op["fn"](engh)
                if ins is None:
                    continue
                if op["dma"] is not None:
                    ins.then_inc(sems[("d", op["dma"])], 16)
                elif op["inc"]:
                    ins.then_inc(sems[e], 1)

        @block.tensor
        def _(t):
            run("pe", t)

        @block.scalar
        def _(t):
            run("act", t)

        @block.vector
        def _(t):
            run("dve", t)

        @block.gpsimd
        def _(t):
            run("pool", t)

        @block.sync
        def _(t):
            run("sp", t)


class Gen:
    def __init__(self, nc, es):
        self.nc = nc
        self.es = es
        self.sc = Sched(nc)
        slab = nc.alloc_sbuf_tensor("slab", [128, 207000], mybir.dt.uint8)
        self.base = nc.lookup_mloc(slab).addr
        self.nps = 0

    def at(self, name, shape, dt, off):
        return self.nc.alloc_sbuf_tensor_at(name, shape, dt, offset=self.base + off)

    def mm(self, out, lhsT, rhs, start=True, stop=True, r=(), w=()):
        self.sc.rec("pe", lambda e: e.matmul(out, lhsT, rhs, start=start, stop=stop), reads=r, writes=w)

    def tr(self, out, in_, ident, r=(), w=()):
        self.sc.rec("pe", lambda e: e.transpose(out, in_, ident), reads=r, writes=w)

    def op(self, eng, method, r=(), w=(), **kw):
        self.sc.rec(eng, lambda e: getattr(e, method)(**kw), reads=r, writes=w)

    def act(self, out, in_, func, r=(), w=(), **kw):
        self.sc.rec("act", lambda e: e.activation(out=out, in_=in_, func=func, **kw), reads=r, writes=w)

    def dma(self, eng, out, in_, sem, r=(), w=()):
        self.sc.rec(eng, lambda e: e.dma_start(out=out, in_=in_), reads=r, writes=w, dma_sem=sem)


class WStream:
    def __init__(self, g, tiles):
        self.g = g
        self.tiles = tiles
        self.plan = []
        self.loaded = 0
        self.used = 0

    def add(self, ap, nk=16):
        self.plan.append((ap, nk))

    def get(self, hold=1):
        n = len(self.tiles)
        while self.loaded < len(self.plan) and self.loaded < self.used + n - (hold - 1):
            ap, nk = self.plan[self.loaded]
            i = self.loaded % n
            self.g.dma("pool", self.tiles[i][:, 0:nk, :], ap, "w%d" % i, w=[("wb", i)])
            self.loaded += 1
        i = self.used % n
        self.used += 1
        return self.tiles[i], ("wb", i)


def build(nl, final, dbg=False, cfg=None):
    cfg = cfg or {}
    nc = bass.Bass("TRN2", target_bir_lowering=False)
    dr = lambda name, shape, dt, kind="ExternalInput": nc.dram_tensor(name, shape, dt, kind=kind).ap()
    xin = dr("xin", [KC, 128, S], F32)
    memT = dr("memT", [KC, 128, NMEM], F32)
    cst = dr("cst", [128, 7 * 128 + 4 * 512], F32)
    nrm = dr("nrm", [128, nl * 64 + 16], F32)
    w_in = dr("w_in", [nl, 56, 128, 16, 128], F32)
    w_ba = dr("w_ba", [nl, 128, 16, 16], F32)
    gconv = dr("gconv", [nl, 128, 24 * 4], F32)
    gsm = dr("gsm", [nl, 128, 17], F32)
    w_out = dr("w_out", [nl, 16, 128, 16, 128], F32)
    w_xq = dr("w_xq", [nl, 4, 128, 16, 128], F32)
    w_xkv = dr("w_xkv", [nl, 8, 128, 16, 128], F32)
    w_xo = dr("w_xo", [nl, 4, 128, 16, 128], F32)
    w_up = dr("w_up", [nl, NFB, 128, 16, 128], F32)
    fconv = dr("fconv", [nl, 128, NFB * 4], F32)
    w_dn = dr("w_dn", [nl, 16, 4, 128, 11, 128], F32)
    okind = "ExternalOutput"
    xw = dr("xw", [KC, 128, S], F32, kind=okind if not final else "Internal")
    yout = dr("yout", [KC, 128, S], F32, kind=okind) if final else None
    mixT = dr("mixT", [KC, 128, S], BF16, kind=okind if dbg else "Internal")
    dbg_keys = []

    def dump(name, ap, keys, shape, dt):
        if not dbg or not cfg.get("dump"):
            return
        if name not in cfg["dump"]:
            return
        t = dr("D_" + name, shape, dt, kind=okind)
        g.dma("sp", t, ap, "dbg", r=keys, w=["D_" + name])
        dbg_keys.append("D_" + name)

    NSB = cfg.get("sb", 8); NGD = cfg.get("gdn", 8); DO_OUT = cfg.get("out", True)
    DO_X = cfg.get("xattn", True); DO_F = cfg.get("ffn", True)
    es = ExitStack()
    with es:
        g = Gen(nc, es)
        sc = g.sc
        o = 0
        cst_t = g.at("cst_t", [128, 7 * 128 + 4 * 512], F32, o); o += (7 * 128 + 4 * 512) * 4
        ident = cst_t[:, 0:128]; Lincl = cst_t[:, 128:256]; Uincl = cst_t[:, 256:384]
        Ustr = cst_t[:, 384:512]; mSL = cst_t[:, 512:640]; mUI = cst_t[:, 640:768]; ones = cst_t[:, 768:896]
        maskd = [cst_t[:, 896 + 512 * d: 896 + 512 * (d + 1)] for d in range(4)]
        identb = g.at("identb", [128, 128], BF16, o); o += 256
        nrm_t = g.at("nrm_t", [128, nl * 64 + 16], F32, o); o += (nl * 64 + 16) * 4
        gconv_t = g.at("gconv_t", [128, 96], F32, o); o += 384
        gsm_t = g.at("gsm_t", [128, 17], F32, o); o += 68 + 28
        fconv_t = g.at("fconv_t", [128, NFB * 4], F32, o); o += NFB * 16
        halo = g.at("halo", [128, NFB, 2], F32, o); o += NFB * 8
        rstd = g.at("rstd", [128, 512], F32, o); o += 2048
        sq = g.at("sq", [128, 2, 512], F32, o); o += 4096
        sqb = g.at("sqb", [128, 2, 512], BF16, o); o += 2048
        onesb = g.at("onesb", [128, 128], BF16, o); o += 256
        NWB = 6
        wbt = g.at("wbt", [128, NWB, 16, 128], BF16, o); o += NWB * 4096
        wtiles = [wbt[:, i] for i in range(NWB)]
        xt_off = o
        xt = g.at("xt", [128, KC, 512], F32, o); o += 32768
        big_off = o
        hT = g.at("hT", [128, KC, S], BF16, o)
        ar = o + 65536
        o += 122880
        assert o <= 207000, o
        ws = WStream(g, wtiles)

        PA = es.enter_context(nc.psum_tensor("PA", [128, 1024], F32))
        PB = es.enter_context(nc.psum_tensor("PB", [128, 1024], F32))
        PC = es.enter_context(nc.psum_tensor("PC", [128, 1024], F32))
        PD = es.enter_context(nc.psum_tensor("PD", [128, 512], F32))
        PTf = es.enter_context(nc.psum_tensor("PT", [128, 512], F32))
        PT = PTf.bitcast(BF16)

        g.dma("sp", cst_t[:], cst, "c", w=["cst"])
        g.dma("sp", nrm_t[:], nrm, "c", w=["nrm"])
        g.op("dve", "tensor_copy", r=["cst"], w=["identb"], out=identb[:], in_=ident)
        g.op("dve", "tensor_copy", r=["cst"], w=["onesb"], out=onesb[:], in_=ones)
        for kc in range(KC):
            g.dma("sp", xw[kc], xin[kc], "xcp", w=[("xw", kc)])

        def gcol(l, which, kc):
            c = l * 64 + which * 16 + kc
            return nrm_t[:, c:c + 1]

        XT_ALIAS = [("cg", 0), ("cg", 1), ("cu", 0), ("cu", 1), ("xr2", 0), ("xr2", 1), "gq", "gk", "gv", "gz"]

        def rmsnorm(src, srckey, gc, dst, T0, T1, dkey, extra_w=()):
            for t0 in range(T0, T1, 512):
                for kc in range(KC):
                    g.dma("sp", xt[:, kc, :], src[kc, :, t0:t0 + 512], "x", r=[(srckey, kc)], w=[("xt", kc)] + XT_ALIAS)
                    g.act(sqb[:, kc % 2, :], xt[:, kc, :], AF.Square, r=[("xt", kc)], w=[("sqb", kc % 2)])
                    g.mm(PD[:], onesb[:], sqb[:, kc % 2, :], start=(kc == 0), stop=(kc == KC - 1),
                         r=["onesb", ("sqb", kc % 2)], w=["PD"])
                g.act(rstd[:], PD[:], AF.Ln, r=["PD"], w=["rstd"], bias=EPS, scale=1.0 / D)
                g.act(rstd[:], rstd[:], AF.Exp, r=["rstd"], w=["rstd"], scale=-0.5)
                for kc in range(KC):
                    g.op("dve", "scalar_tensor_tensor", r=[("xt", kc), "nrm", "rstd"], w=[(dkey, kc, t0)],
                         out=dst[:, kc, t0 - T0:t0 - T0 + 512], in0=xt[:, kc, :], scalar=gc(kc), in1=rstd[:],
                         op0=ALU.mult, op1=ALU.mult)

        def proj(src, skey, T, evac):
            wt, wk = ws.get()
            for tt in range(T // 512):
                pa = (PA, PB)[tt % 2]
                half = (tt // 2) % 2
                pv = pa[:, half * 512:(half + 1) * 512]
                pk = ("PAB", tt % 2, half)
                for kc in range(KC):
                    g.mm(pv, wt[:, kc, :], src[:, kc, tt * 512:(tt + 1) * 512], start=(kc == 0), stop=(kc == KC - 1),
                         r=[wk, skey], w=[pk])
                evac(tt, pv, pk)

        def residual_linear(wplan_nk, nkc, src, skey, T0, T):
            raise NotImplementedError

        for l in range(nl):
            g.dma("sp", gconv_t[:], gconv[l], "c", w=["gconv"])
            g.dma("sp", gsm_t[:], gsm[l], "c", w=["gsm"])
            g.dma("sp", fconv_t[:], fconv[l], "c", w=["fconv"])
            for hd in range(NSB):
                for j in range(3):
                    ws.add(w_in[l, j * 8 + hd])
            for hd in range(NGD):
                for j in range(4):
                    ws.add(w_in[l, 24 + j * 8 + hd])
            for ob in range(16 if DO_OUT else 0):
                ws.add(w_out[l, ob])
            for j in range(8 if DO_X else 0):
                ws.add(w_xkv[l, j])
            for j in range(4 if DO_X else 0):
                ws.add(w_xq[l, j])
            for j in range(4 if DO_X else 0):
                ws.add(w_xo[l, j])
            for tt in range(2 if DO_F else 0):
                for fb in range(44):
                    ws.add(w_up[l, fb]); ws.add(w_up[l, 44 + fb])
                for ob in range(16):
                    for kg in range(4):
                        ws.add(w_dn[l, ob, kg], 11)

            rmsnorm(xw, "xw", lambda kc: gcol(l, 0, kc), hT, 0, S, "hT")

            a = ar
            wba_t = g.at("wba_t%d" % l, [128, 16, 16], BF16, a); a += 512
            bg = g.at("bg%d" % l, [128, 16, 16], F32, a); a += 1024
            beta_t = g.at("beta%d" % l, [128, 16, 8], F32, a); a += 512
            nbeta_t = g.at("nbeta%d" % l, [128, 16, 8], F32, a); a += 512
            g_t = g.at("g_t%d" % l, [128, 16, 8], F32, a); a += 512
            nA = g.at("nA%d" % l, [128, 8], F32, a); a += 32
            gates_end = a
            g.dma("pool", wba_t[:], w_ba[l], "wba", w=["wba", "actT"])
            for blk in range(16):
                for kc in range(KC):
                    g.mm(PD[:, blk * 16:(blk + 1) * 16], hT[:, kc, blk * 128:(blk + 1) * 128], wba_t[:, kc, :],
                         start=(kc == 0), stop=(kc == KC - 1), r=["hT", "wba"], w=["PD"])
            g.op("dve", "tensor_copy", r=["PD"], w=["bg"], out=bg[:].rearrange("p a b -> p (a b)"), in_=PD[:, 0:256])
            g.act(beta_t[:], bg[:, :, 0:8], AF.Exp, r=["bg"], w=["beta"], scale=-1.0)
            g.op("dve", "tensor_scalar_add", r=["beta"], w=["beta"], out=beta_t[:], in0=beta_t[:], scalar1=1.0)
            g.op("dve", "reciprocal", r=["beta"], w=["beta"], out=beta_t[:], in_=beta_t[:])
            g.op("dve", "tensor_scalar_mul", r=["beta"], w=["nbeta"], out=nbeta_t[:], in0=beta_t[:], scalar1=-1.0)
            g.act(nA[:], gsm_t[:, 0:8], AF.Exp, r=["gsm"], w=["nA"])
            g.op("dve", "tensor_scalar_mul", r=["nA"], w=["nA"], out=nA[:], in0=nA[:], scalar1=-1.0)
            for blk in range(16):
                g.op("dve", "tensor_tensor", r=["bg", "gsm"], w=["g_t"], out=g_t[:, blk, :], in0=bg[:, blk, 8:16],
                     in1=gsm_t[:, 8:16], op=ALU.add)
            g.act(g_t[:], g_t[:], AF.Exp, r=["g_t"], w=["g_t"])
            g.act(g_t[:], g_t[:], AF.Ln, r=["g_t"], w=["g_t"], bias=1.0)
            for blk in range(16):
                g.op("dve", "tensor_tensor", r=["g_t", "nA"], w=["g_t"], out=g_t[:, blk, :], in0=g_t[:, blk, :],
                     in1=nA[:], op=ALU.mult)

            a = gates_end
            a = (a + 63) // 64 * 64
            qT = g.at("qT%d" % l, [128, S], BF16, a); a += 4096
            kT = g.at("kT%d" % l, [128, S], BF16, a); a += 4096
            vT = g.at("vT%d" % l, [128, S], BF16, a); a += 4096
            vtok = g.at("vtok%d" % l, [128, 16, 128], BF16, a); a += 4096
            ebuf = g.at("ebuf%d" % l, [128, 3, 512], F32, a); a += 6144
            spbuf = g.at("spbuf%d" % l, [128, 3, 512], F32, a); a += 6144
            ecbuf = g.at("ecbuf%d" % l, [128, 2, 512], F32, a); a += 4096
            racc = g.at("racc%d" % l, [128, 512], F32, a); a += 2048
            wbuf = g.at("wbuf%d" % l, [128, 2, 512], BF16, a); a += 2048
            obuf = g.at("obuf%d" % l, [128, 2, 512], BF16, a); a += 2048
            assert a <= big_off + 122880
            scale = 128.0 ** -0.5
            for hd in range(NSB):
                for j, dst, key in ((0, qT, "qT"), (1, kT, "kT"), (2, vT, "vT")):
                    def ev(tt, pv, pk, dst=dst, key=key):
                        if tt % 2 == 0:
                            g.act(dst[:, tt * 512:(tt + 1) * 512], pv, AF.Copy, r=[pk], w=[(key, tt)])
                        else:
                            g.op("dve", "tensor_copy", r=[pk], w=[(key, tt)], out=dst[:, tt * 512:(tt + 1) * 512], in_=pv)
                    proj(hT, "hT", S, ev)
                for half in range(2):
                    for b in range(8):
                        blk = half * 8 + b
                        g.tr(PT[:, b * 128:(b + 1) * 128], vT[:, blk * 128:(blk + 1) * 128], identb[:],
                             r=[("vT", blk // 4), "identb"], w=["PT"])
                    g.op("dve", "tensor_copy", r=["PT"], w=[("vtok", half)],
                         out=vtok[:, half * 8:(half + 1) * 8, :].rearrange("p a b -> p (a b)"), in_=PT[:])
                its = []
                for qt in range(4):
                    nkb = 4 * qt + 4
                    for it, kb in enumerate(range(nkb - 1, -1, -1)):
                        its.append((qt, it, kb))

                def stA(n):
                    qt, it, kb = its[n]
                    i2 = n % 2; i3 = n % 3
                    pz = PC[:, i2 * 512:(i2 + 1) * 512]; pzk = ("PC", i2)
                    g.mm(pz, kT[:, kb * 128:(kb + 1) * 128], qT[:, qt * 512:(qt + 1) * 512], r=[("kT", kb // 4), ("qT", qt)], w=[pzk])
                    e_ = ebuf[:, i3, :]; sp_ = spbuf[:, i3, :]
                    g.act(e_, pz, AF.Exp, r=[pzk], w=[("e", i3)], scale=scale)
                    g.act(sp_, e_, AF.Ln, r=[("e", i3)], w=[("sp", i3)], bias=1.0)
                    if kb >= 4 * qt:
                        md = maskd[kb - 4 * qt]
                        g.op("dve", "tensor_tensor", r=[("sp", i3), "cst"], w=[("sp", i3)], out=sp_, in0=sp_, in1=md, op=ALU.mult)
                        g.op("dve", "tensor_tensor", r=[("e", i3), "cst"], w=[("e", i3)], out=e_, in0=e_, in1=md, op=ALU.mult)

                def stB(n):
                    qt, it, kb = its[n]
                    i2 = n % 2; i3 = n % 3
                    e_ = ebuf[:, i3, :]; sp_ = spbuf[:, i3, :]; ec_ = ecbuf[:, i2, :]
                    pcs = (PA, PB)[i2][:, 512:1024]; pck = ("PAB", i2, 1)
                    g.mm(pcs, Lincl, sp_, start=True, stop=(it == 0), r=["cst", ("sp", i3)], w=[pck])
                    if it > 0:
                        g.mm(pcs, ones, racc[:], start=False, stop=True, r=["cst", "racc"], w=[pck])
                    g.act(ec_, pcs, AF.Exp, r=[pck], w=[("ec", i2)], scale=-1.0)
                    g.op("dve", "tensor_tensor", r=[("e", i3), ("ec", i2)], w=[("wbuf", i2)], out=wbuf[:, i2, :], in0=e_, in1=ec_, op=ALU.mult)
                    if it == 0:
                        g.op("dve", "tensor_copy", r=[("sp", i3)], w=["racc"], out=racc[:], in_=sp_)
                    elif kb > 0:
                        g.op("dve", "tensor_tensor", r=[("sp", i3), "racc"], w=["racc"], out=racc[:], in0=racc[:], in1=sp_, op=ALU.add)

                def stC(n):
                    qt, it, kb = its[n]
                    i2 = n % 2
                    g.mm(PD[:], vtok[:, kb, :], wbuf[:, i2, :], start=(it == 0), stop=(kb == 0),
                         r=[("vtok", kb // 8), ("wbuf", i2)], w=["PD"])
                    if kb == 0:
                        ob_ = obuf[:, qt % 2, :]
                        g.act(ob_, PD[:], AF.Copy, r=["PD"], w=[("obuf", qt % 2)])
                        g.dma("sp", mixT[hd, :, qt * 512:(qt + 1) * 512], ob_, "mo", r=[("obuf", qt % 2)], w=[("mixT", hd)])

                NI = len(its)
                for n in range(NI + 2):
                    if n < NI:
                        stA(n)
                    if 0 <= n - 1 < NI:
                        stB(n - 1)
                    if 0 <= n - 2 < NI:
                        stC(n - 2)

            a = gates_end
            a = (a + 63) // 64 * 64
            gates_end_al = a
            gq = xt[:].rearrange("p a b -> p (a b)")
            gqT = gq[:, 0:S]; gkT = gq[:, S:2 * S]; gvT = gq[:, 2 * S:3 * S]; gzT = gq[:, 3 * S:4 * S]
            cb = g.at("cb%d" % l, [128, S + 8], F32, a); a += (S + 8) * 4
            ogT = g.at("ogT%d" % l, [128, S], F32, a); a += S * 4
            wide = {}
            for nm in ("Gm", "DecL", "DecT", "egcb", "Qa", "Qb", "Pa", "Pb", "X", "vb", "kbg", "ktail", "u", "wT", "ATm", "qdT"):
                wide[nm] = g.at(nm + str(l), [128, 4, 128], F32, a); a += 2048
            Sst = g.at("Sst%d" % l, [128, 2, 128], F32, a); a += 1024
            vnew = g.at("vnew%d" % l, [128, 128], F32, a); a += 512
            ecols = g.at("ecols%d" % l, [128, 4, 4], F32, a); a += 64
            bcol2 = g.at("bcol2%d" % l, [128, 4], F32, a); a += 64
            ecols1 = g.at("ecols1_%d" % l, [128, 4, 4], F32, a); a += 64
            ktail1 = g.at("ktail1_%d" % l, [128, 4, 128], F32, a); a += 2048
            assert a <= big_off + 122880, a
            cb_off = gates_end_al
            scan_in = {0: {"u": wide["u"], "wT": wide["wT"], "ATm": wide["ATm"], "qdT": wide["qdT"], "ktail": wide["ktail"], "ecols": ecols},
                       1: {"u": g.at("u1_%d" % l, [128, 4, 128], F32, cb_off), "wT": g.at("wT1_%d" % l, [128, 4, 128], F32, cb_off + 2048),
                           "ATm": g.at("ATm1_%d" % l, [128, 4, 128], F32, cb_off + 4096), "qdT": g.at("qdT1_%d" % l, [128, 4, 128], F32, cb_off + 6144),
                           "ktail": ktail1, "ecols": ecols1}}
            CB_ALIAS = ["u1", "wT1", "ATm1", "qdT1"]
            Gm, DecL, DecT, egcb = wide["Gm"], wide["DecL"], wide["DecT"], wide["egcb"]
            X, vb, kbg, ktail, u_, wT_, ATm, qdT = (wide[n] for n in ("X", "vb", "kbg", "ktail", "u", "wT", "ATm", "qdT"))
            fl = lambda t: t[:].rearrange("p a b -> p (a b)")
            class _Stop(Exception):
                pass

            def chk(stage):
                if cfg.get("gstop") == stage:
                    raise _Stop()

            def gdn_head(hd):
                for j, dst, key in ((0, gqT, "gq"), (1, gkT, "gk"), (2, gvT, "gv"), (3, gzT, "gz")):
                    if j < 3:
                        g.op("dve", "memset", w=["cb"] + CB_ALIAS, ap=cb[:, 0:3], constant=0.0)
                        def ev(tt, pv, pk):
                            if tt % 2 == 0:
                                g.act(cb[:, 3 + tt * 512:3 + (tt + 1) * 512], pv, AF.Copy, r=[pk], w=["cb"] + CB_ALIAS)
                            else:
                                g.op("dve", "tensor_copy", r=[pk], w=["cb"] + CB_ALIAS, out=cb[:, 3 + tt * 512:3 + (tt + 1) * 512], in_=pv)
                        proj(hT, "hT", S, ev)
                        fb = j * 8 + hd
                        wc = lambda i: gconv_t[:, fb * 4 + i:fb * 4 + i + 1]
                        g.act(dst, cb[:, 3:3 + S], AF.Identity, r=["cb", "gconv"], w=[key], scale=wc(3))
                        for i in range(3):
                            g.op("dve", "scalar_tensor_tensor", r=["cb", "gconv", key], w=[key], out=dst, in0=cb[:, i:i + S],
                                 scalar=wc(i), in1=dst, op0=ALU.mult, op1=ALU.add)
                        g.act(dst, dst, AF.Silu, r=[key], w=[key])
                    else:
                        def ev(tt, pv, pk):
                            g.act(gzT[:, tt * 512:(tt + 1) * 512], pv, AF.Silu, r=[pk], w=["gz"])
                        proj(hT, "hT", S, ev)
                chk(1)
                for dst, key, sc_ in ((gqT, "gq", 128.0 ** -0.5), (gkT, "gk", 1.0)):
                    for tt in range(4):
                        tsl = slice(tt * 512, (tt + 1) * 512)
                        g.act(sq[:, tt % 2, :], dst[:, tsl], AF.Square, r=[key], w=[("sq", tt % 2)])
                        g.mm(PD[:], ones, sq[:, tt % 2, :], r=["cst", ("sq", tt % 2)], w=["PD"])
                        g.act(rstd[:], PD[:], AF.Ln, r=["PD"], w=["rstd"], bias=EPS, scale=1.0)
                        g.act(rstd[:], rstd[:], AF.Exp, r=["rstd"], w=["rstd"], scale=-0.5)
                        g.op("dve", "scalar_tensor_tensor", r=[key, "rstd"], w=[key], out=dst[:, tsl], in0=dst[:, tsl],
                             scalar=sc_, in1=rstd[:], op0=ALU.mult, op1=ALU.mult)
                chk(2)
                g.op("dve", "memset", w=[("Sst", 0)], ap=Sst[:, 0, :], constant=0.0)

                def prep_stages(grp):
                    par = grp % 2
                    blks = [grp * 4 + b for b in range(4)]
                    si = scan_in[par]
                    u_, wT_, ATm, qdT, ktail, ecols = si["u"], si["wT"], si["ATm"], si["qdT"], si["ktail"], si["ecols"]
                    ku, kw, ka, kq, kk, ke = ("u%d" % par, "wT%d" % par, "ATm%d" % par, "qdT%d" % par, "ktail%d" % par, "ecols%d" % par)
                    st = []

                    def s1():
                        for b, n in enumerate(blks):
                            g.op("dve", "tensor_scalar", r=["cst", "g_t"], w=["Gm"], out=Gm[:, b, :], in0=Uincl,
                                 scalar1=g_t[:, n, hd:hd + 1], scalar2=None, op0=ALU.mult)
                    st.append(s1)

                    def s2():
                        for b, n in enumerate(blks):
                            g.mm(PA[:, b * 128:(b + 1) * 128], Gm[:, b, :], Ustr, r=["Gm", "cst"], w=[("PAB", 0, 0)])
                            g.mm(PA[:, 512 + b * 128:512 + (b + 1) * 128], Ustr, Gm[:, b, :], r=["Gm", "cst"], w=[("PAB", 0, 1)])
                            g.mm(PB[:, b * 128:(b + 1) * 128], ones, Gm[:, b, :], r=["Gm", "cst"], w=[("PAB", 1, 0)])
                            g.mm(PC[:, b * 4:b * 4 + 1], Gm[:, b, :], ones[:, 0:1], r=["Gm", "cst"], w=[("PC", 0)])
                            g.mm(PC[:, b * 4 + 1:b * 4 + 2], Ustr, g_t[:, n, hd:hd + 1], r=["g_t", "cst"], w=[("PC", 0)])
                            g.mm(PC[:, b * 4 + 2:b * 4 + 3], ones, g_t[:, n, hd:hd + 1], r=["g_t", "cst"], w=[("PC", 0)])
                    st.append(s2)

                    def s3():
                        g.act(fl(DecL), PA[:, 0:512], AF.Exp, r=[("PAB", 0, 0)], w=["DecL"])
                        g.act(fl(DecT), PA[:, 512:1024], AF.Exp, r=[("PAB", 0, 1)], w=["DecT"])
                        g.act(fl(egcb), PB[:, 0:512], AF.Exp, r=[("PAB", 1, 0)], w=["egcb"])
                        g.act(ecols[:].rearrange("p a b -> p (a b)"), PC[:, 0:16], AF.Exp, r=[("PC", 0)], w=[ke])
                    st.append(s3)

                    def s4():
                        for b, n in enumerate(blks):
                            g.op("dve", "tensor_tensor", r=["DecL", "cst"], w=["DecL"], out=DecL[:, b, :], in0=DecL[:, b, :], in1=mSL, op=ALU.mult)
                            g.op("dve", "tensor_tensor", r=["DecT", "cst"], w=["DecT"], out=DecT[:, b, :], in0=DecT[:, b, :], in1=mUI, op=ALU.mult)
                            g.op("dve", "tensor_tensor", r=[ke, "beta"], w=["bcol2"], out=bcol2[:, b:b + 1], in0=ecols[:, b, 0:1],
                                 in1=beta_t[:, n, hd:hd + 1], op=ALU.mult)
                        for b, n in enumerate(blks):
                            bs = slice(n * 128, (n + 1) * 128)
                            g.mm(PC[:, b * 128:(b + 1) * 128], gkT[:, bs], gkT[:, bs], r=["gk"], w=[("PC", 0)])
                    st.append(s4)

                    def s5():
                        Q = wide["Qa"]
                        for b, n in enumerate(blks):
                            g.op("dve", "scalar_tensor_tensor", r=[("PC", 0), "nbeta", "DecL"], w=["Qa"], out=Q[:, b, :],
                                 in0=PC[:, b * 128:(b + 1) * 128], scalar=nbeta_t[:, n, hd:hd + 1], in1=DecL[:, b, :],
                                 op0=ALU.mult, op1=ALU.mult)
                        for b in range(4):
                            g.mm(PC[:, 512 + b * 128:512 + (b + 1) * 128], Q[:, b, :], ident, r=["Qa", "cst"], w=[("PC", 1)])
                    st.append(s5)

                    def s6():
                        P = wide["Pa"]
                        g.act(fl(P), PC[:, 512:1024], AF.Copy, r=[("PC", 1)], w=["Pa"])
                        for b in range(4):
                            g.op("dve", "tensor_tensor", r=["Pa", "cst"], w=["X"], out=X[:, b, :], in0=P[:, b, :], in1=ident, op=ALU.add)
                    st.append(s6)
                    names = [("Qa", "Pa"), ("Qb", "Pb")]
                    for step in range(6):
                        qn, pn = names[step % 2]
                        qn2, pn2 = names[(step + 1) % 2]

                        def sa(qn=qn, pn=pn, qn2=qn2, pn2=pn2, step=step):
                            Q, P, Q2, P2 = wide[qn], wide[pn], wide[qn2], wide[pn2]
                            for b in range(4):
                                g.mm(PC[:, b * 128:(b + 1) * 128], P[:, b, :], Q[:, b, :], r=[qn, pn], w=[("PC", 0)])
                            if step < 5:
                                for b in range(4):
                                    g.mm(PC[:, 512 + b * 128:512 + (b + 1) * 128], Q[:, b, :], P[:, b, :], r=[qn, pn], w=[("PC", 1)])
                            g.act(fl(Q2), PC[:, 0:512], AF.Copy, r=[("PC", 0)], w=[qn2])
                            if step < 5:
                                g.op("dve", "tensor_copy", r=[("PC", 1)], w=[pn2], out=fl(P2), in_=PC[:, 512:1024])
                        st.append(sa)

                        def sb_(qn2=qn2):
                            Q2 = wide[qn2]
                            for b in range(4):
                                g.mm(PB[:, 512 + b * 128:512 + (b + 1) * 128], Q2[:, b, :], X[:, b, :], r=[qn2, "X"], w=[("PAB", 1, 1)])
                            g.op("dve", "tensor_tensor", r=[("PAB", 1, 1), "X"], w=["X"], out=fl(X), in0=fl(X), in1=PB[:, 512:1024], op=ALU.add)
                        st.append(sb_)

                    def s7():
                        for b, n in enumerate(blks):
                            bs = slice(n * 128, (n + 1) * 128)
                            g.mm(PA[:, b * 128:(b + 1) * 128], gkT[:, bs], ident, r=["gk", "cst"], w=[("PAB", 0, 0)])
                            g.mm(PA[:, 512 + b * 128:512 + (b + 1) * 128], gvT[:, bs], ident, r=["gv", "cst"], w=[("PAB", 0, 1)])
                            g.mm(PB[:, b * 128:(b + 1) * 128], gkT[:, bs], gqT[:, bs], r=["gk", "gq"], w=[("PAB", 1, 0)])
                    st.append(s7)

                    def s8():
                        for b, n in enumerate(blks):
                            g.op("dve", "tensor_scalar", r=[("PAB", 0, 0), "bcol2"], w=["kbg"], out=kbg[:, b, :], in0=PA[:, b * 128:(b + 1) * 128],
                                 scalar1=bcol2[:, b:b + 1], scalar2=None, op0=ALU.mult)
                            g.op("dve", "tensor_scalar", r=[("PAB", 0, 0), ke], w=[kk], out=ktail[:, b, :], in0=PA[:, b * 128:(b + 1) * 128],
                                 scalar1=ecols[:, b, 1:2], scalar2=None, op0=ALU.mult)
                        for b, n in enumerate(blks):
                            g.act(vb[:, b, :], PA[:, 512 + b * 128:512 + (b + 1) * 128], AF.Identity, r=[("PAB", 0, 1), "beta"], w=["vb"],
                                  scale=beta_t[:, n, hd:hd + 1])
                        g.op("dve", "tensor_tensor", r=[("PAB", 1, 0), "DecT"], w=[ka], out=fl(ATm), in0=fl(DecT), in1=PB[:, 0:512], op=ALU.mult)
                        g.op("dve", "tensor_tensor", r=["gq", "egcb"], w=[kq], out=fl(qdT), in0=gqT[:, grp * 512:(grp + 1) * 512], in1=fl(egcb), op=ALU.mult)
                    st.append(s8)

                    def s9():
                        for b, n in enumerate(blks):
                            g.mm(PA[:, b * 128:(b + 1) * 128], X[:, b, :], vb[:, b, :], r=["X", "vb"], w=[("PAB", 0, 0)])
                            g.mm(PA[:, 512 + b * 128:512 + (b + 1) * 128], kbg[:, b, :], X[:, b, :], r=["X", "kbg"], w=[("PAB", 0, 1)])
                        g.act(fl(u_), PA[:, 0:512], AF.Copy, r=[("PAB", 0, 0)], w=[ku])
                        g.op("dve", "tensor_copy", r=[("PAB", 0, 1)], w=[kw], out=fl(wT_), in_=PA[:, 512:1024])
                    st.append(s9)
                    return st

                def scan_stages(grp):
                    par = grp % 2
                    blks = [grp * 4 + b for b in range(4)]
                    si = scan_in[par]
                    u_, wT_, ATm, qdT, ktail, ecols = si["u"], si["wT"], si["ATm"], si["qdT"], si["ktail"], si["ecols"]
                    ku, kw, ka, kq, kk, ke = ("u%d" % par, "wT%d" % par, "ATm%d" % par, "qdT%d" % par, "ktail%d" % par, "ecols%d" % par)
                    st = []
                    for b, n in enumerate(blks):
                        s0 = n % 2; s1_ = 1 - s0
                        Sc = Sst[:, s0, :]; Sn = Sst[:, s1_, :]

                        def c1(b=b, s0=s0, Sc=Sc):
                            g.mm(PD[:, 0:128], wT_[:, b, :], Sc, r=[kw, ("Sst", s0)], w=["PD"])
                            g.op("dve", "tensor_tensor", r=[ku, "PD"], w=["vnew"], out=vnew[:], in0=u_[:, b, :], in1=PD[:, 0:128], op=ALU.subtract)
                        st.append(c1)

                        def c2(b=b, n=n, s0=s0, s1_=s1_, Sc=Sc, Sn=Sn):
                            g.mm(PTf[:, 0:128], Sc, qdT[:, b, :], start=True, stop=False, r=[("Sst", s0), kq], w=["PT"])
                            g.mm(PTf[:, 0:128], vnew[:], ATm[:, b, :], start=False, stop=True, r=["vnew", ka], w=["PT"])
                            g.mm(PD[:, 128:256], ktail[:, b, :], vnew[:], r=[kk, "vnew"], w=["PD"])
                            g.op("dve", "scalar_tensor_tensor", r=[("Sst", s0), ke, "PD"], w=[("Sst", s1_)], out=Sn, in0=Sc,
                                 scalar=ecols[:, b, 2:3], in1=PD[:, 128:256], op0=ALU.mult, op1=ALU.add)
                            g.act(ogT[:, n * 128:(n + 1) * 128], PTf[:, 0:128], AF.Copy, r=["PT"], w=["ogT"])
                        st.append(c2)
                    return st

                prev = []
                for grp in range(5):
                    cur = prep_stages(grp) if grp < 4 else []
                    na, nb_ = len(cur), len(prev)
                    ia = ib = 0
                    while ia < na or ib < nb_:
                        if ia < na and (ib >= nb_ or ia * nb_ <= ib * na):
                            cur[ia](); ia += 1
                        else:
                            prev[ib](); ib += 1
                    prev = scan_stages(grp) if grp < 4 else []
                chk(7)
                for tt in range(4):
                    tsl = slice(tt * 512, (tt + 1) * 512)
                    g.act(sq[:, tt % 2, :], ogT[:, tsl], AF.Square, r=["ogT"], w=[("sq", tt % 2)])
                    g.mm(PD[:], ones, sq[:, tt % 2, :], r=["cst", ("sq", tt % 2)], w=["PD"])
                    g.act(rstd[:], PD[:], AF.Ln, r=["PD"], w=["rstd"], bias=EPS, scale=1.0 / 128)
                    g.act(rstd[:], rstd[:], AF.Exp, r=["rstd"], w=["rstd"], scale=-0.5)
                    g.op("dve", "scalar_tensor_tensor", r=["ogT", "gsm", "rstd"], w=["ogT"], out=ogT[:, tsl], in0=ogT[:, tsl],
                         scalar=gsm_t[:, 16:17], in1=rstd[:], op0=ALU.mult, op1=ALU.mult)
                    ob_ = obuf[:, tt % 2, :]
                    g.op("dve", "tensor_tensor", r=["ogT", "gz"], w=[("obuf", tt % 2)], out=ob_, in0=ogT[:, tsl], in1=gzT[:, tsl], op=ALU.mult)
                    g.dma("sp", mixT[8 + hd, :, tsl], ob_, "mo", r=[("obuf", tt % 2)], w=[("mixT", 8 + hd)])

            for hd in range(NGD):
                try:
                    gdn_head(hd)
                except _Stop:
                    break

            def resid_block(nblk_k, src, skey, T0, T, xtmp, xkey, wkc_list):
                pass

            xr = g.at("xr%d" % l, [128, 2, 1024], F32, ar)
            for kc in range(KC if DO_OUT else 0):
                g.dma("sp", hT[:, kc, :], mixT[kc], "mi", r=[("mixT", kc)], w=["hT"])

            def lin_resid(src, skey, ncin, T0, T, getw):
                for ob in range(16):
                    wl = getw(ob)
                    for th in range(T // 1024):
                        t0 = T0 + th * 1024
                        xs = xr[:, (ob + th) % 2, :]; xk = ("xr", (ob + th) % 2)
                        g.dma("sp", xs, xw[ob, :, t0:t0 + 1024], "xr", r=[("xw", ob)], w=[xk])
                        pa = (PA, PB)[(ob + th) % 2]; pk0 = ("PAB", (ob + th) % 2, 0); pk1 = ("PAB", (ob + th) % 2, 1)
                        for hf in range(2):
                            for i, (wt, wk, ki, ci) in enumerate(wl):
                                g.mm(pa[:, hf * 512:(hf + 1) * 512], wt[:, ki, :], src[:, ci, th * 1024 + hf * 512: th * 1024 + (hf + 1) * 512],
                                     start=(i == 0), stop=(i == len(wl) - 1), r=[wk, skey], w=[(pk0, pk1)[hf]])
                        g.op("dve", "tensor_tensor", r=[pk0, pk1, xk], w=[xk], out=xs, in0=xs, in1=pa[:], op=ALU.add)
                        g.dma("sp", xw[ob, :, t0:t0 + 1024], xs, "xr", r=[xk], w=[("xw", ob)])

            def getw_out(ob):
                wt, wk = ws.get()
                return [(wt, wk, kc, kc) for kc in range(KC)]
            if DO_OUT:
                lin_resid(hT, "hT", 16, 0, S, getw_out)
            if not DO_X:
                continue

            a = ar + 8192
            memn = g.at("memn%d" % l, [128, KC, NMEM], BF16, a); a += 8192
            KT = g.at("KT%d" % l, [128, 4, NMEM], BF16, a); a += 2048
            Vt = g.at("Vt%d" % l, [128, 2, 512], BF16, a); a += 2048
            xq = g.at("xq%d" % l, [128, 4, S], BF16, a); a += 16384
            xo = g.at("xo%d" % l, [128, 4, S], BF16, a); a += 16384
            pb_t = g.at("pb_t%d" % l, [128, NMEM], BF16, a); a += 512
            pT_t = g.at("pT_t%d" % l, [128, 2, 128], BF16, a); a += 512
            mx = g.at("mx%d" % l, [128, 4], F32, a); a += 64
            mx2 = g.at("mx2_%d" % l, [128, 2, 4], F32, a); a += 64
            sc2 = g.at("sc2_%d" % l, [128, 2, NMEM], F32, a); a += 2048
            assert a <= big_off + 122880
            for kc in range(KC):
                g.dma("sp", xt[:, kc, 0:NMEM], memT[kc], "x", w=[("xt", kc)] + XT_ALIAS)
                g.act(sqb[:, kc % 2, 0:NMEM], xt[:, kc, 0:NMEM], AF.Square, r=[("xt", kc)], w=[("sqb", kc % 2)])
                g.mm(PD[:, 0:NMEM], onesb[:], sqb[:, kc % 2, 0:NMEM], start=(kc == 0), stop=(kc == KC - 1), r=["onesb", ("sqb", kc % 2)], w=["PD"])
            g.act(rstd[:, 0:NMEM], PD[:, 0:NMEM], AF.Ln, r=["PD"], w=["rstd"], bias=EPS, scale=1.0 / D)
            g.act(rstd[:, 0:NMEM], rstd[:, 0:NMEM], AF.Exp, r=["rstd"], w=["rstd"], scale=-0.5)
            for kc in range(KC):
                g.op("dve", "scalar_tensor_tensor", r=[("xt", kc), "nrm", "rstd"], w=["memn"], out=memn[:, kc, :], in0=xt[:, kc, 0:NMEM],
                     scalar=gcol(l, 2, kc), in1=rstd[:, 0:NMEM], op0=ALU.mult, op1=ALU.mult)
            for j in range(4):
                wt, wk = ws.get()
                for kc in range(KC):
                    g.mm(PC[:, 0:NMEM], wt[:, kc, :], memn[:, kc, :], start=(kc == 0), stop=(kc == KC - 1), r=[wk, "memn"], w=[("PC", 0)])
                g.act(KT[:, j, :], PC[:, 0:NMEM], AF.Copy, r=[("PC", 0)], w=["KT"])
            for j in range(4):
                wt, wk = ws.get()
                for mb in range(2):
                    for kc in range(KC):
                        g.mm(PC[:, 512 + mb * 128:512 + (mb + 1) * 128], memn[:, kc, mb * 128:(mb + 1) * 128], wt[:, kc, :],
                             start=(kc == 0), stop=(kc == KC - 1), r=[wk, "memn"], w=[("PC", 1)])
                for mb in range(2):
                    g.act(Vt[:, mb, j * 128:(j + 1) * 128], PC[:, 512 + mb * 128:512 + (mb + 1) * 128], AF.Copy, r=[("PC", 1)], w=["Vt"])
            dump("memn", memn[:].rearrange("p a b -> p (a b)"), ["memn"], [128, KC * NMEM], BF16)
            dump("KT", KT[:].rearrange("p a b -> p (a b)"), ["KT"], [128, 4 * NMEM], BF16)
            dump("Vt", Vt[:].rearrange("p a b -> p (a b)"), ["Vt"], [128, 1024], BF16)
            rmsnorm(xw, "xw", lambda kc: gcol(l, 1, kc), hT, 0, S, "hT")
            for j in range(4):
                def ev(tt, pv, pk, j=j):
                    if tt % 2 == 0:
                        g.act(xq[:, j, tt * 512:(tt + 1) * 512], pv, AF.Copy, r=[pk], w=["xq"])
                    else:
                        g.op("dve", "tensor_copy", r=[pk], w=["xq"], out=xq[:, j, tt * 512:(tt + 1) * 512], in_=pv)
                proj(hT, "hT", S, ev)
            def xA(n):
                tb, j = divmod(n, 4)
                i2 = n % 2
                tbs = slice(tb * 128, (tb + 1) * 128)
                pz = PC[:, i2 * 512:i2 * 512 + NMEM]; pzk = ("PC", i2)
                g.mm(pz, xq[:, j, tbs], KT[:, j, :], r=["xq", "KT"], w=[pzk])
                g.op("dve", "reduce_max", r=[pzk], w=[("mxa", i2)], out=mx2[:, i2, 0:1], in_=pz, axis=mybir.AxisListType.X)
                g.op("dve", "tensor_scalar_mul", r=[("mxa", i2)], w=[("mxa", i2)], out=mx2[:, i2, 1:2], in0=mx2[:, i2, 0:1], scalar1=-scale)
                g.op("dve", "memset", w=[("mxs", i2)], ap=mx2[:, i2, 2:3], constant=0.0)
                g.act(sc2[:, i2, :], pz, AF.Exp, r=[pzk, ("mxa", i2), ("mxs", i2)], w=[("sc2", i2), ("mxs", i2)], bias=mx2[:, i2, 1:2], scale=scale,
                      accum_out=mx2[:, i2, 2:3])

            def xB(n):
                tb, j = divmod(n, 4)
                i2 = n % 2
                tbs = slice(tb * 128, (tb + 1) * 128)
                g.op("dve", "reciprocal", r=[("mxs", i2), ("sc2", i2)], w=[("mxr", i2)], out=mx2[:, i2, 3:4], in_=mx2[:, i2, 2:3])
                g.op("dve", "tensor_scalar", r=[("sc2", i2), ("mxr", i2)], w=["pb_t"], out=pb_t[:], in0=sc2[:, i2, :], scalar1=mx2[:, i2, 3:4], scalar2=None, op0=ALU.mult)
                for mb in range(2):
                    g.tr(PT[:, mb * 128:(mb + 1) * 128], pb_t[:, mb * 128:(mb + 1) * 128], identb[:], r=["pb_t", "identb"], w=["PT"])
                g.act(pT_t[:].rearrange("p a b -> p (a b)"), PT[:, 0:256], AF.Copy, r=["PT"], w=["pT_t"])
                for mb in range(2):
                    g.mm(PD[:, j * 128:(j + 1) * 128], Vt[:, mb, j * 128:(j + 1) * 128], pT_t[:, mb, :], start=(mb == 0), stop=(mb == 1),
                         r=["Vt", "pT_t"], w=["PD"])
                if j == 3:
                    g.op("dve", "tensor_copy", r=["PD"], w=["xo"], out=xo[:, :, tbs], in_=PD[:].rearrange("p (a b) -> p a b", a=4))

            xA(0)
            for n in range(64):
                if n + 1 < 64:
                    xA(n + 1)
                xB(n)

            dump("xq", xq[:].rearrange("p a b -> p (a b)"), ["xq"], [128, 4 * S], BF16)
            dump("xo", xo[:].rearrange("p a b -> p (a b)"), ["xo"], [128, 4 * S], BF16)

            def getw_xo(ob):
                if ob % 4 == 0:
                    getw_xo.cur = ws.get()
                wt, wk = getw_xo.cur
                return [(wt, wk, (ob % 4) * 4 + kc, kc) for kc in range(4)]
            lin_resid(xo, "xo", 4, 0, S, getw_xo)
            if not DO_F:
                continue

            hF = g.at("hF%d" % l, [128, KC, 1024], BF16, big_off)
            actT = g.at("actT%d" % l, [128, 44, 1024], BF16, big_off + 32768)
            cg = g.at("cg%d" % l, [128, 2, 1024], F32, xt_off)
            cu = g.at("cu%d" % l, [128, 2, 1024], F32, xt_off + 8192)
            hc = g.at("hc%d" % l, [128, 2, 4], F32, xt_off + 16384)
            xr2 = g.at("xr2%d" % l, [128, 2, 1024], F32, xt_off + 16384 + 64)
            for tt in range(2):
                T0 = tt * 1024
                rmsnorm(xw, "xw", lambda kc: gcol(l, 3, kc), hF, T0, T0 + 1024, "hF",
                        extra_w=[("cg", 0), ("cg", 1), ("cu", 0), ("cu", 1), ("xr2", 0), ("xr2", 1)])
                for fb in range(44):
                    for gi, (cbuf, ckey, blk) in enumerate(((cg, "cg", fb), (cu, "cu", 44 + fb))):
                        wt, wk = ws.get()
                        pa = (PA, PB)[gi]; pk = ("PAB", gi, 0); pkb = ("PAB", gi, 1)
                        for hf in range(2):
                            for kc in range(KC):
                                g.mm(pa[:, hf * 512:(hf + 1) * 512], wt[:, kc, :], hF[:, kc, hf * 512:(hf + 1) * 512],
                                     start=(kc == 0), stop=(kc == KC - 1), r=[wk, "hF"], w=[(pk, pkb)[hf]])
                        c_ = cbuf[:, fb % 2, :]; ck = (ckey, fb % 2)
                        w0 = fconv_t[:, blk * 4:blk * 4 + 1]; w1 = fconv_t[:, blk * 4 + 1:blk * 4 + 2]
                        w2 = fconv_t[:, blk * 4 + 2:blk * 4 + 3]; bb = fconv_t[:, blk * 4 + 3:blk * 4 + 4]
                        g.act(c_, pa[:], AF.Identity, r=[pk, pkb, "fconv"], w=[ck], scale=w2, bias=bb)
                        g.op("dve", "scalar_tensor_tensor", r=[pk, pkb, "fconv", ck], w=[ck], out=c_[:, 1:1024], in0=pa[:, 0:1023], scalar=w1,
                             in1=c_[:, 1:1024], op0=ALU.mult, op1=ALU.add)
                        g.op("dve", "scalar_tensor_tensor", r=[pk, pkb, "fconv", ck], w=[ck], out=c_[:, 2:1024], in0=pa[:, 0:1022], scalar=w0,
                             in1=c_[:, 2:1024], op0=ALU.mult, op1=ALU.add)
                        if tt == 1:
                            hh = halo[:, blk, :]
                            g.op("dve", "scalar_tensor_tensor", r=["halo", "fconv", ck], w=[ck], out=c_[:, 0:2], in0=hh, scalar=w0,
                                 in1=c_[:, 0:2], op0=ALU.mult, op1=ALU.add)
                            g.op("dve", "scalar_tensor_tensor", r=["halo", "fconv", ck], w=[ck], out=c_[:, 0:1], in0=hh[:, 1:2], scalar=w1,
                                 in1=c_[:, 0:1], op0=ALU.mult, op1=ALU.add)
                        else:
                            g.act(halo[:, blk, :], pa[:, 1022:1024], AF.Copy, r=[pkb], w=["halo"])
                    g.act(cg[:, fb % 2, :], cg[:, fb % 2, :], AF.Silu, r=[("cg", fb % 2)], w=[("cg", fb % 2)])
                    g.op("dve", "tensor_tensor", r=[("cg", fb % 2), ("cu", fb % 2)], w=[("actT", fb)], out=actT[:, fb, :],
                         in0=cg[:, fb % 2, :], in1=cu[:, fb % 2, :], op=ALU.mult)
                if tt == 0:
                    dump("hF", hF[:].rearrange("p a b -> p (a b)"), ["hF"], [128, KC * 1024], BF16)
                    dump("actT", actT[:].rearrange("p a b -> p (a b)"), ["actT"], [128, 44 * 1024], BF16)
                for ob in range(16):
                    wl = []
                    for kg in range(4):
                        wt, wk = ws.get(hold=4)
                        wl += [(wt, wk, i, kg * 11 + i) for i in range(11)]
                    xs = xr2[:, ob % 2, :]; xk = ("xr2", ob % 2)
                    g.dma("sp", xs, xw[ob, :, T0:T0 + 1024], "xr", r=[("xw", ob)], w=[xk] + [("xt", kc_) for kc_ in range(8, 13)])
                    pa = (PA, PB)[ob % 2]; pk = ("PAB", ob % 2, 0); pkb = ("PAB", ob % 2, 1)
                    for hf in range(2):
                        for i, (wt, wk, ki, ci) in enumerate(wl):
                            g.mm(pa[:, hf * 512:(hf + 1) * 512], wt[:, ki, :], actT[:, ci, hf * 512:(hf + 1) * 512],
                                 start=(i == 0), stop=(i == 43), r=[wk, ("actT", ci)], w=[(pk, pkb)[hf]])
                    g.op("dve", "tensor_tensor", r=[pk, pkb, xk], w=[xk], out=xs, in0=xs, in1=pa[:], op=ALU.add)
                    g.dma("sp", xw[ob, :, T0:T0 + 1024], xs, "xr", r=[xk], w=[("xw", ob)])

        if final:
            fT = g.at("fT", [128, KC, 512], F32, big_off)
            for t0 in range(0, S, 512):
                for kc in range(KC):
                    g.dma("sp", xt[:, kc, :], xw[kc, :, t0:t0 + 512], "x", r=[("xw", kc)], w=[("xt", kc)] + XT_ALIAS)
                    g.act(sqb[:, kc % 2, :], xt[:, kc, :], AF.Square, r=[("xt", kc)], w=[("sqb", kc % 2)])
                    g.mm(PD[:], onesb[:], sqb[:, kc % 2, :], start=(kc == 0), stop=(kc == KC - 1), r=["onesb", ("sqb", kc % 2)], w=["PD"])
                g.act(rstd[:], PD[:], AF.Ln, r=["PD"], w=["rstd"], bias=EPS, scale=1.0 / D)
                g.act(rstd[:], rstd[:], AF.Exp, r=["rstd"], w=["rstd"], scale=-0.5)
                for kc in range(KC):
                    g.op("dve", "scalar_tensor_tensor", r=[("xt", kc), "nrm", "rstd"], w=[("fT", kc)], out=fT[:, kc, :], in0=xt[:, kc, :],
                         scalar=nrm_t[:, nl * 64 + kc:nl * 64 + kc + 1], in1=rstd[:], op0=ALU.mult, op1=ALU.mult)
                    g.dma("sp", yout[kc, :, t0:t0 + 512], fT[:, kc, :], "yo", r=[("fT", kc)], w=[("yout", kc)])
            sc.rec("sp", lambda e: None, reads=["yout"])
        else:
            sc.rec("sp", lambda e: None, reads=["xw", "mixT"] + dbg_keys)
        sc.emit(es)
    return nc


def _consts():
    i = np.arange(128)
    ident = np.eye(128, dtype=np.float32)
    Lincl = (i[:, None] >= i[None, :]).astype(np.float32)
    Uincl = (i[:, None] <= i[None, :]).astype(np.float32)
    Ustr = (i[:, None] > i[None, :]).astype(np.float32)
    mSL = (i[:, None] > i[None, :]).astype(np.float32)
    mUI = (i[:, None] <= i[None, :]).astype(np.float32)
    ones = np.ones((128, 128), np.float32)
    t = np.arange(512)
    md = [((128 * d + i[:, None]) < t[None, :]).astype(np.float32) for d in range(4)]
    return np.ascontiguousarray(np.concatenate([ident, Lincl, Uincl, Ustr, mSL, mUI, ones] + md, axis=1))


def _blk(w, nk):
    K, C = w.shape
    return np.ascontiguousarray(w.reshape(K // 128, 128, C // 128, 128).transpose(2, 1, 0, 3))


def _col(v):
    return np.ascontiguousarray(v.reshape(-1, 128).T)


def prep_layers(inp, ls):
    f = lambda k: np.asarray(inp[k], dtype=np.float32)
    nl = len(ls)
    out = {}
    nrm = np.zeros((128, nl * 64 + 16), np.float32)
    for i, l in enumerate(ls):
        for j, k in enumerate(("mix_norm", "xattn_norm", "mem_norm", "ffn_norm")):
            nrm[:, i * 64 + j * 16:i * 64 + (j + 1) * 16] = _col(f(k)[l])
    nrm[:, nl * 64:] = _col(f("final_norm"))
    out["nrm"] = nrm
    w_in = f("w_in")
    out["w_in"] = np.stack([_blk(w_in[l][:, :7168], 16) for l in ls])
    out["w_ba"] = np.stack([np.ascontiguousarray(w_in[l][:, 7168:7184].reshape(16, 128, 16).transpose(1, 0, 2)) for l in ls])
    gc = f("gdn_conv")
    out["gconv"] = np.stack([np.ascontiguousarray(gc[l].reshape(4, 24, 128).transpose(2, 1, 0).reshape(128, 96)) for l in ls])
    gsm = np.zeros((nl, 128, 17), np.float32)
    for i, l in enumerate(ls):
        gsm[i, :, 0:8] = f("gdn_a_log")[l][None, :]
        gsm[i, :, 8:16] = f("gdn_dt_bias")[l][None, :]
        gsm[i, :, 16] = f("gdn_norm")[l]
    out["gsm"] = gsm
    out["w_out"] = np.stack([_blk(f("w_out")[l], 16) for l in ls])
    out["w_xq"] = np.stack([_blk(f("w_xq")[l], 16) for l in ls])
    out["w_xkv"] = np.stack([_blk(f("w_xkv")[l], 16) for l in ls])
    wxo = f("w_xo")
    out["w_xo"] = np.stack([np.ascontiguousarray(
        wxo[l].reshape(4, 128, 4, 4, 128).transpose(2, 1, 3, 0, 4).reshape(4, 128, 16, 128)) for l in ls])
    out["w_up"] = np.stack([_blk(f("w_up")[l], 16) for l in ls])
    fc = f("ffn_conv"); fb_ = f("ffn_conv_bias")
    fconv = np.zeros((nl, 128, NFB, 4), np.float32)
    for i, l in enumerate(ls):
        fconv[i, :, :, 0:3] = fc[l].reshape(3, NFB, 128).transpose(2, 1, 0)
        fconv[i, :, :, 3] = fb_[l].reshape(NFB, 128).T
    out["fconv"] = fconv.reshape(nl, 128, NFB * 4)
    wd = f("w_down")
    out["w_dn"] = np.stack([np.ascontiguousarray(
        wd[l].reshape(4, 11, 128, 16, 128).transpose(3, 0, 2, 1, 4)) for l in ls])
    out["cst"] = _consts()
    return out


_NC_CACHE = {}


def _get_nc(nl, final, dbg=False):
    key = (nl, final, dbg)
    if key not in _NC_CACHE:
        _NC_CACHE[key] = build(nl, final, dbg)
    return _NC_CACHE[key]


NCORES = 8
FUSED = True


def kernel(**inputs):
    x = np.asarray(inputs["x"], dtype=np.float32)
    mem = np.asarray(inputs["mem"], dtype=np.float32)
    B = x.shape[0]
    xT = [np.ascontiguousarray(x[b].T.reshape(KC, 128, S)) for b in range(B)]
    mT = [np.ascontiguousarray(mem[b].T.reshape(KC, 128, NMEM)) for b in range(B)]
    groups = [[0, 1, 2, 3]] if FUSED else [[0], [1], [2], [3]]
    cur = xT
    for gi, ls in enumerate(groups):
        final = gi == len(groups) - 1
        wts = prep_layers(inputs, ls)
        nc = _get_nc(len(ls), final)
        in_maps = []
        for c in range(NCORES):
            m = dict(wts)
            m["xin"] = cur[c % B]
            m["memT"] = mT[c % B]
            in_maps.append(m)
        res = run_bass_kernel_spmd(nc, in_maps, core_ids=list(range(NCORES)))
        key = "yout" if final else "xw"
        cur = [np.asarray(res.results[b][key]) for b in range(B)]
    out = np.stack([np.ascontiguousarray(cur[b].reshape(D, S).T) for b in range(B)])
    return out.astype(np.float32)
```

```python
import numpy as np
from contextlib import ExitStack
import concourse.bass as bass
import concourse.mybir as mybir
from concourse.bass_utils import run_bass_kernel_spmd

F32 = mybir.dt.float32
BF16 = mybir.dt.bfloat16
AF = mybir.ActivationFunctionType
ALU = mybir.AluOpType

S = 2048
D = 2048
KC = 16
NMEM = 256
DFF = 5632
NFB = 88
EPS = 1e-6
ENGS = ("pe", "act", "dve", "pool", "sp")


class Sched:
    def __init__(self, nc):
        self.nc = nc
        self.ops = {e: [] for e in ENGS}
        self.state = {}
        self.subs = {}
        self.seen = {e: {} for e in ENGS}
        self.dma_count = {}

    @staticmethod
    def _norm(k):
        if isinstance(k, tuple):
            return (k[0], k[1:] if len(k) > 1 else None)
        return (k, None)

    def _conf(self, base, sub):
        if sub is None:
            return [(base, None)] + [(base, s) for s in self.subs.get(base, ())]
        return [(base, sub), (base, None)]

    def rec(self, eng, fn, reads=(), writes=(), dma_sem=None):
        deps = {}
        if eng != "pe":
            pr = [k for k in reads if (k[0] if isinstance(k, tuple) else k) in ("PAB", "PC", "PD", "PT")]
            if pr:
                reads = [k for k in reads if k not in pr]
                writes = list(writes) + pr

        def add(pid, raw=False):
            if pid is None:
                return
            stream, idx = pid
            if stream == eng and (not raw or eng in ("pe", "sp")):
                return
            if deps.get(stream, -1) < idx:
                deps[stream] = idx

        rk = [self._norm(k) for k in reads]
        wk = [self._norm(k) for k in writes]
        for base, sub in rk:
            for ck in self._conf(base, sub):
                st = self.state.get(ck)
                if st is not None:
                    add(st[0], True)
        for base, sub in wk:
            for ck in self._conf(base, sub):
                st = self.state.get(ck)
                if st is not None:
                    add(st[0])
                    for s_, i_ in st[1].items():
                        add((s_, i_))
        waits = []
        seen = self.seen[eng]
        for stream, idx in deps.items():
            if isinstance(stream, tuple):
                cnt = self.dma_count[stream[1]]
                if seen.get(stream, -1) >= cnt:
                    continue
                seen[stream] = cnt
                waits.append((stream, cnt))
            else:
                if seen.get(stream, -1) >= idx:
                    continue
                seen[stream] = idx
                self.ops[stream][idx]["inc"] = True
                waits.append((stream, idx))
        if dma_sem is not None:
            c = self.dma_count.get(dma_sem, 0) + 1
            self.dma_count[dma_sem] = c
            pid = (("d", dma_sem), c)
        else:
            pid = (eng, len(self.ops[eng]))
        self.ops[eng].append({"fn": fn, "waits": waits, "inc": False, "dma": dma_sem})
        for base, sub in rk:
            st = self.state.setdefault((base, sub), [None, {}])
            if sub is not None:
                self.subs.setdefault(base, set()).add(sub)
            if st[1].get(pid[0], -1) < pid[1]:
                st[1][pid[0]] = pid[1]
        for base, sub in wk:
            if sub is None:
                for s in self.subs.get(base, ()):
                    self.state.pop((base, s), None)
                self.subs[base] = set()
            else:
                self.subs.setdefault(base, set()).add(sub)
            self.state[(base, sub)] = [pid, {}]

    def emit(self, es):
        nc = self.nc
        sems = {}
        for e in ("pe", "act", "dve", "pool"):
            sems[e] = es.enter_context(nc.semaphore("s_" + e))
        for name in self.dma_count:
            sems[("d", name)] = es.enter_context(nc.semaphore("d_" + name))
        cum = {}
        for e in ENGS:
            c = 0
            arr = []
            for op in self.ops[e]:
                if op["inc"]:
                    c += 1
                arr.append(c)
            cum[e] = arr
        block = es.enter_context(nc.Block())
        ops = self.ops

        def run(e, engh):
            for op in ops[e]:
                for stream, v in op["waits"]:
                    if isinstance(stream, tuple):
                        engh.wait_ge(sems[stream], 16 * v)
                    else:
                        engh.wait_ge(sems[stream], cum[stream][v])
                ins = op["fn"](engh)
                if ins is None:
                    continue
                if op["dma"] is not None:
                    ins.then_inc(sems[("d", op["dma"])], 16)
                elif op["inc"]:
                    ins.then_inc(sems[e], 1)

        @block.tensor
        def _(t):
            run("pe", t)

        @block.scalar
        def _(t):
            run("act", t)

        @block.vector
        def _(t):
            run("dve", t)

        @block.gpsimd
        def _(t):
            run("pool", t)

        @block.sync
        def _(t):
            run("sp", t)


class Gen:
    def __init__(self, nc, es):
        self.nc = nc
        self.es = es
        self.sc = Sched(nc)
        slab = nc.alloc_sbuf_tensor("slab", [128, 207000], mybir.dt.uint8)
        self.base = nc.lookup_mloc(slab).addr
        self.nps = 0

    def at(self, name, shape, dt, off):
        return self.nc.alloc_sbuf_tensor_at(name, shape, dt, offset=self.base + off)

    def mm(self, out, lhsT, rhs, start=True, stop=True, r=(), w=()):
        self.sc.rec("pe", lambda e: e.matmul(out, lhsT, rhs, start=start, stop=stop), reads=r, writes=w)

    def tr(self, out, in_, ident, r=(), w=()):
        self.sc.rec("pe", lambda e: e.transpose(out, in_, ident), reads=r, writes=w)

    def op(self, eng, method, r=(), w=(), **kw):
        self.sc.rec(eng, lambda e: getattr(e, method)(**kw), reads=r, writes=w)

    def act(self, out, in_, func, r=(), w=(), **kw):
        self.sc.rec("act", lambda e: e.activation(out=out, in_=in_, func=func, **kw), reads=r, writes=w)

    def dma(self, eng, out, in_, sem, r=(), w=()):
        self.sc.rec(eng, lambda e: e.dma_start(out=out, in_=in_), reads=r, writes=w, dma_sem=sem)


class WStream:
    def __init__(self, g, tiles):
        self.g = g
        self.tiles = tiles
        self.plan = []
        self.loaded = 0
        self.used = 0

    def add(self, ap, nk=16):
        self.plan.append((ap, nk))

    def get(self, hold=1):
        n = len(self.tiles)
        while self.loaded < len(self.plan) and self.loaded < self.used + n - (hold - 1):
            ap, nk = self.plan[self.loaded]
            i = self.loaded % n
            self.g.dma("pool", self.tiles[i][:, 0:nk, :], ap, "w%d" % i, w=[("wb", i)])
            self.loaded += 1
        i = self.used % n
        self.used += 1
        return self.tiles[i], ("wb", i)


def build(nl, final, dbg=False, cfg=None):
    cfg = cfg or {}
    nc = bass.Bass("TRN2", target_bir_lowering=False)
    dr = lambda name, shape, dt, kind="ExternalInput": nc.dram_tensor(name, shape, dt, kind=kind).ap()
    xin = dr("xin", [KC, 128, S], F32)
    memT = dr("memT", [KC, 128, NMEM], F32)
    cst = dr("cst", [128, 7 * 128 + 4 * 512], F32)
    nrm = dr("nrm", [128, nl * 64 + 16], F32)
    w_in = dr("w_in", [nl, 56, 128, 16, 128], F32)
    w_ba = dr("w_ba", [nl, 128, 16, 16], F32)
    gconv = dr("gconv", [nl, 128, 24 * 4], F32)
    gsm = dr("gsm", [nl, 128, 17], F32)
    w_out = dr("w_out", [nl, 16, 128, 16, 128], F32)
    w_xq = dr("w_xq", [nl, 4, 128, 16, 128], F32)
    w_xkv = dr("w_xkv", [nl, 8, 128, 16, 128], F32)
    w_xo = dr("w_xo", [nl, 4, 128, 16, 128], F32)
    w_up = dr("w_up", [nl, NFB, 128, 16, 128], F32)
    fconv = dr("fconv", [nl, 128, NFB * 4], F32)
    w_dn = dr("w_dn", [nl, 16, 4, 128, 11, 128], F32)
    okind = "ExternalOutput"
    xw = dr("xw", [KC, 128, S], F32, kind=okind if not final else "Internal")
    yout = dr("yout", [KC, 128, S], F32, kind=okind) if final else None
    mixT = dr("mixT", [KC, 128, S], BF16, kind=okind if dbg else "Internal")
    dbg_keys = []

    def dump(name, ap, keys, shape, dt):
        if not dbg or not cfg.get("dump"):
            return
        if name not in cfg["dump"]:
            return
        t = dr("D_" + name, shape, dt, kind=okind)
        g.dma("sp", t, ap, "dbg", r=keys, w=["D_" + name])
        dbg_keys.append("D_" + name)

    NSB = cfg.get("sb", 8); NGD = cfg.get("gdn", 8); DO_OUT = cfg.get("out", True)
    DO_X = cfg.get("xattn", True); DO_F = cfg.get("ffn", True)
    es = ExitStack()
    with es:
        g = Gen(nc, es)
        sc = g.sc
        o = 0
        cst_t = g.at("cst_t", [128, 7 * 128 + 4 * 512], F32, o); o += (7 * 128 + 4 * 512) * 4
        ident = cst_t[:, 0:128]; Lincl = cst_t[:, 128:256]; Uincl = cst_t[:, 256:384]
        Ustr = cst_t[:, 384:512]; mSL = cst_t[:, 512:640]; mUI = cst_t[:, 640:768]; ones = cst_t[:, 768:896]
        maskd = [cst_t[:, 896 + 512 * d: 896 + 512 * (d + 1)] for d in range(4)]
        identb = g.at("identb", [128, 128], BF16, o); o += 256
        nrm_t = g.at("nrm_t", [128, nl * 64 + 16], F32, o); o += (nl * 64 + 16) * 4
        gconv_t = g.at("gconv_t", [128, 96], F32, o); o += 384
        gsm_t = g.at("gsm_t", [128, 17], F32, o); o += 68 + 28
        fconv_t = g.at("fconv_t", [128, NFB * 4], F32, o); o += NFB * 16
        halo = g.at("halo", [128, NFB, 2], F32, o); o += NFB * 8
        rstd = g.at("rstd", [128, 512], F32, o); o += 2048
        sq = g.at("sq", [128, 2, 512], F32, o); o += 4096
        sqb = g.at("sqb", [128, 2, 512], BF16, o); o += 2048
        onesb = g.at("onesb", [128, 128], BF16, o); o += 256
        NWB = 6
        wbt = g.at("wbt", [128, NWB, 16, 128], BF16, o); o += NWB * 4096
        wtiles = [wbt[:, i] for i in range(NWB)]
        xt_off = o
        xt = g.at("xt", [128, KC, 512], F32, o); o += 32768
        big_off = o
        hT = g.at("hT", [128, KC, S], BF16, o)
        ar = o + 65536
        o += 122880
        assert o <= 207000, o
        ws = WStream(g, wtiles)

        PA = es.enter_context(nc.psum_tensor("PA", [128, 1024], F32))
        PB = es.enter_context(nc.psum_tensor("PB", [128, 1024], F32))
        PC = es.enter_context(nc.psum_tensor("PC", [128, 1024], F32))
        PD = es.enter_context(nc.psum_tensor("PD", [128, 512], F32))
        PTf = es.enter_context(nc.psum_tensor("PT", [128, 512], F32))
        PT = PTf.bitcast(BF16)

        g.dma("sp", cst_t[:], cst, "c", w=["cst"])
        g.dma("sp", nrm_t[:], nrm, "c", w=["nrm"])
        g.op("dve", "tensor_copy", r=["cst"], w=["identb"], out=identb[:], in_=ident)
        g.op("dve", "tensor_copy", r=["cst"], w=["onesb"], out=onesb[:], in_=ones)
        for kc in range(KC):
            g.dma("sp", xw[kc], xin[kc], "xcp", w=[("xw", kc)])

        def gcol(l, which, kc):
            c = l * 64 + which * 16 + kc
            return nrm_t[:, c:c + 1]

        XT_ALIAS = [("cg", 0), ("cg", 1), ("cu", 0), ("cu", 1), ("xr2", 0), ("xr2", 1), "gq", "gk", "gv", "gz"]

        def rmsnorm(src, srckey, gc, dst, T0, T1, dkey, extra_w=()):
            for t0 in range(T0, T1, 512):
                for kc in range(KC):
                    g.dma("sp", xt[:, kc, :], src[kc, :, t0:t0 + 512], "x", r=[(srckey, kc)], w=[("xt", kc)] + XT_ALIAS)
                    g.act(sqb[:, kc % 2, :], xt[:, kc, :], AF.Square, r=[("xt", kc)], w=[("sqb", kc % 2)])
                    g.mm(PD[:], onesb[:], sqb[:, kc % 2, :], start=(kc == 0), stop=(kc == KC - 1),
                         r=["onesb", ("sqb", kc % 2)], w=["PD"])
                g.act(rstd[:], PD[:], AF.Ln, r=["PD"], w=["rstd"], bias=EPS, scale=1.0 / D)
                g.act(rstd[:], rstd[:], AF.Exp, r=["rstd"], w=["rstd"], scale=-0.5)
                for kc in range(KC):
                    g.op("dve", "scalar_tensor_tensor", r=[("xt", kc), "nrm", "rstd"], w=[(dkey, kc, t0)],
                         out=dst[:, kc, t0 - T0:t0 - T0 + 512], in0=xt[:, kc, :], scalar=gc(kc), in1=rstd[:],
                         op0=ALU.mult, op1=ALU.mult)

        def proj(src, skey, T, evac):
            wt, wk = ws.get()
            for tt in range(T // 512):
                pa = (PA, PB)[tt % 2]
                half = (tt // 2) % 2
                pv = pa[:, half * 512:(half + 1) * 512]
                pk = ("PAB", tt % 2, half)
                for kc in range(KC):
                    g.mm(pv, wt[:, kc, :], src[:, kc, tt * 512:(tt + 1) * 512], start=(kc == 0), stop=(kc == KC - 1),
                         r=[wk, skey], w=[pk])
                evac(tt, pv, pk)

        def residual_linear(wplan_nk, nkc, src, skey, T0, T):
            raise NotImplementedError

        for l in range(nl):
            g.dma("sp", gconv_t[:], gconv[l], "c", w=["gconv"])
            g.dma("sp", gsm_t[:], gsm[l], "c", w=["gsm"])
            g.dma("sp", fconv_t[:], fconv[l], "c", w=["fconv"])
            for hd in range(NSB):
                for j in range(3):
                    ws.add(w_in[l, j * 8 + hd])
            for hd in range(NGD):
                for j in range(4):
                    ws.add(w_in[l, 24 + j * 8 + hd])
            for ob in range(16 if DO_OUT else 0):
                ws.add(w_out[l, ob])
            for j in range(8 if DO_X else 0):
                ws.add(w_xkv[l, j])
            for j in range(4 if DO_X else 0):
                ws.add(w_xq[l, j])
            for j in range(4 if DO_X else 0):
                ws.add(w_xo[l, j])
            for tt in range(2 if DO_F else 0):
                for fb in range(44):
                    ws.add(w_up[l, fb]); ws.add(w_up[l, 44 + fb])
                for ob in range(16):
                    for kg in range(4):
                        ws.add(w_dn[l, ob, kg], 11)

            rmsnorm(xw, "xw", lambda kc: gcol(l, 0, kc), hT, 0, S, "hT")

            a = ar
            wba_t = g.at("wba_t%d" % l, [128, 16, 16], BF16, a); a += 512
            bg = g.at("bg%d" % l, [128, 16, 16], F32, a); a += 1024
            beta_t = g.at("beta%d" % l, [128, 16, 8], F32, a); a += 512
            nbeta_t = g.at("nbeta%d" % l, [128, 16, 8], F32, a); a += 512
            g_t = g.at("g_t%d" % l, [128, 16, 8], F32, a); a += 512
            nA = g.at("nA%d" % l, [128, 8], F32, a); a += 32
            gates_end = a
            g.dma("pool", wba_t[:], w_ba[l], "wba", w=["wba", "actT"])
            for blk in range(16):
                for kc in range(KC):
                    g.mm(PD[:, blk * 16:(blk + 1) * 16], hT[:, kc, blk * 128:(blk + 1) * 128], wba_t[:, kc, :],
                         start=(kc == 0), stop=(kc == KC - 1), r=["hT", "wba"], w=["PD"])
            g.op("dve", "tensor_copy", r=["PD"], w=["bg"], out=bg[:].rearrange("p a b -> p (a b)"), in_=PD[:, 0:256])
            g.act(beta_t[:], bg[:, :, 0:8], AF.Exp, r=["bg"], w=["beta"], scale=-1.0)
            g.op("dve", "tensor_scalar_add", r=["beta"], w=["beta"], out=beta_t[:], in0=beta_t[:], scalar1=1.0)
            g.op("dve", "reciprocal", r=["beta"], w=["beta"], out=beta_t[:], in_=beta_t[:])
            g.op("dve", "tensor_scalar_mul", r=["beta"], w=["nbeta"], out=nbeta_t[:], in0=beta_t[:], scalar1=-1.0)
            g.act(nA[:], gsm_t[:, 0:8], AF.Exp, r=["gsm"], w=["nA"])
            g.op("dve", "tensor_scalar_mul", r=["nA"], w=["nA"], out=nA[:], in0=nA[:], scalar1=-1.0)
            for blk in range(16):
                g.op("dve", "tensor_tensor", r=["bg", "gsm"], w=["g_t"], out=g_t[:, blk, :], in0=bg[:, blk, 8:16],
                     in1=gsm_t[:, 8:16], op=ALU.add)
            g.act(g_t[:], g_t[:], AF.Exp, r=["g_t"], w=["g_t"])
            g.act(g_t[:], g_t[:], AF.Ln, r=["g_t"], w=["g_t"], bias=1.0)
            for blk in range(16):
                g.op("dve", "tensor_tensor", r=["g_t", "nA"], w=["g_t"], out=g_t[:, blk, :], in0=g_t[:, blk, :],
                     in1=nA[:], op=ALU.mult)

            a = gates_end
            a = (a + 63) // 64 * 64
            qTs = [g.at("qT%d_%d" % (l, i), [128, S], BF16, a + 4096 * i) for i in range(2)]; a += 8192
            kTs = [g.at("kT%d_%d" % (l, i), [128, S], BF16, a + 4096 * i) for i in range(2)]; a += 8192
            vtoks = [g.at("vtok%d_%d" % (l, i), [128, 16, 128], BF16, a + 4096 * i) for i in range(2)]; a += 8192
            vT = g.at("vT%d" % l, [128, S], BF16, a); a += 4096
            ebuf = g.at("ebuf%d" % l, [128, 3, 512], F32, a); a += 6144
            spbuf = g.at("spbuf%d" % l, [128, 3, 512], F32, a); a += 6144
            ecbuf = g.at("ecbuf%d" % l, [128, 2, 512], F32, a); a += 4096
            racc = g.at("racc%d" % l, [128, 512], F32, a); a += 2048
            wbuf = g.at("wbuf%d" % l, [128, 2, 512], BF16, a); a += 2048
            obuf = g.at("obuf%d" % l, [128, 2, 512], BF16, a); a += 2048
            assert a <= big_off + 122880
            scale = 128.0 ** -0.5

            def proj_closures(hd):
                par = hd % 2
                cl = []
                for j, dst, key in ((0, qTs[par], "qT%d" % par), (1, kTs[par], "kT%d" % par), (2, vT, "vT")):
                    holder = {}
                    for tt in range(4):
                        def c(tt=tt, dst=dst, key=key, holder=holder):
                            if tt == 0:
                                holder["w"] = ws.get()
                            wt, wk = holder["w"]
                            pv = (PA, PB)[tt % 2][:, 0:512]; pk = ("PAB", tt % 2, 0)
                            for kc in range(KC):
                                g.mm(pv, wt[:, kc, :], hT[:, kc, tt * 512:(tt + 1) * 512], start=(kc == 0), stop=(kc == KC - 1),
                                     r=[wk, "hT"], w=[pk])
                            if tt % 2 == 0:
                                g.act(dst[:, tt * 512:(tt + 1) * 512], pv, AF.Copy, r=[pk], w=[(key, tt)])
                            else:
                                g.op("dve", "tensor_copy", r=[pk], w=[(key, tt)], out=dst[:, tt * 512:(tt + 1) * 512], in_=pv)
                        cl.append(c)
                for half in range(2):
                    def c(half=half, par=par):
                        for b in range(8):
                            blk = half * 8 + b
                            g.tr(PT[:, b * 128:(b + 1) * 128], vT[:, blk * 128:(blk + 1) * 128], identb[:],
                                 r=[("vT", blk // 4), "identb"], w=["PT"])
                        g.op("dve", "tensor_copy", r=["PT"], w=[("vtok%d" % par, half)],
                             out=vtoks[par][:, half * 8:(half + 1) * 8, :].rearrange("p a b -> p (a b)"), in_=PT[:])
                    cl.append(c)
                return cl

            for c_ in (proj_closures(0) if NSB > 0 else []):
                c_()
            for hd in range(NSB):
                par = hd % 2
                qT, kT, vtok = qTs[par], kTs[par], vtoks[par]
                qTn, kTn, vtn = "qT%d" % par, "kT%d" % par, "vtok%d" % par
                nxt = proj_closures(hd + 1) if hd + 1 < NSB else []
                its = []
                for qt in range(4):
                    nkb = 4 * qt + 4
                    for it, kb in enumerate(range(nkb - 1, -1, -1)):
                        its.append((qt, it, kb))

                def stA(n):
                    qt, it, kb = its[n]
                    i2 = n % 2; i3 = n % 3
                    pz = PC[:, i2 * 512:(i2 + 1) * 512]; pzk = ("PC", i2)
                    g.mm(pz, kT[:, kb * 128:(kb + 1) * 128], qT[:, qt * 512:(qt + 1) * 512], r=[(kTn, kb // 4), (qTn, qt)], w=[pzk])
                    e_ = ebuf[:, i3, :]; sp_ = spbuf[:, i3, :]
                    g.act(e_, pz, AF.Exp, r=[pzk], w=[("e", i3)], scale=scale)
                    g.act(sp_, e_, AF.Ln, r=[("e", i3)], w=[("sp", i3)], bias=1.0)
                    if kb >= 4 * qt:
                        md = maskd[kb - 4 * qt]
                        g.op("dve", "tensor_tensor", r=[("sp", i3), "cst"], w=[("sp", i3)], out=sp_, in0=sp_, in1=md, op=ALU.mult)
                        g.op("dve", "tensor_tensor", r=[("e", i3), "cst"], w=[("e", i3)], out=e_, in0=e_, in1=md, op=ALU.mult)

                def stB(n):
                    qt, it, kb = its[n]
                    i2 = n % 2; i3 = n % 3
                    e_ = ebuf[:, i3, :]; sp_ = spbuf[:, i3, :]; ec_ = ecbuf[:, i2, :]
                    pcs = (PA, PB)[i2][:, 512:1024]; pck = ("PAB", i2, 1)
                    g.mm(pcs, Lincl, sp_, start=True, stop=(it == 0), r=["cst", ("sp", i3)], w=[pck])
                    if it > 0:
                        g.mm(pcs, ones, racc[:], start=False, stop=True, r=["cst", "racc"], w=[pck])
                    g.act(ec_, pcs, AF.Exp, r=[pck], w=[("ec", i2)], scale=-1.0)
                    g.op("dve", "tensor_tensor", r=[("e", i3), ("ec", i2)], w=[("wbuf", i2)], out=wbuf[:, i2, :], in0=e_, in1=ec_, op=ALU.mult)
                    if it == 0:
                        g.op("dve", "tensor_copy", r=[("sp", i3)], w=["racc"], out=racc[:], in_=sp_)
                    elif kb > 0:
                        g.op("dve", "tensor_tensor", r=[("sp", i3), "racc"], w=["racc"], out=racc[:], in0=racc[:], in1=sp_, op=ALU.add)

                def stC(n):
                    qt, it, kb = its[n]
                    i2 = n % 2
                    g.mm(PD[:], vtok[:, kb, :], wbuf[:, i2, :], start=(it == 0), stop=(kb == 0),
                         r=[(vtn, kb // 8), ("wbuf", i2)], w=["PD"])
                    if kb == 0:
                        ob_ = obuf[:, qt % 2, :]
                        g.act(ob_, PD[:], AF.Copy, r=["PD"], w=[("obuf", qt % 2)])
                        g.dma("sp", mixT[hd, :, qt * 512:(qt + 1) * 512], ob_, "mo", r=[("obuf", qt % 2)], w=[("mixT", hd)])

                NI = len(its)
                for n in range(NI + 2):
                    if n < NI:
                        stA(n)
                    if 0 <= n - 1 < NI:
                        stB(n - 1)
                    if 0 <= n - 2 < NI:
                        stC(n - 2)
                    if nxt and n % 3 == 2:
                        nxt.pop(0)()
                while nxt:
                    nxt.pop(0)()

            a = gates_end
            a = (a + 63) // 64 * 64
            gates_end_al = a
            gq = xt[:].rearrange("p a b -> p (a b)")
            gqT = gq[:, 0:S]; gkT = gq[:, S:2 * S]; gvT = gq[:, 2 * S:3 * S]; gzT = gq[:, 3 * S:4 * S]
            cb = g.at("cb%d" % l, [128, S + 8], F32, a); a += (S + 8) * 4
            ogT = g.at("ogT%d" % l, [128, S], F32, a); a += S * 4
            wide = {}
            for nm in ("Gm", "DecL", "DecT", "egcb", "Qa", "Qb", "Pa", "Pb", "X", "vb", "kbg", "ktail", "u", "wT", "ATm", "qdT"):
                wide[nm] = g.at(nm + str(l), [128, 4, 128], F32, a); a += 2048
            Sst = g.at("Sst%d" % l, [128, 2, 128], F32, a); a += 1024
            vnew = g.at("vnew%d" % l, [128, 128], F32, a); a += 512
            ecols = g.at("ecols%d" % l, [128, 4, 4], F32, a); a += 64
            bcol2 = g.at("bcol2%d" % l, [128, 4], F32, a); a += 64
            ecols1 = g.at("ecols1_%d" % l, [128, 4, 4], F32, a); a += 64
            ktail1 = g.at("ktail1_%d" % l, [128, 4, 128], F32, a); a += 2048
            assert a <= big_off + 122880, a
            cb_off = gates_end_al
            scan_in = {0: {"u": wide["u"], "wT": wide["wT"], "ATm": wide["ATm"], "qdT": wide["qdT"], "ktail": wide["ktail"], "ecols": ecols},
                       1: {"u": g.at("u1_%d" % l, [128, 4, 128], F32, cb_off), "wT": g.at("wT1_%d" % l, [128, 4, 128], F32, cb_off + 2048),
                           "ATm": g.at("ATm1_%d" % l, [128, 4, 128], F32, cb_off + 4096), "qdT": g.at("qdT1_%d" % l, [128, 4, 128], F32, cb_off + 6144),
                           "ktail": ktail1, "ecols": ecols1}}
            CB_ALIAS = ["u1", "wT1", "ATm1", "qdT1"]
            Gm, DecL, DecT, egcb = wide["Gm"], wide["DecL"], wide["DecT"], wide["egcb"]
            X, vb, kbg, ktail, u_, wT_, ATm, qdT = (wide[n] for n in ("X", "vb", "kbg", "ktail", "u", "wT", "ATm", "qdT"))
            fl = lambda t: t[:].rearrange("p a b -> p (a b)")
            class _Stop(Exception):
                pass

            def chk(stage):
                if cfg.get("gstop") == stage:
                    raise _Stop()

            def gdn_head(hd):
                for j, dst, key in ((0, gqT, "gq"), (1, gkT, "gk"), (2, gvT, "gv"), (3, gzT, "gz")):
                    if j < 3:
                        g.op("dve", "memset", w=["cb"] + CB_ALIAS, ap=cb[:, 0:3], constant=0.0)
                        def ev(tt, pv, pk):
                            if tt % 2 == 0:
                                g.act(cb[:, 3 + tt * 512:3 + (tt + 1) * 512], pv, AF.Copy, r=[pk], w=["cb"] + CB_ALIAS)
                            else:
                                g.op("dve", "tensor_copy", r=[pk], w=["cb"] + CB_ALIAS, out=cb[:, 3 + tt * 512:3 + (tt + 1) * 512], in_=pv)
                        proj(hT, "hT", S, ev)
                        fb = j * 8 + hd
                        wc = lambda i: gconv_t[:, fb * 4 + i:fb * 4 + i + 1]
                        g.act(dst, cb[:, 3:3 + S], AF.Identity, r=["cb", "gconv"], w=[key], scale=wc(3))
                        for i in range(3):
                            g.op("dve", "scalar_tensor_tensor", r=["cb", "gconv", key], w=[key], out=dst, in0=cb[:, i:i + S],
                                 scalar=wc(i), in1=dst, op0=ALU.mult, op1=ALU.add)
                        g.act(dst, dst, AF.Silu, r=[key], w=[key])
                    else:
                        def ev(tt, pv, pk):
                            g.act(gzT[:, tt * 512:(tt + 1) * 512], pv, AF.Silu, r=[pk], w=["gz"])
                        proj(hT, "hT", S, ev)
                chk(1)
                for dst, key, sc_ in ((gqT, "gq", 128.0 ** -0.5), (gkT, "gk", 1.0)):
                    for tt in range(4):
                        tsl = slice(tt * 512, (tt + 1) * 512)
                        g.act(sq[:, tt % 2, :], dst[:, tsl], AF.Square, r=[key], w=[("sq", tt % 2)])
                        g.mm(PD[:], ones, sq[:, tt % 2, :], r=["cst", ("sq", tt % 2)], w=["PD"])
                        g.act(rstd[:], PD[:], AF.Ln, r=["PD"], w=["rstd"], bias=EPS, scale=1.0)
                        g.act(rstd[:], rstd[:], AF.Exp, r=["rstd"], w=["rstd"], scale=-0.5)
                        g.op("dve", "scalar_tensor_tensor", r=[key, "rstd"], w=[key], out=dst[:, tsl], in0=dst[:, tsl],
                             scalar=sc_, in1=rstd[:], op0=ALU.mult, op1=ALU.mult)
                chk(2)
                g.op("dve", "memset", w=[("Sst", 0)], ap=Sst[:, 0, :], constant=0.0)

                def prep_stages(grp):
                    par = grp % 2
                    blks = [grp * 4 + b for b in range(4)]
                    si = scan_in[par]
                    u_, wT_, ATm, qdT, ktail, ecols = si["u"], si["wT"], si["ATm"], si["qdT"], si["ktail"], si["ecols"]
                    ku, kw, ka, kq, kk, ke = ("u%d" % par, "wT%d" % par, "ATm%d" % par, "qdT%d" % par, "ktail%d" % par, "ecols%d" % par)
                    st = []

                    def s1():
                        for b, n in enumerate(blks):
                            g.op("dve", "tensor_scalar", r=["cst", "g_t"], w=["Gm"], out=Gm[:, b, :], in0=Uincl,
                                 scalar1=g_t[:, n, hd:hd + 1], scalar2=None, op0=ALU.mult)
                    st.append(s1)

                    def s2():
                        for b, n in enumerate(blks):
                            g.mm(PA[:, b * 128:(b + 1) * 128], Gm[:, b, :], Ustr, r=["Gm", "cst"], w=[("PAB", 0, 0)])
                            g.mm(PA[:, 512 + b * 128:512 + (b + 1) * 128], Ustr, Gm[:, b, :], r=["Gm", "cst"], w=[("PAB", 0, 1)])
                            g.mm(PB[:, b * 128:(b + 1) * 128], ones, Gm[:, b, :], r=["Gm", "cst"], w=[("PAB", 1, 0)])
                            g.mm(PC[:, b * 4:b * 4 + 1], Gm[:, b, :], ones[:, 0:1], r=["Gm", "cst"], w=[("PC", 0)])
                            g.mm(PC[:, b * 4 + 1:b * 4 + 2], Ustr, g_t[:, n, hd:hd + 1], r=["g_t", "cst"], w=[("PC", 0)])
                            g.mm(PC[:, b * 4 + 2:b * 4 + 3], ones, g_t[:, n, hd:hd + 1], r=["g_t", "cst"], w=[("PC", 0)])
                    st.append(s2)

                    def s3():
                        g.act(fl(DecL), PA[:, 0:512], AF.Exp, r=[("PAB", 0, 0)], w=["DecL"])
                        g.act(fl(DecT), PA[:, 512:1024], AF.Exp, r=[("PAB", 0, 1)], w=["DecT"])
                        g.act(fl(egcb), PB[:, 0:512], AF.Exp, r=[("PAB", 1, 0)], w=["egcb"])
                        g.act(ecols[:].rearrange("p a b -> p (a b)"), PC[:, 0:16], AF.Exp, r=[("PC", 0)], w=[ke])
                    st.append(s3)

                    def s4():
                        for b, n in enumerate(blks):
                            g.op("dve", "tensor_tensor", r=["DecL", "cst"], w=["DecL"], out=DecL[:, b, :], in0=DecL[:, b, :], in1=mSL, op=ALU.mult)
                            g.op("dve", "tensor_tensor", r=["DecT", "cst"], w=["DecT"], out=DecT[:, b, :], in0=DecT[:, b, :], in1=mUI, op=ALU.mult)
                            g.op("dve", "tensor_tensor", r=[ke, "beta"], w=["bcol2"], out=bcol2[:, b:b + 1], in0=ecols[:, b, 0:1],
                                 in1=beta_t[:, n, hd:hd + 1], op=ALU.mult)
                        for b, n in enumerate(blks):
                            bs = slice(n * 128, (n + 1) * 128)
                            g.mm(PC[:, b * 128:(b + 1) * 128], gkT[:, bs], gkT[:, bs], r=["gk"], w=[("PC", 0)])
                    st.append(s4)

                    def s5():
                        Q = wide["Qa"]
                        for b, n in enumerate(blks):
                            g.op("dve", "scalar_tensor_tensor", r=[("PC", 0), "nbeta", "DecL"], w=["Qa"], out=Q[:, b, :],
                                 in0=PC[:, b * 128:(b + 1) * 128], scalar=nbeta_t[:, n, hd:hd + 1], in1=DecL[:, b, :],
                                 op0=ALU.mult, op1=ALU.mult)
                        for b in range(4):
                            g.mm(PC[:, 512 + b * 128:512 + (b + 1) * 128], Q[:, b, :], ident, r=["Qa", "cst"], w=[("PC", 1)])
                    st.append(s5)

                    def s6():
                        P = wide["Pa"]
                        g.act(fl(P), PC[:, 512:1024], AF.Copy, r=[("PC", 1)], w=["Pa"])
                        for b in range(4):
                            g.op("dve", "tensor_tensor", r=["Pa", "cst"], w=["X"], out=X[:, b, :], in0=P[:, b, :], in1=ident, op=ALU.add)
                    st.append(s6)
                    names = [("Qa", "Pa"), ("Qb", "Pb")]
                    for step in range(6):
                        qn, pn = names[step % 2]
                        qn2, pn2 = names[(step + 1) % 2]

                        def sa(qn=qn, pn=pn, qn2=qn2, pn2=pn2, step=step):
                            Q, P, Q2, P2 = wide[qn], wide[pn], wide[qn2], wide[pn2]
                            for b in range(4):
                                g.mm(PC[:, b * 128:(b + 1) * 128], P[:, b, :], Q[:, b, :], r=[qn, pn], w=[("PC", 0)])
                            if step < 5:
                                for b in range(4):
                                    g.mm(PC[:, 512 + b * 128:512 + (b + 1) * 128], Q[:, b, :], P[:, b, :], r=[qn, pn], w=[("PC", 1)])
                            g.act(fl(Q2), PC[:, 0:512], AF.Copy, r=[("PC", 0)], w=[qn2])
                            if step < 5:
                                g.op("dve", "tensor_copy", r=[("PC", 1)], w=[pn2], out=fl(P2), in_=PC[:, 512:1024])
                        st.append(sa)

                        def sb_(qn2=qn2):
                            Q2 = wide[qn2]
                            for b in range(4):
                                g.mm(PB[:, 512 + b * 128:512 + (b + 1) * 128], Q2[:, b, :], X[:, b, :], r=[qn2, "X"], w=[("PAB", 1, 1)])
                            g.op("dve", "tensor_tensor", r=[("PAB", 1, 1), "X"], w=["X"], out=fl(X), in0=fl(X), in1=PB[:, 512:1024], op=ALU.add)
                        st.append(sb_)

                    def s7():
                        for b, n in enumerate(blks):
                            bs = slice(n * 128, (n + 1) * 128)
                            g.mm(PA[:, b * 128:(b + 1) * 128], gkT[:, bs], ident, r=["gk", "cst"], w=[("PAB", 0, 0)])
                            g.mm(PA[:, 512 + b * 128:512 + (b + 1) * 128], gvT[:, bs], ident, r=["gv", "cst"], w=[("PAB", 0, 1)])
                            g.mm(PB[:, b * 128:(b + 1) * 128], gkT[:, bs], gqT[:, bs], r=["gk", "gq"], w=[("PAB", 1, 0)])
                    st.append(s7)

                    def s8():
                        for b, n in enumerate(blks):
                            g.op("dve", "tensor_scalar", r=[("PAB", 0, 0), "bcol2"], w=["kbg"], out=kbg[:, b, :], in0=PA[:, b * 128:(b + 1) * 128],
                                 scalar1=bcol2[:, b:b + 1], scalar2=None, op0=ALU.mult)
                            g.op("dve", "tensor_scalar", r=[("PAB", 0, 0), ke], w=[kk], out=ktail[:, b, :], in0=PA[:, b * 128:(b + 1) * 128],
                                 scalar1=ecols[:, b, 1:2], scalar2=None, op0=ALU.mult)
                        for b, n in enumerate(blks):
                            g.act(vb[:, b, :], PA[:, 512 + b * 128:512 + (b + 1) * 128], AF.Identity, r=[("PAB", 0, 1), "beta"], w=["vb"],
                                  scale=beta_t[:, n, hd:hd + 1])
                        g.op("dve", "tensor_tensor", r=[("PAB", 1, 0), "DecT"], w=[ka], out=fl(ATm), in0=fl(DecT), in1=PB[:, 0:512], op=ALU.mult)
                        g.op("dve", "tensor_tensor", r=["gq", "egcb"], w=[kq], out=fl(qdT), in0=gqT[:, grp * 512:(grp + 1) * 512], in1=fl(egcb), op=ALU.mult)
                    st.append(s8)

                    def s9():
                        for b, n in enumerate(blks):
                            g.mm(PA[:, b * 128:(b + 1) * 128], X[:, b, :], vb[:, b, :], r=["X", "vb"], w=[("PAB", 0, 0)])
                            g.mm(PA[:, 512 + b * 128:512 + (b + 1) * 128], kbg[:, b, :], X[:, b, :], r=["X", "kbg"], w=[("PAB", 0, 1)])
                        g.act(fl(u_), PA[:, 0:512], AF.Copy, r=[("PAB", 0, 0)], w=[ku])
                        g.op("dve", "tensor_copy", r=[("PAB", 0, 1)], w=[kw], out=fl(wT_), in_=PA[:, 512:1024])
                    st.append(s9)
                    return st

                def scan_stages(grp):
                    par = grp % 2
                    blks = [grp * 4 + b for b in range(4)]
                    si = scan_in[par]
                    u_, wT_, ATm, qdT, ktail, ecols = si["u"], si["wT"], si["ATm"], si["qdT"], si["ktail"], si["ecols"]
                    ku, kw, ka, kq, kk, ke = ("u%d" % par, "wT%d" % par, "ATm%d" % par, "qdT%d" % par, "ktail%d" % par, "ecols%d" % par)
                    st = []
                    for b, n in enumerate(blks):
                        s0 = n % 2; s1_ = 1 - s0
                        Sc = Sst[:, s0, :]; Sn = Sst[:, s1_, :]

                        def c1(b=b, s0=s0, Sc=Sc):
                            g.mm(PD[:, 0:128], wT_[:, b, :], Sc, r=[kw, ("Sst", s0)], w=["PD"])
                            g.op("dve", "tensor_tensor", r=[ku, "PD"], w=["vnew"], out=vnew[:], in0=u_[:, b, :], in1=PD[:, 0:128], op=ALU.subtract)
                        st.append(c1)

                        def c2(b=b, n=n, s0=s0, s1_=s1_, Sc=Sc, Sn=Sn):
                            g.mm(PTf[:, 0:128], Sc, qdT[:, b, :], start=True, stop=False, r=[("Sst", s0), kq], w=["PT"])
                            g.mm(PTf[:, 0:128], vnew[:], ATm[:, b, :], start=False, stop=True, r=["vnew", ka], w=["PT"])
                            g.mm(PD[:, 128:256], ktail[:, b, :], vnew[:], r=[kk, "vnew"], w=["PD"])
                            g.op("dve", "scalar_tensor_tensor", r=[("Sst", s0), ke, "PD"], w=[("Sst", s1_)], out=Sn, in0=Sc,
                                 scalar=ecols[:, b, 2:3], in1=PD[:, 128:256], op0=ALU.mult, op1=ALU.add)
                            g.act(ogT[:, n * 128:(n + 1) * 128], PTf[:, 0:128], AF.Copy, r=["PT"], w=["ogT"])
                        st.append(c2)
                    return st

                prev = []
                for grp in range(5):
                    cur = prep_stages(grp) if grp < 4 else []
                    na, nb_ = len(cur), len(prev)
                    ia = ib = 0
                    while ia < na or ib < nb_:
                        if ia < na and (ib >= nb_ or ia * nb_ <= ib * na):
                            cur[ia](); ia += 1
                        else:
                            prev[ib](); ib += 1
                    prev = scan_stages(grp) if grp < 4 else []
                chk(7)
                for tt in range(4):
                    tsl = slice(tt * 512, (tt + 1) * 512)
                    g.act(sq[:, tt % 2, :], ogT[:, tsl], AF.Square, r=["ogT"], w=[("sq", tt % 2)])
                    g.mm(PD[:], ones, sq[:, tt % 2, :], r=["cst", ("sq", tt % 2)], w=["PD"])
                    g.act(rstd[:], PD[:], AF.Ln, r=["PD"], w=["rstd"], bias=EPS, scale=1.0 / 128)
                    g.act(rstd[:], rstd[:], AF.Exp, r=["rstd"], w=["rstd"], scale=-0.5)
                    g.op("dve", "scalar_tensor_tensor", r=["ogT", "gsm", "rstd"], w=["ogT"], out=ogT[:, tsl], in0=ogT[:, tsl],
                         scalar=gsm_t[:, 16:17], in1=rstd[:], op0=ALU.mult, op1=ALU.mult)
                    ob_ = obuf[:, tt % 2, :]
                    g.op("dve", "tensor_tensor", r=["ogT", "gz"], w=[("obuf", tt % 2)], out=ob_, in0=ogT[:, tsl], in1=gzT[:, tsl], op=ALU.mult)
                    g.dma("sp", mixT[8 + hd, :, tsl], ob_, "mo", r=[("obuf", tt % 2)], w=[("mixT", 8 + hd)])

            for hd in range(NGD):
                try:
                    gdn_head(hd)
                except _Stop:
                    break

            def resid_block(nblk_k, src, skey, T0, T, xtmp, xkey, wkc_list):
                pass

            xr = g.at("xr%d" % l, [128, 2, 1024], F32, ar)
            for kc in range(KC if DO_OUT else 0):
                g.dma("sp", hT[:, kc, :], mixT[kc], "mi", r=[("mixT", kc)], w=["hT"])

            def lin_resid(src, skey, ncin, T0, T, getw):
                for ob in range(16):
                    wl = getw(ob)
                    for th in range(T // 1024):
                        t0 = T0 + th * 1024
                        xs = xr[:, (ob + th) % 2, :]; xk = ("xr", (ob + th) % 2)
                        g.dma("sp", xs, xw[ob, :, t0:t0 + 1024], "xr", r=[("xw", ob)], w=[xk])
                        pa = (PA, PB)[(ob + th) % 2]; pk0 = ("PAB", (ob + th) % 2, 0); pk1 = ("PAB", (ob + th) % 2, 1)
                        for hf in range(2):
                            for i, (wt, wk, ki, ci) in enumerate(wl):
                                g.mm(pa[:, hf * 512:(hf + 1) * 512], wt[:, ki, :], src[:, ci, th * 1024 + hf * 512: th * 1024 + (hf + 1) * 512],
                                     start=(i == 0), stop=(i == len(wl) - 1), r=[wk, skey], w=[(pk0, pk1)[hf]])
                        g.op("dve", "tensor_tensor", r=[pk0, pk1, xk], w=[xk], out=xs, in0=xs, in1=pa[:], op=ALU.add)
                        g.dma("sp", xw[ob, :, t0:t0 + 1024], xs, "xr", r=[xk], w=[("xw", ob)])

            def getw_out(ob):
                wt, wk = ws.get()
                return [(wt, wk, kc, kc) for kc in range(KC)]
            if DO_OUT:
                lin_resid(hT, "hT", 16, 0, S, getw_out)
            if not DO_X:
                continue

            a = ar + 8192
            memn = g.at("memn%d" % l, [128, KC, NMEM], BF16, a); a += 8192
            KT = g.at("KT%d" % l, [128, 4, NMEM], BF16, a); a += 2048
            Vt = g.at("Vt%d" % l, [128, 2, 512], BF16, a); a += 2048
            xq = g.at("xq%d" % l, [128, 4, S], BF16, a); a += 16384
            xo = g.at("xo%d" % l, [128, 4, S], BF16, a); a += 16384
            pb_t = g.at("pb_t%d" % l, [128, NMEM], BF16, a); a += 512
            pT_t = g.at("pT_t%d" % l, [128, 2, 128], BF16, a); a += 512
            mx = g.at("mx%d" % l, [128, 4], F32, a); a += 64
            mx2 = g.at("mx2_%d" % l, [128, 2, 4], F32, a); a += 64
            sc2 = g.at("sc2_%d" % l, [128, 2, NMEM], F32, a); a += 2048
            assert a <= big_off + 122880
            for kc in range(KC):
                g.dma("sp", xt[:, kc, 0:NMEM], memT[kc], "x", w=[("xt", kc)] + XT_ALIAS)
                g.act(sqb[:, kc % 2, 0:NMEM], xt[:, kc, 0:NMEM], AF.Square, r=[("xt", kc)], w=[("sqb", kc % 2)])
                g.mm(PD[:, 0:NMEM], onesb[:], sqb[:, kc % 2, 0:NMEM], start=(kc == 0), stop=(kc == KC - 1), r=["onesb", ("sqb", kc % 2)], w=["PD"])
            g.act(rstd[:, 0:NMEM], PD[:, 0:NMEM], AF.Ln, r=["PD"], w=["rstd"], bias=EPS, scale=1.0 / D)
            g.act(rstd[:, 0:NMEM], rstd[:, 0:NMEM], AF.Exp, r=["rstd"], w=["rstd"], scale=-0.5)
            for kc in range(KC):
                g.op("dve", "scalar_tensor_tensor", r=[("xt", kc), "nrm", "rstd"], w=["memn"], out=memn[:, kc, :], in0=xt[:, kc, 0:NMEM],
                     scalar=gcol(l, 2, kc), in1=rstd[:, 0:NMEM], op0=ALU.mult, op1=ALU.mult)
            for j in range(4):
                wt, wk = ws.get()
                for kc in range(KC):
                    g.mm(PC[:, 0:NMEM], wt[:, kc, :], memn[:, kc, :], start=(kc == 0), stop=(kc == KC - 1), r=[wk, "memn"], w=[("PC", 0)])
                g.act(KT[:, j, :], PC[:, 0:NMEM], AF.Copy, r=[("PC", 0)], w=["KT"])
            for j in range(4):
                wt, wk = ws.get()
                for mb in range(2):
                    for kc in range(KC):
                        g.mm(PC[:, 512 + mb * 128:512 + (mb + 1) * 128], memn[:, kc, mb * 128:(mb + 1) * 128], wt[:, kc, :],
                             start=(kc == 0), stop=(kc == KC - 1), r=[wk, "memn"], w=[("PC", 1)])
                for mb in range(2):
                    g.act(Vt[:, mb, j * 128:(j + 1) * 128], PC[:, 512 + mb * 128:512 + (mb + 1) * 128], AF.Copy, r=[("PC", 1)], w=["Vt"])
            dump("memn", memn[:].rearrange("p a b -> p (a b)"), ["memn"], [128, KC * NMEM], BF16)
            dump("KT", KT[:].rearrange("p a b -> p (a b)"), ["KT"], [128, 4 * NMEM], BF16)
            dump("Vt", Vt[:].rearrange("p a b -> p (a b)"), ["Vt"], [128, 1024], BF16)
            rmsnorm(xw, "xw", lambda kc: gcol(l, 1, kc), hT, 0, S, "hT")
            for j in range(4):
                def ev(tt, pv, pk, j=j):
                    if tt % 2 == 0:
                        g.act(xq[:, j, tt * 512:(tt + 1) * 512], pv, AF.Copy, r=[pk], w=["xq"])
                    else:
                        g.op("dve", "tensor_copy", r=[pk], w=["xq"], out=xq[:, j, tt * 512:(tt + 1) * 512], in_=pv)
                proj(hT, "hT", S, ev)
            def xA(n):
                tb, j = divmod(n, 4)
                i2 = n % 2
                tbs = slice(tb * 128, (tb + 1) * 128)
                pz = PC[:, i2 * 512:i2 * 512 + NMEM]; pzk = ("PC", i2)
                g.mm(pz, xq[:, j, tbs], KT[:, j, :], r=["xq", "KT"], w=[pzk])
                g.op("dve", "reduce_max", r=[pzk], w=[("mxa", i2)], out=mx2[:, i2, 0:1], in_=pz, axis=mybir.AxisListType.X)
                g.op("dve", "tensor_scalar_mul", r=[("mxa", i2)], w=[("mxa", i2)], out=mx2[:, i2, 1:2], in0=mx2[:, i2, 0:1], scalar1=-scale)
                g.op("dve", "memset", w=[("mxs", i2)], ap=mx2[:, i2, 2:3], constant=0.0)
                g.act(sc2[:, i2, :], pz, AF.Exp, r=[pzk, ("mxa", i2), ("mxs", i2)], w=[("sc2", i2), ("mxs", i2)], bias=mx2[:, i2, 1:2], scale=scale,
                      accum_out=mx2[:, i2, 2:3])

            def xB(n):
                tb, j = divmod(n, 4)
                i2 = n % 2
                tbs = slice(tb * 128, (tb + 1) * 128)
                g.op("dve", "reciprocal", r=[("mxs", i2), ("sc2", i2)], w=[("mxr", i2)], out=mx2[:, i2, 3:4], in_=mx2[:, i2, 2:3])
                g.op("dve", "tensor_scalar", r=[("sc2", i2), ("mxr", i2)], w=["pb_t"], out=pb_t[:], in0=sc2[:, i2, :], scalar1=mx2[:, i2, 3:4], scalar2=None, op0=ALU.mult)
                for mb in range(2):
                    g.tr(PT[:, mb * 128:(mb + 1) * 128], pb_t[:, mb * 128:(mb + 1) * 128], identb[:], r=["pb_t", "identb"], w=["PT"])
                g.act(pT_t[:].rearrange("p a b -> p (a b)"), PT[:, 0:256], AF.Copy, r=["PT"], w=["pT_t"])
                for mb in range(2):
                    g.mm(PD[:, j * 128:(j + 1) * 128], Vt[:, mb, j * 128:(j + 1) * 128], pT_t[:, mb, :], start=(mb == 0), stop=(mb == 1),
                         r=["Vt", "pT_t"], w=["PD"])
                if j == 3:
                    g.op("dve", "tensor_copy", r=["PD"], w=["xo"], out=xo[:, :, tbs], in_=PD[:].rearrange("p (a b) -> p a b", a=4))

            xA(0)
            for n in range(64):
                if n + 1 < 64:
                    xA(n + 1)
                xB(n)

            dump("xq", xq[:].rearrange("p a b -> p (a b)"), ["xq"], [128, 4 * S], BF16)
            dump("xo", xo[:].rearrange("p a b -> p (a b)"), ["xo"], [128, 4 * S], BF16)

            def getw_xo(ob):
                if ob % 4 == 0:
                    getw_xo.cur = ws.get()
                wt, wk = getw_xo.cur
                return [(wt, wk, (ob % 4) * 4 + kc, kc) for kc in range(4)]
            lin_resid(xo, "xo", 4, 0, S, getw_xo)
            if not DO_F:
                continue

            hF = g.at("hF%d" % l, [128, KC, 1024], BF16, big_off)
            actT = g.at("actT%d" % l, [128, 44, 1024], BF16, big_off + 32768)
            cg = g.at("cg%d" % l, [128, 2, 1024], F32, xt_off)
            cu = g.at("cu%d" % l, [128, 2, 1024], F32, xt_off + 8192)
            hc = g.at("hc%d" % l, [128, 2, 4], F32, xt_off + 16384)
            xr2 = g.at("xr2%d" % l, [128, 2, 1024], F32, xt_off + 16384 + 64)
            for tt in range(2):
                T0 = tt * 1024
                rmsnorm(xw, "xw", lambda kc: gcol(l, 3, kc), hF, T0, T0 + 1024, "hF",
                        extra_w=[("cg", 0), ("cg", 1), ("cu", 0), ("cu", 1), ("xr2", 0), ("xr2", 1)])
                for fb in range(44):
                    for gi, (cbuf, ckey, blk) in enumerate(((cg, "cg", fb), (cu, "cu", 44 + fb))):
                        wt, wk = ws.get()
                        pa = (PA, PB)[gi]; pk = ("PAB", gi, 0); pkb = ("PAB", gi, 1)
                        for hf in range(2):
                            for kc in range(KC):
                                g.mm(pa[:, hf * 512:(hf + 1) * 512], wt[:, kc, :], hF[:, kc, hf * 512:(hf + 1) * 512],
                                     start=(kc == 0), stop=(kc == KC - 1), r=[wk, "hF"], w=[(pk, pkb)[hf]])
                        c_ = cbuf[:, fb % 2, :]; ck = (ckey, fb % 2)
                        w0 = fconv_t[:, blk * 4:blk * 4 + 1]; w1 = fconv_t[:, blk * 4 + 1:blk * 4 + 2]
                        w2 = fconv_t[:, blk * 4 + 2:blk * 4 + 3]; bb = fconv_t[:, blk * 4 + 3:blk * 4 + 4]
                        g.act(c_, pa[:], AF.Identity, r=[pk, pkb, "fconv"], w=[ck], scale=w2, bias=bb)
                        g.op("dve", "scalar_tensor_tensor", r=[pk, pkb, "fconv", ck], w=[ck], out=c_[:, 1:1024], in0=pa[:, 0:1023], scalar=w1,
                             in1=c_[:, 1:1024], op0=ALU.mult, op1=ALU.add)
                        g.op("dve", "scalar_tensor_tensor", r=[pk, pkb, "fconv", ck], w=[ck], out=c_[:, 2:1024], in0=pa[:, 0:1022], scalar=w0,
                             in1=c_[:, 2:1024], op0=ALU.mult, op1=ALU.add)
                        if tt == 1:
                            hh = halo[:, blk, :]
                            g.op("dve", "scalar_tensor_tensor", r=["halo", "fconv", ck], w=[ck], out=c_[:, 0:2], in0=hh, scalar=w0,
                                 in1=c_[:, 0:2], op0=ALU.mult, op1=ALU.add)
                            g.op("dve", "scalar_tensor_tensor", r=["halo", "fconv", ck], w=[ck], out=c_[:, 0:1], in0=hh[:, 1:2], scalar=w1,
                                 in1=c_[:, 0:1], op0=ALU.mult, op1=ALU.add)
                        else:
                            g.act(halo[:, blk, :], pa[:, 1022:1024], AF.Copy, r=[pkb], w=["halo"])
                    g.act(cg[:, fb % 2, :], cg[:, fb % 2, :], AF.Silu, r=[("cg", fb % 2)], w=[("cg", fb % 2)])
                    g.op("dve", "tensor_tensor", r=[("cg", fb % 2), ("cu", fb % 2)], w=[("actT", fb)], out=actT[:, fb, :],
                         in0=cg[:, fb % 2, :], in1=cu[:, fb % 2, :], op=ALU.mult)
                if tt == 0:
                    dump("hF", hF[:].rearrange("p a b -> p (a b)"), ["hF"], [128, KC * 1024], BF16)
                    dump("actT", actT[:].rearrange("p a b -> p (a b)"), ["actT"], [128, 44 * 1024], BF16)
                for ob in range(16):
                    wl = []
                    for kg in range(4):
                        wt, wk = ws.get(hold=4)
                        wl += [(wt, wk, i, kg * 11 + i) for i in range(11)]
                    xs = xr2[:, ob % 2, :]; xk = ("xr2", ob % 2)
                    g.dma("sp", xs, xw[ob, :, T0:T0 + 1024], "xr", r=[("xw", ob)], w=[xk] + [("xt", kc_) for kc_ in range(8, 13)])
                    pa = (PA, PB)[ob % 2]; pk = ("PAB", ob % 2, 0); pkb = ("PAB", ob % 2, 1)
                    for hf in range(2):
                        for i, (wt, wk, ki, ci) in enumerate(wl):
                            g.mm(pa[:, hf * 512:(hf + 1) * 512], wt[:, ki, :], actT[:, ci, hf * 512:(hf + 1) * 512],
                                 start=(i == 0), stop=(i == 43), r=[wk, ("actT", ci)], w=[(pk, pkb)[hf]])
                    g.op("dve", "tensor_tensor", r=[pk, pkb, xk], w=[xk], out=xs, in0=xs, in1=pa[:], op=ALU.add)
                    g.dma("sp", xw[ob, :, T0:T0 + 1024], xs, "xr", r=[xk], w=[("xw", ob)])

        if final:
            fT = g.at("fT", [128, KC, 512], F32, big_off)
            for t0 in range(0, S, 512):
                for kc in range(KC):
                    g.dma("sp", xt[:, kc, :], xw[kc, :, t0:t0 + 512], "x", r=[("xw", kc)], w=[("xt", kc)] + XT_ALIAS)
                    g.act(sqb[:, kc % 2, :], xt[:, kc, :], AF.Square, r=[("xt", kc)], w=[("sqb", kc % 2)])
                    g.mm(PD[:], onesb[:], sqb[:, kc % 2, :], start=(kc == 0), stop=(kc == KC - 1), r=["onesb", ("sqb", kc % 2)], w=["PD"])
                g.act(rstd[:], PD[:], AF.Ln, r=["PD"], w=["rstd"], bias=EPS, scale=1.0 / D)
                g.act(rstd[:], rstd[:], AF.Exp, r=["rstd"], w=["rstd"], scale=-0.5)
                for kc in range(KC):
                    g.op("dve", "scalar_tensor_tensor", r=[("xt", kc), "nrm", "rstd"], w=[("fT", kc)], out=fT[:, kc, :], in0=xt[:, kc, :],
                         scalar=nrm_t[:, nl * 64 + kc:nl * 64 + kc + 1], in1=rstd[:], op0=ALU.mult, op1=ALU.mult)
                    g.dma("sp", yout[kc, :, t0:t0 + 512], fT[:, kc, :], "yo", r=[("fT", kc)], w=[("yout", kc)])
            sc.rec("sp", lambda e: None, reads=["yout"])
        else:
            sc.rec("sp", lambda e: None, reads=["xw", "mixT"] + dbg_keys)
        sc.emit(es)
    return nc


def _consts():
    i = np.arange(128)
    ident = np.eye(128, dtype=np.float32)
    Lincl = (i[:, None] >= i[None, :]).astype(np.float32)
    Uincl = (i[:, None] <= i[None, :]).astype(np.float32)
    Ustr = (i[:, None] > i[None, :]).astype(np.float32)
    mSL = (i[:, None] > i[None, :]).astype(np.float32)
    mUI = (i[:, None] <= i[None, :]).astype(np.float32)
    ones = np.ones((128, 128), np.float32)
    t = np.arange(512)
    md = [((128 * d + i[:, None]) < t[None, :]).astype(np.float32) for d in range(4)]
    return np.ascontiguousarray(np.concatenate([ident, Lincl, Uincl, Ustr, mSL, mUI, ones] + md, axis=1))


def _blk(w, nk):
    K, C = w.shape
    return np.ascontiguousarray(w.reshape(K // 128, 128, C // 128, 128).transpose(2, 1, 0, 3))


def _col(v):
    return np.ascontiguousarray(v.reshape(-1, 128).T)


def prep_layers(inp, ls):
    f = lambda k: np.asarray(inp[k], dtype=np.float32)
    nl = len(ls)
    out = {}
    nrm = np.zeros((128, nl * 64 + 16), np.float32)
    for i, l in enumerate(ls):
        for j, k in enumerate(("mix_norm", "xattn_norm", "mem_norm", "ffn_norm")):
            nrm[:, i * 64 + j * 16:i * 64 + (j + 1) * 16] = _col(f(k)[l])
    nrm[:, nl * 64:] = _col(f("final_norm"))
    out["nrm"] = nrm
    w_in = f("w_in")
    out["w_in"] = np.stack([_blk(w_in[l][:, :7168], 16) for l in ls])
    out["w_ba"] = np.stack([np.ascontiguousarray(w_in[l][:, 7168:7184].reshape(16, 128, 16).transpose(1, 0, 2)) for l in ls])
    gc = f("gdn_conv")
    out["gconv"] = np.stack([np.ascontiguousarray(gc[l].reshape(4, 24, 128).transpose(2, 1, 0).reshape(128, 96)) for l in ls])
    gsm = np.zeros((nl, 128, 17), np.float32)
    for i, l in enumerate(ls):
        gsm[i, :, 0:8] = f("gdn_a_log")[l][None, :]
        gsm[i, :, 8:16] = f("gdn_dt_bias")[l][None, :]
        gsm[i, :, 16] = f("gdn_norm")[l]
    out["gsm"] = gsm
    out["w_out"] = np.stack([_blk(f("w_out")[l], 16) for l in ls])
    out["w_xq"] = np.stack([_blk(f("w_xq")[l], 16) for l in ls])
    out["w_xkv"] = np.stack([_blk(f("w_xkv")[l], 16) for l in ls])
    wxo = f("w_xo")
    out["w_xo"] = np.stack([np.ascontiguousarray(
        wxo[l].reshape(4, 128, 4, 4, 128).transpose(2, 1, 3, 0, 4).reshape(4, 128, 16, 128)) for l in ls])
    out["w_up"] = np.stack([_blk(f("w_up")[l], 16) for l in ls])
    fc = f("ffn_conv"); fb_ = f("ffn_conv_bias")
    fconv = np.zeros((nl, 128, NFB, 4), np.float32)
    for i, l in enumerate(ls):
        fconv[i, :, :, 0:3] = fc[l].reshape(3, NFB, 128).transpose(2, 1, 0)
        fconv[i, :, :, 3] = fb_[l].reshape(NFB, 128).T
    out["fconv"] = fconv.reshape(nl, 128, NFB * 4)
    wd = f("w_down")
    out["w_dn"] = np.stack([np.ascontiguousarray(
        wd[l].reshape(4, 11, 128, 16, 128).transpose(3, 0, 2, 1, 4)) for l in ls])
    out["cst"] = _consts()
    return out


_NC_CACHE = {}


def _get_nc(nl, final, dbg=False):
    key = (nl, final, dbg)
    if key not in _NC_CACHE:
        _NC_CACHE[key] = build(nl, final, dbg)
    return _NC_CACHE[key]


NCORES = 8
FUSED = True


def kernel(**inputs):
    x = np.asarray(inputs["x"], dtype=np.float32)
    mem = np.asarray(inputs["mem"], dtype=np.float32)
    B = x.shape[0]
    xT = [np.ascontiguousarray(x[b].T.reshape(KC, 128, S)) for b in range(B)]
    mT = [np.ascontiguousarray(mem[b].T.reshape(KC, 128, NMEM)) for b in range(B)]
    groups = [[0, 1, 2, 3]] if FUSED else [[0], [1], [2], [3]]
    cur = xT
    for gi, ls in enumerate(groups):
        final = gi == len(groups) - 1
        wts = prep_layers(inputs, ls)
        nc = _get_nc(len(ls), final)
        in_maps = []
        for c in range(NCORES):
            m = dict(wts)
            m["xin"] = cur[c % B]
            m["memT"] = mT[c % B]
            in_maps.append(m)
        res = run_bass_kernel_spmd(nc, in_maps, core_ids=list(range(NCORES)))
        key = "yout" if final else "xw"
        cur = [np.asarray(res.results[b][key]) for b in range(B)]
    out = np.stack([np.ascontiguousarray(cur[b].reshape(D, S).T) for b in range(B)])
    return out.astype(np.float32)
```

```python
import numpy as np
from contextlib import ExitStack
import concourse.bass as bass
import concourse.mybir as mybir
from concourse.bass_utils import run_bass_kernel_spmd

F32 = mybir.dt.float32
BF16 = mybir.dt.bfloat16
AF = mybir.ActivationFunctionType
ALU = mybir.AluOpType

S = 2048
D = 2048
KC = 16
NMEM = 256
DFF = 5632
NFB = 88
EPS = 1e-6
ENGS = ("pe", "act", "dve", "pool", "sp")


class Sched:
    def __init__(self, nc):
        self.nc = nc
        self.ops = {e: [] for e in ENGS}
        self.state = {}
        self.subs = {}
        self.seen = {e: {} for e in ENGS}
        self.dma_count = {}

    @staticmethod
    def _norm(k):
        if isinstance(k, tuple):
            return (k[0], k[1:] if len(k) > 1 else None)
        return (k, None)

    def _conf(self, base, sub):
        if sub is None:
            return [(base, None)] + [(base, s) for s in self.subs.get(base, ())]
        return [(base, sub), (base, None)]

    def rec(self, eng, fn, reads=(), writes=(), dma_sem=None):
        deps = {}
        if eng != "pe":
            pr = [k for k in reads if (k[0] if isinstance(k, tuple) else k) in ("PAB", "PC", "PD", "PT")]
            if pr:
                reads = [k for k in reads if k not in pr]
                writes = list(writes) + pr

        def add(pid, raw=False):
            if pid is None:
                return
            stream, idx = pid
            if stream == eng and (not raw or eng in ("pe", "sp")):
                return
            if deps.get(stream, -1) < idx:
                deps[stream] = idx

        rk = [self._norm(k) for k in reads]
        wk = [self._norm(k) for k in writes]
        for base, sub in rk:
            for ck in self._conf(base, sub):
                st = self.state.get(ck)
                if st is not None:
                    add(st[0], True)
        for base, sub in wk:
            for ck in self._conf(base, sub):
                st = self.state.get(ck)
                if st is not None:
                    add(st[0])
                    for s_, i_ in st[1].items():
                        add((s_, i_))
        waits = []
        seen = self.seen[eng]
        for stream, idx in deps.items():
            if isinstance(stream, tuple):
                cnt = self.dma_count[stream[1]]
                if seen.get(stream, -1) >= cnt:
                    continue
                seen[stream] = cnt
                waits.append((stream, cnt))
            else:
                if seen.get(stream, -1) >= idx:
                    continue
                seen[stream] = idx
                self.ops[stream][idx]["inc"] = True
                waits.append((stream, idx))
        if dma_sem is not None:
            c = self.dma_count.get(dma_sem, 0) + 1
            self.dma_count[dma_sem] = c
            pid = (("d", dma_sem), c)
        else:
            pid = (eng, len(self.ops[eng]))
        self.ops[eng].append({"fn": fn, "waits": waits, "inc": False, "dma": dma_sem})
        for base, sub in rk:
            st = self.state.setdefault((base, sub), [None, {}])
            if sub is not None:
                self.subs.setdefault(base, set()).add(sub)
            if st[1].get(pid[0], -1) < pid[1]:
                st[1][pid[0]] = pid[1]
        for base, sub in wk:
            if sub is None:
                for s in self.subs.get(base, ()):
                    self.state.pop((base, s), None)
                self.subs[base] = set()
            else:
                self.subs.setdefault(base, set()).add(sub)
            self.state[(base, sub)] = [pid, {}]

    def emit(self, es):
        nc = self.nc
        sems = {}
        for e in ("pe", "act", "dve", "pool"):
            sems[e] = es.enter_context(nc.semaphore("s_" + e))
        for name in self.dma_count:
            sems[("d", name)] = es.enter_context(nc.semaphore("d_" + name))
        cum = {}
        for e in ENGS:
            c = 0
            arr = []
            for op in self.ops[e]:
                if op["inc"]:
                    c += 1
                arr.append(c)
            cum[e] = arr
        block = es.enter_context(nc.Block())
        ops = self.ops

        def run(e, engh):
            for op in ops[e]:
                for stream, v in op["waits"]:
                    if isinstance(stream, tuple):
                        engh.wait_ge(sems[stream], 16 * v)
                    else:
                        engh.wait_ge(sems[stream], cum[stream][v])
                ins = op["fn"](engh)
                if ins is None:
                    continue
                if op["dma"] is not None:
                    ins.then_inc(sems[("d", op["dma"])], 16)
                elif op["inc"]:
                    ins.then_inc(sems[e], 1)

        @block.tensor
        def _(t):
            run("pe", t)

        @block.scalar
        def _(t):
            run("act", t)

        @block.vector
        def _(t):
            run("dve", t)

        @block.gpsimd
        def _(t):
            run("pool", t)

        @block.sync
        def _(t):
            run("sp", t)


class Gen:
    def __init__(self, nc, es):
        self.nc = nc
        self.es = es
        self.sc = Sched(nc)
        slab = nc.alloc_sbuf_tensor("slab", [128, 207000], mybir.dt.uint8)
        self.base = nc.lookup_mloc(slab).addr
        self.nps = 0

    def at(self, name, shape, dt, off):
        return self.nc.alloc_sbuf_tensor_at(name, shape, dt, offset=self.base + off)

    def mm(self, out, lhsT, rhs, start=True, stop=True, r=(), w=()):
        self.sc.rec("pe", lambda e: e.matmul(out, lhsT, rhs, start=start, stop=stop), reads=r, writes=w)

    def tr(self, out, in_, ident, r=(), w=()):
        self.sc.rec("pe", lambda e: e.transpose(out, in_, ident), reads=r, writes=w)

    def op(self, eng, method, r=(), w=(), **kw):
        self.sc.rec(eng, lambda e: getattr(e, method)(**kw), reads=r, writes=w)

    def act(self, out, in_, func, r=(), w=(), **kw):
        self.sc.rec("act", lambda e: e.activation(out=out, in_=in_, func=func, **kw), reads=r, writes=w)

    def dma(self, eng, out, in_, sem, r=(), w=()):
        self.sc.rec(eng, lambda e: e.dma_start(out=out, in_=in_), reads=r, writes=w, dma_sem=sem)


class WStream:
    def __init__(self, g, tiles):
        self.g = g
        self.tiles = tiles
        self.plan = []
        self.loaded = 0
        self.used = 0

    def add(self, ap, nk=16):
        self.plan.append((ap, nk))

    def get(self, hold=1):
        n = len(self.tiles)
        while self.loaded < len(self.plan) and self.loaded < self.used + n - (hold - 1):
            ap, nk = self.plan[self.loaded]
            i = self.loaded % n
            self.g.dma("pool", self.tiles[i][:, 0:nk, :], ap, "w%d" % i, w=[("wb", i)])
            self.loaded += 1
        i = self.used % n
        self.used += 1
        return self.tiles[i], ("wb", i)


def build(nl, final, dbg=False, cfg=None):
    cfg = cfg or {}
    nc = bass.Bass("TRN2", target_bir_lowering=False)
    dr = lambda name, shape, dt, kind="ExternalInput": nc.dram_tensor(name, shape, dt, kind=kind).ap()
    xin = dr("xin", [KC, 128, S], F32)
    memT = dr("memT", [KC, 128, NMEM], F32)
    cst = dr("cst", [128, 7 * 128 + 4 * 512], F32)
    nrm = dr("nrm", [128, nl * 64 + 16], F32)
    w_in = dr("w_in", [nl, 56, 128, 16, 128], F32)
    w_ba = dr("w_ba", [nl, 128, 16, 16], F32)
    gconv = dr("gconv", [nl, 128, 24 * 4], F32)
    gsm = dr("gsm", [nl, 128, 17], F32)
    w_out = dr("w_out", [nl, 16, 128, 16, 128], F32)
    w_xq = dr("w_xq", [nl, 4, 128, 16, 128], F32)
    w_xkv = dr("w_xkv", [nl, 8, 128, 16, 128], F32)
    w_xo = dr("w_xo", [nl, 4, 128, 16, 128], F32)
    w_up = dr("w_up", [nl, NFB, 128, 16, 128], F32)
    fconv = dr("fconv", [nl, 128, NFB * 4], F32)
    w_dn = dr("w_dn", [nl, 16, 4, 128, 11, 128], F32)
    okind = "ExternalOutput"
    xw = dr("xw", [KC, 128, S], F32, kind=okind if not final else "Internal")
    yout = dr("yout", [KC, 128, S], F32, kind=okind) if final else None
    mixT = dr("mixT", [KC, 128, S], BF16, kind=okind if dbg else "Internal")
    dbg_keys = []

    def dump(name, ap, keys, shape, dt):
        if not dbg or not cfg.get("dump"):
            return
        if name not in cfg["dump"]:
            return
        t = dr("D_" + name, shape, dt, kind=okind)
        g.dma("sp", t, ap, "dbg", r=keys, w=["D_" + name])
        dbg_keys.append("D_" + name)

    NSB = cfg.get("sb", 8); NGD = cfg.get("gdn", 8); DO_OUT = cfg.get("out", True)
    DO_X = cfg.get("xattn", True); DO_F = cfg.get("ffn", True)
    es = ExitStack()
    with es:
        g = Gen(nc, es)
        sc = g.sc
        o = 0
        cst_t = g.at("cst_t", [128, 7 * 128 + 4 * 512], F32, o); o += (7 * 128 + 4 * 512) * 4
        ident = cst_t[:, 0:128]; Lincl = cst_t[:, 128:256]; Uincl = cst_t[:, 256:384]
        Ustr = cst_t[:, 384:512]; mSL = cst_t[:, 512:640]; mUI = cst_t[:, 640:768]; ones = cst_t[:, 768:896]
        maskd = [cst_t[:, 896 + 512 * d: 896 + 512 * (d + 1)] for d in range(4)]
        identb = g.at("identb", [128, 128], BF16, o); o += 256
        nrm_t = g.at("nrm_t", [128, nl * 64 + 16], F32, o); o += (nl * 64 + 16) * 4
        gconv_t = g.at("gconv_t", [128, 96], F32, o); o += 384
        gsm_t = g.at("gsm_t", [128, 17], F32, o); o += 68 + 28
        fconv_t = g.at("fconv_t", [128, NFB * 4], F32, o); o += NFB * 16
        halo = g.at("halo", [128, NFB, 2], F32, o); o += NFB * 8
        rstd = g.at("rstd", [128, 512], F32, o); o += 2048
        sq = g.at("sq", [128, 2, 512], F32, o); o += 4096
        sqb = g.at("sqb", [128, 2, 512], BF16, o); o += 2048
        onesb = g.at("onesb", [128, 128], BF16, o); o += 256
        NWB = 6
        wbt = g.at("wbt", [128, NWB, 16, 128], BF16, o); o += NWB * 4096
        wtiles = [wbt[:, i] for i in range(NWB)]
        xt_off = o
        xt = g.at("xt", [128, KC, 512], F32, o); o += 32768
        big_off = o
        hT = g.at("hT", [128, KC, S], BF16, o)
        ar = o + 65536
        o += 122880
        assert o <= 207000, o
        ws = WStream(g, wtiles)

        PA = es.enter_context(nc.psum_tensor("PA", [128, 1024], F32))
        PB = es.enter_context(nc.psum_tensor("PB", [128, 1024], F32))
        PC = es.enter_context(nc.psum_tensor("PC", [128, 1024], F32))
        PD = es.enter_context(nc.psum_tensor("PD", [128, 512], F32))
        PTf = es.enter_context(nc.psum_tensor("PT", [128, 512], F32))
        PT = PTf.bitcast(BF16)

        g.dma("sp", cst_t[:], cst, "c", w=["cst"])
        g.dma("sp", nrm_t[:], nrm, "c", w=["nrm"])
        g.op("dve", "tensor_copy", r=["cst"], w=["identb"], out=identb[:], in_=ident)
        g.op("dve", "tensor_copy", r=["cst"], w=["onesb"], out=onesb[:], in_=ones)
        for kc in range(KC):
            g.dma("sp", xw[kc], xin[kc], "xcp", w=[("xw", kc)])

        def gcol(l, which, kc):
            c = l * 64 + which * 16 + kc
            return nrm_t[:, c:c + 1]

        XT_ALIAS = [("cg", 0), ("cg", 1), ("cu", 0), ("cu", 1), ("xr2", 0), ("xr2", 1), "gq", "gk", "gv", "gz"]

        def rmsnorm(src, srckey, gc, dst, T0, T1, dkey, extra_w=()):
            for t0 in range(T0, T1, 512):
                for kc in range(KC):
                    g.dma("sp", xt[:, kc, :], src[kc, :, t0:t0 + 512], "x", r=[(srckey, kc)], w=[("xt", kc)] + XT_ALIAS)
                    g.act(sqb[:, kc % 2, :], xt[:, kc, :], AF.Square, r=[("xt", kc)], w=[("sqb", kc % 2)])
                    g.mm(PD[:], onesb[:], sqb[:, kc % 2, :], start=(kc == 0), stop=(kc == KC - 1),
                         r=["onesb", ("sqb", kc % 2)], w=["PD"])
                g.act(rstd[:], PD[:], AF.Ln, r=["PD"], w=["rstd"], bias=EPS, scale=1.0 / D)
                g.act(rstd[:], rstd[:], AF.Exp, r=["rstd"], w=["rstd"], scale=-0.5)
                for kc in range(KC):
                    g.op("dve", "scalar_tensor_tensor", r=[("xt", kc), "nrm", "rstd"], w=[(dkey, kc, t0)],
                         out=dst[:, kc, t0 - T0:t0 - T0 + 512], in0=xt[:, kc, :], scalar=gc(kc), in1=rstd[:],
                         op0=ALU.mult, op1=ALU.mult)

        def proj(src, skey, T, evac):
            wt, wk = ws.get()
            for tt in range(T // 512):
                pa = (PA, PB)[tt % 2]
                half = (tt // 2) % 2
                pv = pa[:, half * 512:(half + 1) * 512]
                pk = ("PAB", tt % 2, half)
                for kc in range(KC):
                    g.mm(pv, wt[:, kc, :], src[:, kc, tt * 512:(tt + 1) * 512], start=(kc == 0), stop=(kc == KC - 1),
                         r=[wk, skey], w=[pk])
                evac(tt, pv, pk)

        def residual_linear(wplan_nk, nkc, src, skey, T0, T):
            raise NotImplementedError

        for l in range(nl):
            g.dma("sp", gconv_t[:], gconv[l], "c", w=["gconv"])
            g.dma("sp", gsm_t[:], gsm[l], "c", w=["gsm"])
            g.dma("sp", fconv_t[:], fconv[l], "c", w=["fconv"])
            for hd in range(NSB):
                for j in range(3):
                    ws.add(w_in[l, j * 8 + hd])
            for hd in range(NGD):
                for j in range(4):
                    ws.add(w_in[l, 24 + j * 8 + hd])
            for ob in range(16 if DO_OUT else 0):
                ws.add(w_out[l, ob])
            for j in range(8 if DO_X else 0):
                ws.add(w_xkv[l, j])
            for j in range(4 if DO_X else 0):
                ws.add(w_xq[l, j])
            for j in range(4 if DO_X else 0):
                ws.add(w_xo[l, j])
            for tt in range(2 if DO_F else 0):
                for fb in range(44):
                    ws.add(w_up[l, fb]); ws.add(w_up[l, 44 + fb])
                for ob in range(16):
                    for kg in range(4):
                        ws.add(w_dn[l, ob, kg], 11)

            rmsnorm(xw, "xw", lambda kc: gcol(l, 0, kc), hT, 0, S, "hT")

            a = ar
            wba_t = g.at("wba_t%d" % l, [128, 16, 16], BF16, a); a += 512
            bg = g.at("bg%d" % l, [128, 16, 16], F32, a); a += 1024
            beta_t = g.at("beta%d" % l, [128, 16, 8], F32, a); a += 512
            nbeta_t = g.at("nbeta%d" % l, [128, 16, 8], F32, a); a += 512
            g_t = g.at("g_t%d" % l, [128, 16, 8], F32, a); a += 512
            nA = g.at("nA%d" % l, [128, 8], F32, a); a += 32
            gates_end = a
            g.dma("pool", wba_t[:], w_ba[l], "wba", w=["wba", "actT"])
            for blk in range(16):
                for kc in range(KC):
                    g.mm(PD[:, blk * 16:(blk + 1) * 16], hT[:, kc, blk * 128:(blk + 1) * 128], wba_t[:, kc, :],
                         start=(kc == 0), stop=(kc == KC - 1), r=["hT", "wba"], w=["PD"])
            g.op("dve", "tensor_copy", r=["PD"], w=["bg"], out=bg[:].rearrange("p a b -> p (a b)"), in_=PD[:, 0:256])
            g.act(beta_t[:], bg[:, :, 0:8], AF.Exp, r=["bg"], w=["beta"], scale=-1.0)
            g.op("dve", "tensor_scalar_add", r=["beta"], w=["beta"], out=beta_t[:], in0=beta_t[:], scalar1=1.0)
            g.op("dve", "reciprocal", r=["beta"], w=["beta"], out=beta_t[:], in_=beta_t[:])
            g.op("dve", "tensor_scalar_mul", r=["beta"], w=["nbeta"], out=nbeta_t[:], in0=beta_t[:], scalar1=-1.0)
            g.act(nA[:], gsm_t[:, 0:8], AF.Exp, r=["gsm"], w=["nA"])
            g.op("dve", "tensor_scalar_mul", r=["nA"], w=["nA"], out=nA[:], in0=nA[:], scalar1=-1.0)
            for blk in range(16):
                g.op("dve", "tensor_tensor", r=["bg", "gsm"], w=["g_t"], out=g_t[:, blk, :], in0=bg[:, blk, 8:16],
                     in1=gsm_t[:, 8:16], op=ALU.add)
            g.act(g_t[:], g_t[:], AF.Exp, r=["g_t"], w=["g_t"])
            g.act(g_t[:], g_t[:], AF.Ln, r=["g_t"], w=["g_t"], bias=1.0)
            for blk in range(16):
                g.op("dve", "tensor_tensor", r=["g_t", "nA"], w=["g_t"], out=g_t[:, blk, :], in0=g_t[:, blk, :],
                     in1=nA[:], op=ALU.mult)

            a = gates_end
            a = (a + 63) // 64 * 64
            qTs = [g.at("qT%d_%d" % (l, i), [128, S], BF16, a + 4096 * i) for i in range(2)]; a += 8192
            kTs = [g.at("kT%d_%d" % (l, i), [128, S], BF16, a + 4096 * i) for i in range(2)]; a += 8192
            vtoks = [g.at("vtok%d_%d" % (l, i), [128, 16, 128], BF16, a + 4096 * i) for i in range(2)]; a += 8192
            vT = g.at("vT%d" % l, [128, S], BF16, a); a += 4096
            ebuf = g.at("ebuf%d" % l, [128, 3, 512], F32, a); a += 6144
            spbuf = g.at("spbuf%d" % l, [128, 3, 512], F32, a); a += 6144
            ecbuf = g.at("ecbuf%d" % l, [128, 2, 512], F32, a); a += 4096
            racc = g.at("racc%d" % l, [128, 512], F32, a); a += 2048
            wbuf = g.at("wbuf%d" % l, [128, 2, 512], BF16, a); a += 2048
            obuf = g.at("obuf%d" % l, [128, 2, 512], BF16, a); a += 2048
            assert a <= big_off + 122880
            scale = 128.0 ** -0.5

            def proj_closures(hd):
                par = hd % 2
                cl = []
                for j, dst, key in ((0, qTs[par], "qT%d" % par), (1, kTs[par], "kT%d" % par), (2, vT, "vT")):
                    holder = {}
                    for tt in range(4):
                        def c(tt=tt, dst=dst, key=key, holder=holder):
                            if tt == 0:
                                holder["w"] = ws.get()
                            wt, wk = holder["w"]
                            pv = (PA, PB)[tt % 2][:, 0:512]; pk = ("PAB", tt % 2, 0)
                            for kc in range(KC):
                                g.mm(pv, wt[:, kc, :], hT[:, kc, tt * 512:(tt + 1) * 512], start=(kc == 0), stop=(kc == KC - 1),
                                     r=[wk, "hT"], w=[pk])
                            if tt % 2 == 0:
                                g.act(dst[:, tt * 512:(tt + 1) * 512], pv, AF.Copy, r=[pk], w=[(key, tt)])
                            else:
                                g.op("dve", "tensor_copy", r=[pk], w=[(key, tt)], out=dst[:, tt * 512:(tt + 1) * 512], in_=pv)
                        cl.append(c)
                for half in range(2):
                    def c(half=half, par=par):
                        for b in range(8):
                            blk = half * 8 + b
                            g.tr(PT[:, b * 128:(b + 1) * 128], vT[:, blk * 128:(blk + 1) * 128], identb[:],
                                 r=[("vT", blk // 4), "identb"], w=["PT"])
                        g.op("dve", "tensor_copy", r=["PT"], w=[("vtok%d" % par, half)],
                             out=vtoks[par][:, half * 8:(half + 1) * 8, :].rearrange("p a b -> p (a b)"), in_=PT[:])
                    cl.append(c)
                return cl

            for c_ in (proj_closures(0) if NSB > 0 else []):
                c_()
            for hd in range(NSB):
                par = hd % 2
                qT, kT, vtok = qTs[par], kTs[par], vtoks[par]
                qTn, kTn, vtn = "qT%d" % par, "kT%d" % par, "vtok%d" % par
                nxt = proj_closures(hd + 1) if hd + 1 < NSB else []
                its = []
                for qt in range(4):
                    nkb = 4 * qt + 4
                    for it, kb in enumerate(range(nkb - 1, -1, -1)):
                        its.append((qt, it, kb))

                def stA(n):
                    qt, it, kb = its[n]
                    i2 = n % 2; i3 = n % 3
                    pz = PC[:, i2 * 512:(i2 + 1) * 512]; pzk = ("PC", i2)
                    g.mm(pz, kT[:, kb * 128:(kb + 1) * 128], qT[:, qt * 512:(qt + 1) * 512], r=[(kTn, kb // 4), (qTn, qt)], w=[pzk])
                    e_ = ebuf[:, i3, :]; sp_ = spbuf[:, i3, :]
                    g.act(e_, pz, AF.Exp, r=[pzk], w=[("e", i3)], scale=scale)
                    g.act(sp_, e_, AF.Ln, r=[("e", i3)], w=[("sp", i3)], bias=1.0)
                    if kb >= 4 * qt:
                        md = maskd[kb - 4 * qt]
                        g.op("dve", "tensor_tensor", r=[("sp", i3), "cst"], w=[("sp", i3)], out=sp_, in0=sp_, in1=md, op=ALU.mult)
                        g.op("dve", "tensor_tensor", r=[("e", i3), "cst"], w=[("e", i3)], out=e_, in0=e_, in1=md, op=ALU.mult)

                def stB(n):
                    qt, it, kb = its[n]
                    i2 = n % 2; i3 = n % 3
                    e_ = ebuf[:, i3, :]; sp_ = spbuf[:, i3, :]; ec_ = ecbuf[:, i2, :]
                    pcs = (PA, PB)[i2][:, 512:1024]; pck = ("PAB", i2, 1)
                    g.mm(pcs, Lincl, sp_, start=True, stop=(it == 0), r=["cst", ("sp", i3)], w=[pck])
                    if it > 0:
                        g.mm(pcs, ones, racc[:], start=False, stop=True, r=["cst", "racc"], w=[pck])
                    g.act(ec_, pcs, AF.Exp, r=[pck], w=[("ec", i2)], scale=-1.0)
                    g.op("dve", "tensor_tensor", r=[("e", i3), ("ec", i2)], w=[("wbuf", i2)], out=wbuf[:, i2, :], in0=e_, in1=ec_, op=ALU.mult)
                    if it == 0:
                        g.op("dve", "tensor_copy", r=[("sp", i3)], w=["racc"], out=racc[:], in_=sp_)
                    elif kb > 0:
                        g.op("dve", "tensor_tensor", r=[("sp", i3), "racc"], w=["racc"], out=racc[:], in0=racc[:], in1=sp_, op=ALU.add)

                def stC(n):
                    qt, it, kb = its[n]
                    i2 = n % 2
                    g.mm(PD[:], vtok[:, kb, :], wbuf[:, i2, :], start=(it == 0), stop=(kb == 0),
                         r=[(vtn, kb // 8), ("wbuf", i2)], w=["PD"])
                    if kb == 0:
                        ob_ = obuf[:, qt % 2, :]
                        g.act(ob_, PD[:], AF.Copy, r=["PD"], w=[("obuf", qt % 2)])
                        g.dma("sp", mixT[hd, :, qt * 512:(qt + 1) * 512], ob_, "mo", r=[("obuf", qt % 2)], w=[("mixT", hd)])

                NI = len(its)
                for n in range(NI + 2):
                    if n < NI:
                        stA(n)
                    if 0 <= n - 1 < NI:
                        stB(n - 1)
                    if 0 <= n - 2 < NI:
                        stC(n - 2)
                    if nxt and n % 3 == 2:
                        nxt.pop(0)()
                while nxt:
                    nxt.pop(0)()

            a = gates_end
            a = (a + 63) // 64 * 64
            gates_end_al = a
            gq = xt[:].rearrange("p a b -> p (a b)")
            gqT = gq[:, 0:S]; gkT = gq[:, S:2 * S]; gvT = gq[:, 2 * S:3 * S]; gzT = gq[:, 3 * S:4 * S]
            cb = g.at("cb%d" % l, [128, S + 8], F32, a); a += (S + 8) * 4
            ogT = g.at("ogT%d" % l, [128, S], F32, a); a += S * 4
            wide = {}
            for nm in ("Gm", "DecL", "DecT", "egcb", "Qa", "Qb", "Pa", "Pb", "X", "vb", "kbg", "ktail", "u", "wT", "ATm", "qdT"):
                wide[nm] = g.at(nm + str(l), [128, 4, 128], F32, a); a += 2048
            Sst = g.at("Sst%d" % l, [128, 2, 128], F32, a); a += 1024
            vnew = g.at("vnew%d" % l, [128, 128], F32, a); a += 512
            ecols = g.at("ecols%d" % l, [128, 4, 4], F32, a); a += 64
            bcol2 = g.at("bcol2%d" % l, [128, 4], F32, a); a += 64
            ecols1 = g.at("ecols1_%d" % l, [128, 4, 4], F32, a); a += 64
            ktail1 = g.at("ktail1_%d" % l, [128, 4, 128], F32, a); a += 2048
            assert a <= big_off + 122880, a
            cb_off = gates_end_al
            scan_in = {0: {"u": wide["u"], "wT": wide["wT"], "ATm": wide["ATm"], "qdT": wide["qdT"], "ktail": wide["ktail"], "ecols": ecols},
                       1: {"u": g.at("u1_%d" % l, [128, 4, 128], F32, cb_off), "wT": g.at("wT1_%d" % l, [128, 4, 128], F32, cb_off + 2048),
                           "ATm": g.at("ATm1_%d" % l, [128, 4, 128], F32, cb_off + 4096), "qdT": g.at("qdT1_%d" % l, [128, 4, 128], F32, cb_off + 6144),
                           "ktail": ktail1, "ecols": ecols1}}
            CB_ALIAS = ["u1", "wT1", "ATm1", "qdT1"]
            Gm, DecL, DecT, egcb = wide["Gm"], wide["DecL"], wide["DecT"], wide["egcb"]
            X, vb, kbg, ktail, u_, wT_, ATm, qdT = (wide[n] for n in ("X", "vb", "kbg", "ktail", "u", "wT", "ATm", "qdT"))
            fl = lambda t: t[:].rearrange("p a b -> p (a b)")
            class _Stop(Exception):
                pass

            def chk(stage):
                if cfg.get("gstop") == stage:
                    raise _Stop()

            def gdn_head(hd):
                for j, dst, key in ((0, gqT, "gq"), (1, gkT, "gk"), (2, gvT, "gv"), (3, gzT, "gz")):
                    if j < 3:
                        g.op("dve", "memset", w=["cb"] + CB_ALIAS, ap=cb[:, 0:3], constant=0.0)
                        def ev(tt, pv, pk):
                            if tt % 2 == 0:
                                g.act(cb[:, 3 + tt * 512:3 + (tt + 1) * 512], pv, AF.Copy, r=[pk], w=["cb"] + CB_ALIAS)
                            else:
                                g.op("dve", "tensor_copy", r=[pk], w=["cb"] + CB_ALIAS, out=cb[:, 3 + tt * 512:3 + (tt + 1) * 512], in_=pv)
                        proj(hT, "hT", S, ev)
                        fb = j * 8 + hd
                        wc = lambda i: gconv_t[:, fb * 4 + i:fb * 4 + i + 1]
                        g.act(dst, cb[:, 3:3 + S], AF.Identity, r=["cb", "gconv"], w=[key], scale=wc(3))
                        for i in range(3):
                            g.op("dve", "scalar_tensor_tensor", r=["cb", "gconv", key], w=[key], out=dst, in0=cb[:, i:i + S],
                                 scalar=wc(i), in1=dst, op0=ALU.mult, op1=ALU.add)
                        g.act(dst, dst, AF.Silu, r=[key], w=[key])
                    else:
                        def ev(tt, pv, pk):
                            g.act(gzT[:, tt * 512:(tt + 1) * 512], pv, AF.Silu, r=[pk], w=["gz"])
                        proj(hT, "hT", S, ev)
                chk(1)
                for dst, key, sc_ in ((gqT, "gq", 128.0 ** -0.5), (gkT, "gk", 1.0)):
                    for tt in range(4):
                        tsl = slice(tt * 512, (tt + 1) * 512)
                        g.act(sq[:, tt % 2, :], dst[:, tsl], AF.Square, r=[key], w=[("sq", tt % 2)])
                        g.mm(PD[:], ones, sq[:, tt % 2, :], r=["cst", ("sq", tt % 2)], w=["PD"])
                        g.act(rstd[:], PD[:], AF.Ln, r=["PD"], w=["rstd"], bias=EPS, scale=1.0)
                        g.act(rstd[:], rstd[:], AF.Exp, r=["rstd"], w=["rstd"], scale=-0.5)
                        g.op("dve", "scalar_tensor_tensor", r=[key, "rstd"], w=[key], out=dst[:, tsl], in0=dst[:, tsl],
                             scalar=sc_, in1=rstd[:], op0=ALU.mult, op1=ALU.mult)
                chk(2)
                g.op("dve", "memset", w=[("Sst", 0)], ap=Sst[:, 0, :], constant=0.0)

                def prep_stages(grp):
                    par = grp % 2
                    blks = [grp * 4 + b for b in range(4)]
                    si = scan_in[par]
                    u_, wT_, ATm, qdT, ktail, ecols = si["u"], si["wT"], si["ATm"], si["qdT"], si["ktail"], si["ecols"]
                    ku, kw, ka, kq, kk, ke = ("u%d" % par, "wT%d" % par, "ATm%d" % par, "qdT%d" % par, "ktail%d" % par, "ecols%d" % par)
                    st = []

                    def s1():
                        for b, n in enumerate(blks):
                            g.op("dve", "tensor_scalar", r=["cst", "g_t"], w=["Gm"], out=Gm[:, b, :], in0=Uincl,
                                 scalar1=g_t[:, n, hd:hd + 1], scalar2=None, op0=ALU.mult)
                    st.append(s1)

                    def s2():
                        for b, n in enumerate(blks):
                            g.mm(PA[:, b * 128:(b + 1) * 128], Gm[:, b, :], Ustr, r=["Gm", "cst"], w=[("PAB", 0, 0)])
                            g.mm(PA[:, 512 + b * 128:512 + (b + 1) * 128], Ustr, Gm[:, b, :], r=["Gm", "cst"], w=[("PAB", 0, 1)])
                            g.mm(PB[:, b * 128:(b + 1) * 128], ones, Gm[:, b, :], r=["Gm", "cst"], w=[("PAB", 1, 0)])
                            g.mm(PC[:, b * 4:b * 4 + 1], Gm[:, b, :], ones[:, 0:1], r=["Gm", "cst"], w=[("PC", 0)])
                            g.mm(PC[:, b * 4 + 1:b * 4 + 2], Ustr, g_t[:, n, hd:hd + 1], r=["g_t", "cst"], w=[("PC", 0)])
                            g.mm(PC[:, b * 4 + 2:b * 4 + 3], ones, g_t[:, n, hd:hd + 1], r=["g_t", "cst"], w=[("PC", 0)])
                    st.append(s2)

                    def s3():
                        g.act(fl(DecL), PA[:, 0:512], AF.Exp, r=[("PAB", 0, 0)], w=["DecL"])
                        g.act(fl(DecT), PA[:, 512:1024], AF.Exp, r=[("PAB", 0, 1)], w=["DecT"])
                        g.act(fl(egcb), PB[:, 0:512], AF.Exp, r=[("PAB", 1, 0)], w=["egcb"])
                        g.act(ecols[:].rearrange("p a b -> p (a b)"), PC[:, 0:16], AF.Exp, r=[("PC", 0)], w=[ke])
                    st.append(s3)

                    def s4():
                        for b, n in enumerate(blks):
                            g.op("dve", "tensor_tensor", r=["DecL", "cst"], w=["DecL"], out=DecL[:, b, :], in0=DecL[:, b, :], in1=mSL, op=ALU.mult)
                            g.op("dve", "tensor_tensor", r=["DecT", "cst"], w=["DecT"], out=DecT[:, b, :], in0=DecT[:, b, :], in1=mUI, op=ALU.mult)
                            g.op("dve", "tensor_tensor", r=[ke, "beta"], w=["bcol2"], out=bcol2[:, b:b + 1], in0=ecols[:, b, 0:1],
                                 in1=beta_t[:, n, hd:hd + 1], op=ALU.mult)
                        for b, n in enumerate(blks):
                            bs = slice(n * 128, (n + 1) * 128)
                            g.mm(PC[:, b * 128:(b + 1) * 128], gkT[:, bs], gkT[:, bs], r=["gk"], w=[("PC", 0)])
                    st.append(s4)

                    def s5():
                        Q = wide["Qa"]
                        for b, n in enumerate(blks):
                            g.op("dve", "scalar_tensor_tensor", r=[("PC", 0), "nbeta", "DecL"], w=["Qa"], out=Q[:, b, :],
                                 in0=PC[:, b * 128:(b + 1) * 128], scalar=nbeta_t[:, n, hd:hd + 1], in1=DecL[:, b, :],
                                 op0=ALU.mult, op1=ALU.mult)
                        for b in range(4):
                            g.mm(PC[:, 512 + b * 128:512 + (b + 1) * 128], Q[:, b, :], ident, r=["Qa", "cst"], w=[("PC", 1)])
                    st.append(s5)

                    def s6():
                        P = wide["Pa"]
                        g.act(fl(P), PC[:, 512:1024], AF.Copy, r=[("PC", 1)], w=["Pa"])
                        for b in range(4):
                            g.op("dve", "tensor_tensor", r=["Pa", "cst"], w=["X"], out=X[:, b, :], in0=P[:, b, :], in1=ident, op=ALU.add)
                    st.append(s6)
                    names = [("Qa", "Pa"), ("Qb", "Pb")]
                    for step in range(6):
                        qn, pn = names[step % 2]
                        qn2, pn2 = names[(step + 1) % 2]

                        def sa(qn=qn, pn=pn, qn2=qn2, pn2=pn2, step=step):
                            Q, P, Q2, P2 = wide[qn], wide[pn], wide[qn2], wide[pn2]
                            for b in range(4):
                                g.mm(PC[:, b * 128:(b + 1) * 128], P[:, b, :], Q[:, b, :], r=[qn, pn], w=[("PC", 0)])
                            if step < 5:
                                for b in range(4):
                                    g.mm(PC[:, 512 + b * 128:512 + (b + 1) * 128], Q[:, b, :], P[:, b, :], r=[qn, pn], w=[("PC", 1)])
                            g.act(fl(Q2), PC[:, 0:512], AF.Copy, r=[("PC", 0)], w=[qn2])
                            if step < 5:
                                g.op("dve", "tensor_copy", r=[("PC", 1)], w=[pn2], out=fl(P2), in_=PC[:, 512:1024])
                        st.append(sa)

                        def sb_(qn2=qn2):
                            Q2 = wide[qn2]
                            for b in range(4):
                                g.mm(PB[:, 512 + b * 128:512 + (b + 1) * 128], Q2[:, b, :], X[:, b, :], r=[qn2, "X"], w=[("PAB", 1, 1)])
                            g.op("dve", "tensor_tensor", r=[("PAB", 1, 1), "X"], w=["X"], out=fl(X), in0=fl(X), in1=PB[:, 512:1024], op=ALU.add)
                        st.append(sb_)

                    def s7():
                        for b, n in enumerate(blks):
                            bs = slice(n * 128, (n + 1) * 128)
                            g.mm(PA[:, b * 128:(b + 1) * 128], gkT[:, bs], ident, r=["gk", "cst"], w=[("PAB", 0, 0)])
                            g.mm(PA[:, 512 + b * 128:512 + (b + 1) * 128], gvT[:, bs], ident, r=["gv", "cst"], w=[("PAB", 0, 1)])
                            g.mm(PB[:, b * 128:(b + 1) * 128], gkT[:, bs], gqT[:, bs], r=["gk", "gq"], w=[("PAB", 1, 0)])
                    st.append(s7)

                    def s8():
                        for b, n in enumerate(blks):
                            g.op("dve", "tensor_scalar", r=[("PAB", 0, 0), "bcol2"], w=["kbg"], out=kbg[:, b, :], in0=PA[:, b * 128:(b + 1) * 128],
                                 scalar1=bcol2[:, b:b + 1], scalar2=None, op0=ALU.mult)
                            g.op("dve", "tensor_scalar", r=[("PAB", 0, 0), ke], w=[kk], out=ktail[:, b, :], in0=PA[:, b * 128:(b + 1) * 128],
                                 scalar1=ecols[:, b, 1:2], scalar2=None, op0=ALU.mult)
                        for b, n in enumerate(blks):
                            g.act(vb[:, b, :], PA[:, 512 + b * 128:512 + (b + 1) * 128], AF.Identity, r=[("PAB", 0, 1), "beta"], w=["vb"],
                                  scale=beta_t[:, n, hd:hd + 1])
                        g.op("dve", "tensor_tensor", r=[("PAB", 1, 0), "DecT"], w=[ka], out=fl(ATm), in0=fl(DecT), in1=PB[:, 0:512], op=ALU.mult)
                        g.op("dve", "tensor_tensor", r=["gq", "egcb"], w=[kq], out=fl(qdT), in0=gqT[:, grp * 512:(grp + 1) * 512], in1=fl(egcb), op=ALU.mult)
                    st.append(s8)

                    def s9():
                        for b, n in enumerate(blks):
                            g.mm(PA[:, b * 128:(b + 1) * 128], X[:, b, :], vb[:, b, :], r=["X", "vb"], w=[("PAB", 0, 0)])
                            g.mm(PA[:, 512 + b * 128:512 + (b + 1) * 128], kbg[:, b, :], X[:, b, :], r=["X", "kbg"], w=[("PAB", 0, 1)])
                        g.act(fl(u_), PA[:, 0:512], AF.Copy, r=[("PAB", 0, 0)], w=[ku])
                        g.op("dve", "tensor_copy", r=[("PAB", 0, 1)], w=[kw], out=fl(wT_), in_=PA[:, 512:1024])
                    st.append(s9)
                    return st

                def scan_stages(grp):
                    par = grp % 2
                    blks = [grp * 4 + b for b in range(4)]
                    si = scan_in[par]
                    u_, wT_, ATm, qdT, ktail, ecols = si["u"], si["wT"], si["ATm"], si["qdT"], si["ktail"], si["ecols"]
                    ku, kw, ka, kq, kk, ke = ("u%d" % par, "wT%d" % par, "ATm%d" % par, "qdT%d" % par, "ktail%d" % par, "ecols%d" % par)
                    st = []
                    for b, n in enumerate(blks):
                        s0 = n % 2; s1_ = 1 - s0
                        Sc = Sst[:, s0, :]; Sn = Sst[:, s1_, :]

                        def c1(b=b, s0=s0, Sc=Sc):
                            g.mm(PD[:, 0:128], wT_[:, b, :], Sc, r=[kw, ("Sst", s0)], w=["PD"])
                            g.op("dve", "tensor_tensor", r=[ku, "PD"], w=["vnew"], out=vnew[:], in0=u_[:, b, :], in1=PD[:, 0:128], op=ALU.subtract)
                        st.append(c1)

                        def c2(b=b, n=n, s0=s0, s1_=s1_, Sc=Sc, Sn=Sn):
                            g.mm(PTf[:, 0:128], Sc, qdT[:, b, :], start=True, stop=False, r=[("Sst", s0), kq], w=["PT"])
                            g.mm(PTf[:, 0:128], vnew[:], ATm[:, b, :], start=False, stop=True, r=["vnew", ka], w=["PT"])
                            g.mm(PD[:, 128:256], ktail[:, b, :], vnew[:], r=[kk, "vnew"], w=["PD"])
                            g.op("dve", "scalar_tensor_tensor", r=[("Sst", s0), ke, "PD"], w=[("Sst", s1_)], out=Sn, in0=Sc,
                                 scalar=ecols[:, b, 2:3], in1=PD[:, 128:256], op0=ALU.mult, op1=ALU.add)
                            g.act(ogT[:, n * 128:(n + 1) * 128], PTf[:, 0:128], AF.Copy, r=["PT"], w=["ogT"])
                        st.append(c2)
                    return st

                prev = []
                for grp in range(5):
                    cur = prep_stages(grp) if grp < 4 else []
                    na, nb_ = len(cur), len(prev)
                    ia = ib = 0
                    while ia < na or ib < nb_:
                        if ia < na and (ib >= nb_ or ia * nb_ <= ib * na):
                            cur[ia](); ia += 1
                        else:
                            prev[ib](); ib += 1
                    prev = scan_stages(grp) if grp < 4 else []
                chk(7)
                for tt in range(4):
                    tsl = slice(tt * 512, (tt + 1) * 512)
                    g.act(sq[:, tt % 2, :], ogT[:, tsl], AF.Square, r=["ogT"], w=[("sq", tt % 2)])
                    g.mm(PD[:], ones, sq[:, tt % 2, :], r=["cst", ("sq", tt % 2)], w=["PD"])
                    g.act(rstd[:], PD[:], AF.Ln, r=["PD"], w=["rstd"], bias=EPS, scale=1.0 / 128)
                    g.act(rstd[:], rstd[:], AF.Exp, r=["rstd"], w=["rstd"], scale=-0.5)
                    g.op("dve", "scalar_tensor_tensor", r=["ogT", "gsm", "rstd"], w=["ogT"], out=ogT[:, tsl], in0=ogT[:, tsl],
                         scalar=gsm_t[:, 16:17], in1=rstd[:], op0=ALU.mult, op1=ALU.mult)
                    ob_ = obuf[:, tt % 2, :]
                    g.op("dve", "tensor_tensor", r=["ogT", "gz"], w=[("obuf", tt % 2)], out=ob_, in0=ogT[:, tsl], in1=gzT[:, tsl], op=ALU.mult)
                    g.dma("sp", mixT[8 + hd, :, tsl], ob_, "mo", r=[("obuf", tt % 2)], w=[("mixT", 8 + hd)])

            for hd in range(NGD):
                try:
                    gdn_head(hd)
                except _Stop:
                    break

            def resid_block(nblk_k, src, skey, T0, T, xtmp, xkey, wkc_list):
                pass

            xr = g.at("xr%d" % l, [128, 2, 1024], F32, ar)
            for kc in range(KC if DO_OUT else 0):
                g.dma("sp", hT[:, kc, :], mixT[kc], "mi", r=[("mixT", kc)], w=["hT"])

            def lin_resid(src, skey, ncin, T0, T, getw):
                for ob in range(16):
                    wl = getw(ob)
                    for th in range(T // 1024):
                        t0 = T0 + th * 1024
                        xs = xr[:, (ob + th) % 2, :]; xk = ("xr", (ob + th) % 2)
                        g.dma("sp", xs, xw[ob, :, t0:t0 + 1024], "xr", r=[("xw", ob)], w=[xk])
                        pa = (PA, PB)[(ob + th) % 2]; pk0 = ("PAB", (ob + th) % 2, 0); pk1 = ("PAB", (ob + th) % 2, 1)
                        for hf in range(2):
                            for i, (wt, wk, ki, ci) in enumerate(wl):
                                g.mm(pa[:, hf * 512:(hf + 1) * 512], wt[:, ki, :], src[:, ci, th * 1024 + hf * 512: th * 1024 + (hf + 1) * 512],
                                     start=(i == 0), stop=(i == len(wl) - 1), r=[wk, skey], w=[(pk0, pk1)[hf]])
                        g.op("dve", "tensor_tensor", r=[pk0, pk1, xk], w=[xk], out=xs, in0=xs, in1=pa[:], op=ALU.add)
                        g.dma("sp", xw[ob, :, t0:t0 + 1024], xs, "xr", r=[xk], w=[("xw", ob)])

            def getw_out(ob):
                wt, wk = ws.get()
                return [(wt, wk, kc, kc) for kc in range(KC)]
            if DO_OUT:
                lin_resid(hT, "hT", 16, 0, S, getw_out)
            if not DO_X:
                continue

            a = ar + 8192
            memn = g.at("memn%d" % l, [128, KC, NMEM], BF16, a); a += 8192
            KT = g.at("KT%d" % l, [128, 4, NMEM], BF16, a); a += 2048
            Vt = g.at("Vt%d" % l, [128, 2, 512], BF16, a); a += 2048
            xq = g.at("xq%d" % l, [128, 4, S], BF16, a); a += 16384
            xo = g.at("xo%d" % l, [128, 4, S], BF16, a); a += 16384
            pb_t = g.at("pb_t%d" % l, [128, NMEM], BF16, a); a += 512
            pT_t = g.at("pT_t%d" % l, [128, 2, 128], BF16, a); a += 512
            mx = g.at("mx%d" % l, [128, 4], F32, a); a += 64
            mx2 = g.at("mx2_%d" % l, [128, 2, 4], F32, a); a += 64
            sc2 = g.at("sc2_%d" % l, [128, 2, NMEM], F32, a); a += 2048
            assert a <= big_off + 122880
            for kc in range(KC):
                g.dma("sp", xt[:, kc, 0:NMEM], memT[kc], "x", w=[("xt", kc)] + XT_ALIAS)
                g.act(sqb[:, kc % 2, 0:NMEM], xt[:, kc, 0:NMEM], AF.Square, r=[("xt", kc)], w=[("sqb", kc % 2)])
                g.mm(PD[:, 0:NMEM], onesb[:], sqb[:, kc % 2, 0:NMEM], start=(kc == 0), stop=(kc == KC - 1), r=["onesb", ("sqb", kc % 2)], w=["PD"])
            g.act(rstd[:, 0:NMEM], PD[:, 0:NMEM], AF.Ln, r=["PD"], w=["rstd"], bias=EPS, scale=1.0 / D)
            g.act(rstd[:, 0:NMEM], rstd[:, 0:NMEM], AF.Exp, r=["rstd"], w=["rstd"], scale=-0.5)
            for kc in range(KC):
                g.op("dve", "scalar_tensor_tensor", r=[("xt", kc), "nrm", "rstd"], w=["memn"], out=memn[:, kc, :], in0=xt[:, kc, 0:NMEM],
                     scalar=gcol(l, 2, kc), in1=rstd[:, 0:NMEM], op0=ALU.mult, op1=ALU.mult)
            for j in range(4):
                wt, wk = ws.get()
                for kc in range(KC):
                    g.mm(PC[:, 0:NMEM], wt[:, kc, :], memn[:, kc, :], start=(kc == 0), stop=(kc == KC - 1), r=[wk, "memn"], w=[("PC", 0)])
                g.act(KT[:, j, :], PC[:, 0:NMEM], AF.Copy, r=[("PC", 0)], w=["KT"])
            for j in range(4):
                wt, wk = ws.get()
                for mb in range(2):
                    for kc in range(KC):
                        g.mm(PC[:, 512 + mb * 128:512 + (mb + 1) * 128], memn[:, kc, mb * 128:(mb + 1) * 128], wt[:, kc, :],
                             start=(kc == 0), stop=(kc == KC - 1), r=[wk, "memn"], w=[("PC", 1)])
                for mb in range(2):
                    g.act(Vt[:, mb, j * 128:(j + 1) * 128], PC[:, 512 + mb * 128:512 + (mb + 1) * 128], AF.Copy, r=[("PC", 1)], w=["Vt"])
            dump("memn", memn[:].rearrange("p a b -> p (a b)"), ["memn"], [128, KC * NMEM], BF16)
            dump("KT", KT[:].rearrange("p a b -> p (a b)"), ["KT"], [128, 4 * NMEM], BF16)
            dump("Vt", Vt[:].rearrange("p a b -> p (a b)"), ["Vt"], [128, 1024], BF16)
            rmsnorm(xw, "xw", lambda kc: gcol(l, 1, kc), hT, 0, S, "hT")
            for j in range(4):
                def ev(tt, pv, pk, j=j):
                    if tt % 2 == 0:
                        g.act(xq[:, j, tt * 512:(tt + 1) * 512], pv, AF.Copy, r=[pk], w=["xq"])
                    else:
                        g.op("dve", "tensor_copy", r=[pk], w=["xq"], out=xq[:, j, tt * 512:(tt + 1) * 512], in_=pv)
                proj(hT, "hT", S, ev)
            def xA(n):
                tb, j = divmod(n, 4)
                i2 = n % 2
                tbs = slice(tb * 128, (tb + 1) * 128)
                pz = PC[:, i2 * 512:i2 * 512 + NMEM]; pzk = ("PC", i2)
                g.mm(pz, xq[:, j, tbs], KT[:, j, :], r=["xq", "KT"], w=[pzk])
                g.op("dve", "reduce_max", r=[pzk], w=[("mxa", i2)], out=mx2[:, i2, 0:1], in_=pz, axis=mybir.AxisListType.X)
                g.op("dve", "tensor_scalar_mul", r=[("mxa", i2)], w=[("mxa", i2)], out=mx2[:, i2, 1:2], in0=mx2[:, i2, 0:1], scalar1=-scale)
                g.op("dve", "memset", w=[("mxs", i2)], ap=mx2[:, i2, 2:3], constant=0.0)
                g.act(sc2[:, i2, :], pz, AF.Exp, r=[pzk, ("mxa", i2), ("mxs", i2)], w=[("sc2", i2), ("mxs", i2)], bias=mx2[:, i2, 1:2], scale=scale,
                      accum_out=mx2[:, i2, 2:3])

            def xB(n):
                tb, j = divmod(n, 4)
                i2 = n % 2
                tbs = slice(tb * 128, (tb + 1) * 128)
                g.op("dve", "reciprocal", r=[("mxs", i2), ("sc2", i2)], w=[("mxr", i2)], out=mx2[:, i2, 3:4], in_=mx2[:, i2, 2:3])
                g.op("dve", "tensor_scalar", r=[("sc2", i2), ("mxr", i2)], w=["pb_t"], out=pb_t[:], in0=sc2[:, i2, :], scalar1=mx2[:, i2, 3:4], scalar2=None, op0=ALU.mult)
                for mb in range(2):
                    g.tr(PT[:, mb * 128:(mb + 1) * 128], pb_t[:, mb * 128:(mb + 1) * 128], identb[:], r=["pb_t", "identb"], w=["PT"])
                g.act(pT_t[:].rearrange("p a b -> p (a b)"), PT[:, 0:256], AF.Copy, r=["PT"], w=["pT_t"])
                for mb in range(2):
                    g.mm(PD[:, j * 128:(j + 1) * 128], Vt[:, mb, j * 128:(j + 1) * 128], pT_t[:, mb, :], start=(mb == 0), stop=(mb == 1),
                         r=["Vt", "pT_t"], w=["PD"])
                if j == 3:
                    g.op("dve", "tensor_copy", r=["PD"], w=["xo"], out=xo[:, :, tbs], in_=PD[:].rearrange("p (a b) -> p a b", a=4))

            xA(0)
            for n in range(64):
                if n + 1 < 64:
                    xA(n + 1)
                xB(n)

            dump("xq", xq[:].rearrange("p a b -> p (a b)"), ["xq"], [128, 4 * S], BF16)
            dump("xo", xo[:].rearrange("p a b -> p (a b)"), ["xo"], [128, 4 * S], BF16)

            def getw_xo(ob):
                if ob % 4 == 0:
                    getw_xo.cur = ws.get()
                wt, wk = getw_xo.cur
                return [(wt, wk, (ob % 4) * 4 + kc, kc) for kc in range(4)]
            lin_resid(xo, "xo", 4, 0, S, getw_xo)
            if not DO_F:
                continue

            hF = g.at("hF%d" % l, [128, KC, 1024], BF16, big_off)
            actT = g.at("actT%d" % l, [128, 44, 1024], BF16, big_off + 32768)
            cg = g.at("cg%d" % l, [128, 2, 1024], F32, xt_off)
            cu = g.at("cu%d" % l, [128, 2, 1024], F32, xt_off + 8192)
            hc = g.at("hc%d" % l, [128, 2, 4], F32, xt_off + 16384)
            xr2 = g.at("xr2%d" % l, [128, 2, 1024], F32, xt_off + 16384 + 64)
            for tt in range(2):
                T0 = tt * 1024
                rmsnorm(xw, "xw", lambda kc: gcol(l, 3, kc), hF, T0, T0 + 1024, "hF",
                        extra_w=[("cg", 0), ("cg", 1), ("cu", 0), ("cu", 1), ("xr2", 0), ("xr2", 1)])
                for fb in range(44):
                    for gi, (cbuf, ckey, blk) in enumerate(((cg, "cg", fb), (cu, "cu", 44 + fb))):
                        wt, wk = ws.get()
                        pa = (PA, PB)[gi]; pk = ("PAB", gi, 0); pkb = ("PAB", gi, 1)
                        for hf in range(2):
                            for kc in range(KC):
                                g.mm(pa[:, hf * 512:(hf + 1) * 512], wt[:, kc, :], hF[:, kc, hf * 512:(hf + 1) * 512],
                                     start=(kc == 0), stop=(kc == KC - 1), r=[wk, "hF"], w=[(pk, pkb)[hf]])
                        c_ = cbuf[:, fb % 2, :]; ck = (ckey, fb % 2)
                        w0 = fconv_t[:, blk * 4:blk * 4 + 1]; w1 = fconv_t[:, blk * 4 + 1:blk * 4 + 2]
                        w2 = fconv_t[:, blk * 4 + 2:blk * 4 + 3]; bb = fconv_t[:, blk * 4 + 3:blk * 4 + 4]
                        g.act(c_, pa[:], AF.Identity, r=[pk, pkb, "fconv"], w=[ck], scale=w2, bias=bb)
                        g.op("dve", "scalar_tensor_tensor", r=[pk, pkb, "fconv", ck], w=[ck], out=c_[:, 1:1024], in0=pa[:, 0:1023], scalar=w1,
                             in1=c_[:, 1:1024], op0=ALU.mult, op1=ALU.add)
                        g.op("dve", "scalar_tensor_tensor", r=[pk, pkb, "fconv", ck], w=[ck], out=c_[:, 2:1024], in0=pa[:, 0:1022], scalar=w0,
                             in1=c_[:, 2:1024], op0=ALU.mult, op1=ALU.add)
                        if tt == 1:
                            hh = halo[:, blk, :]
                            g.op("dve", "scalar_tensor_tensor", r=["halo", "fconv", ck], w=[ck], out=c_[:, 0:2], in0=hh, scalar=w0,
                                 in1=c_[:, 0:2], op0=ALU.mult, op1=ALU.add)
                            g.op("dve", "scalar_tensor_tensor", r=["halo", "fconv", ck], w=[ck], out=c_[:, 0:1], in0=hh[:, 1:2], scalar=w1,
                                 in1=c_[:, 0:1], op0=ALU.mult, op1=ALU.add)
                        else:
                            g.act(halo[:, blk, :], pa[:, 1022:1024], AF.Copy, r=[pkb], w=["halo"])
                    g.act(cg[:, fb % 2, :], cg[:, fb % 2, :], AF.Silu, r=[("cg", fb % 2)], w=[("cg", fb % 2)])
                    g.op("dve", "tensor_tensor", r=[("cg", fb % 2), ("cu", fb % 2)], w=[("actT", fb)], out=actT[:, fb, :],
                         in0=cg[:, fb % 2, :], in1=cu[:, fb % 2, :], op=ALU.mult)
                if tt == 0:
                    dump("hF", hF[:].rearrange("p a b -> p (a b)"), ["hF"], [128, KC * 1024], BF16)
                    dump("actT", actT[:].rearrange("p a b -> p (a b)"), ["actT"], [128, 44 * 1024], BF16)
                for ob in range(16):
                    wl = []
                    for kg in range(4):
                        wt, wk = ws.get(hold=4)
                        wl += [(wt, wk, i, kg * 11 + i) for i in range(11)]
                    xs = xr2[:, ob % 2, :]; xk = ("xr2", ob % 2)
                    g.dma("sp", xs, xw[ob, :, T0:T0 + 1024], "xr", r=[("xw", ob)], w=[xk] + [("xt", kc_) for kc_ in range(8, 13)])
                    pa = (PA, PB)[ob % 2]; pk = ("PAB", ob % 2, 0); pkb = ("PAB", ob % 2, 1)
                    for hf in range(2):
                        for i, (wt, wk, ki, ci) in enumerate(wl):
                            g.mm(pa[:, hf * 512:(hf + 1) * 512], wt[:, ki, :], actT[:, ci, hf * 512:(hf + 1) * 512],
                                 start=(i == 0), stop=(i == 43), r=[wk, ("actT", ci)], w=[(pk, pkb)[hf]])
                    g.op("dve", "tensor_tensor", r=[pk, pkb, xk], w=[xk], out=xs, in0=xs, in1=pa[:], op=ALU.add)
                    g.dma("sp", xw[ob, :, T0:T0 + 1024], xs, "xr", r=[xk], w=[("xw", ob)])

        if final:
            fT = g.at("fT", [128, KC, 512], F32, big_off)
            for t0 in range(0, S, 512):
                for kc in range(KC):
                    g.dma("sp", xt[:, kc, :], xw[kc, :, t0:t0 + 512], "x", r=[("xw", kc)], w=[("xt", kc)] + XT_ALIAS)
                    g.act(sqb[:, kc % 2, :], xt[:, kc, :], AF.Square, r=[("xt", kc)], w=[("sqb", kc % 2)])
                    g.mm(PD[:], onesb[:], sqb[:, kc % 2, :], start=(kc == 0), stop=(kc == KC - 1), r=["onesb", ("sqb", kc % 2)], w=["PD"])
                g.act(rstd[:], PD[:], AF.Ln, r=["PD"], w=["rstd"], bias=EPS, scale=1.0 / D)
                g.act(rstd[:], rstd[:], AF.Exp, r=["rstd"], w=["rstd"], scale=-0.5)
                for kc in range(KC):
                    g.op("dve", "scalar_tensor_tensor", r=[("xt", kc), "nrm", "rstd"], w=[("fT", kc)], out=fT[:, kc, :], in0=xt[:, kc, :],
                         scalar=nrm_t[:, nl * 64 + kc:nl * 64 + kc + 1], in1=rstd[:], op0=ALU.mult, op1=ALU.mult)
                    g.dma("sp", yout[kc, :, t0:t0 + 512], fT[:, kc, :], "yo", r=[("fT", kc)], w=[("yout", kc)])
            sc.rec("sp", lambda e: None, reads=["yout"])
        else:
            sc.rec("sp", lambda e: None, reads=["xw", "mixT"] + dbg_keys)
        sc.emit(es)
    return nc


def _consts():
    i = np.arange(128)
    ident = np.eye(128, dtype=np.float32)
    Lincl = (i[:, None] >= i[None, :]).astype(np.float32)
    Uincl = (i[:, None] <= i[None, :]).astype(np.float32)
    Ustr = (i[:, None] > i[None, :]).astype(np.float32)
    mSL = (i[:, None] > i[None, :]).astype(np.float32)
    mUI = (i[:, None] <= i[None, :]).astype(np.float32)
    ones = np.ones((128, 128), np.float32)
    t = np.arange(512)
    md = [((128 * d + i[:, None]) < t[None, :]).astype(np.float32) for d in range(4)]
    return np.ascontiguousarray(np.concatenate([ident, Lincl, Uincl, Ustr, mSL, mUI, ones] + md, axis=1))


def _blk(w, nk):
    K, C = w.shape
    return np.ascontiguousarray(w.reshape(K // 128, 128, C // 128, 128).transpose(2, 1, 0, 3))


def _col(v):
    return np.ascontiguousarray(v.reshape(-1, 128).T)


def prep_layers(inp, ls):
    f = lambda k: np.asarray(inp[k], dtype=np.float32)
    nl = len(ls)
    out = {}
    nrm = np.zeros((128, nl * 64 + 16), np.float32)
    for i, l in enumerate(ls):
        for j, k in enumerate(("mix_norm", "xattn_norm", "mem_norm", "ffn_norm")):
            nrm[:, i * 64 + j * 16:i * 64 + (j + 1) * 16] = _col(f(k)[l])
    nrm[:, nl * 64:] = _col(f("final_norm"))
    out["nrm"] = nrm
    w_in = f("w_in")
    out["w_in"] = np.stack([_blk(w_in[l][:, :7168], 16) for l in ls])
    out["w_ba"] = np.stack([np.ascontiguousarray(w_in[l][:, 7168:7184].reshape(16, 128, 16).transpose(1, 0, 2)) for l in ls])
    gc = f("gdn_conv")
    out["gconv"] = np.stack([np.ascontiguousarray(gc[l].reshape(4, 24, 128).transpose(2, 1, 0).reshape(128, 96)) for l in ls])
    gsm = np.zeros((nl, 128, 17), np.float32)
    for i, l in enumerate(ls):
        gsm[i, :, 0:8] = f("gdn_a_log")[l][None, :]
        gsm[i, :, 8:16] = f("gdn_dt_bias")[l][None, :]
        gsm[i, :, 16] = f("gdn_norm")[l]
    out["gsm"] = gsm
    out["w_out"] = np.stack([_blk(f("w_out")[l], 16) for l in ls])
    out["w_xq"] = np.stack([_blk(f("w_xq")[l], 16) for l in ls])
    out["w_xkv"] = np.stack([_blk(f("w_xkv")[l], 16) for l in ls])
    wxo = f("w_xo")
    out["w_xo"] = np.stack([np.ascontiguousarray(
        wxo[l].reshape(4, 128, 4, 4, 128).transpose(2, 1, 3, 0, 4).reshape(4, 128, 16, 128)) for l in ls])
    out["w_up"] = np.stack([_blk(f("w_up")[l], 16) for l in ls])
    fc = f("ffn_conv"); fb_ = f("ffn_conv_bias")
    fconv = np.zeros((nl, 128, NFB, 4), np.float32)
    for i, l in enumerate(ls):
        fconv[i, :, :, 0:3] = fc[l].reshape(3, NFB, 128).transpose(2, 1, 0)
        fconv[i, :, :, 3] = fb_[l].reshape(NFB, 128).T
    out["fconv"] = fconv.reshape(nl, 128, NFB * 4)
    wd = f("w_down")
    out["w_dn"] = np.stack([np.ascontiguousarray(
        wd[l].reshape(4, 11, 128, 16, 128).transpose(3, 0, 2, 1, 4)) for l in ls])
    out["cst"] = _consts()
    return out


_NC_CACHE = {}


def _get_nc(nl, final, dbg=False):
    key = (nl, final, dbg)
    if key not in _NC_CACHE:
        _NC_CACHE[key] = build(nl, final, dbg)
    return _NC_CACHE[key]


NCORES = 8
FUSED = True


def kernel(**inputs):
    x = np.asarray(inputs["x"], dtype=np.float32)
    mem = np.asarray(inputs["mem"], dtype=np.float32)
    B = x.shape[0]
    xT = [np.ascontiguousarray(x[b].T.reshape(KC, 128, S)) for b in range(B)]
    mT = [np.ascontiguousarray(mem[b].T.reshape(KC, 128, NMEM)) for b in range(B)]
    groups = [[0, 1, 2, 3]] if FUSED else [[0], [1], [2], [3]]
    work = {0: 0, 1: 1, 4: 2, 5: 3}
    cur = xT
    for gi, ls in enumerate(groups):
        final = gi == len(groups) - 1
        wts = prep_layers(inputs, ls)
        zeros = {k: np.zeros_like(v) for k, v in wts.items()}
        zx = np.zeros_like(cur[0]); zm = np.zeros_like(mT[0])
        nc = _get_nc(len(ls), final)
        in_maps = []
        for c in range(NCORES):
            if c in work:
                m = dict(wts)
                m["xin"] = cur[work[c]]
                m["memT"] = mT[work[c]]
            else:
                m = dict(zeros)
                m["xin"] = zx
                m["memT"] = zm
            in_maps.append(m)
        res = run_bass_kernel_spmd(nc, in_maps, core_ids=list(range(NCORES)))
        key = "yout" if final else "xw"
        inv = {b: c for c, b in work.items()}
        cur = [np.asarray(res.results[inv[b]][key]) for b in range(B)]
    out = np.stack([np.ascontiguousarray(cur[b].reshape(D, S).T) for b in range(B)])
    return out.astype(np.float32)
```
